# Optimizing a Trainium2 kernel written in Bass

```python
import math
import jax
import jax.numpy as jnp
from jax import lax
import numpy as np

D_MODEL = 1024
BATCH = 16
SEQ = 4096
DEPTH = 4
DEC_BATCH = 4
DEC_SEQ = 8192
PAST_LEN = 128

HEAD_DIM = 64
D_FF = -(-(8 * D_MODEL) // (3 * 256)) * 256
N_EVEN = (DEPTH + 1) // 2
N_ODD = DEPTH // 2
D_SSD = D_MODEL
SSD_HEADS = D_SSD // HEAD_DIM
SSD_GROUPS = 2
SSD_HG = SSD_HEADS // SSD_GROUPS
SSD_STATE = 128
SSD_CONV = 5
SSD_CHUNK = 128
SSD_CONV_CH = D_SSD + 2 * SSD_GROUPS * SSD_STATE
D_RWKV = D_MODEL
RWKV_HEADS = D_RWKV // HEAD_DIM
DECAY_RANK = 64
ICL_RANK = 64
GATE_RANK = 128
RWKV_PROJ = 3 * D_RWKV + DECAY_RANK + ICL_RANK + GATE_RANK
RWKV_LN_EPS = 64e-5
D_CONV = D_MODEL
CONV_WIDTH = 31
ATT_Q_HEADS = 16
ATT_KV_HEADS = 4
ATT_GQ = ATT_Q_HEADS // ATT_KV_HEADS
WINDOW = 128
ATT_BLOCK = 128
REL_BUCKETS = 32
REL_MAX_DIST = 128
EV_IN = D_SSD + SSD_CONV_CH + 2 * SSD_HEADS + RWKV_PROJ
EV_OUT = D_SSD + D_RWKV
OD_IN = 2 * D_CONV + (ATT_Q_HEADS + 2 * ATT_KV_HEADS) * HEAD_DIM
OD_OUT = D_CONV + ATT_Q_HEADS * HEAD_DIM

kernel_name = 'hybrid_ssd_rwkv_conformer_swa_encoder'


def _rms_norm(x, w, eps=1e-6):
    xf = x.astype(jnp.float32)
    xf = xf * lax.rsqrt(jnp.mean(xf * xf, axis=-1, keepdims=True) + eps)
    return (xf * w.astype(jnp.float32)).astype(x.dtype)


def _group_rms_norm(x, w, groups, eps=1e-6):
    shp = x.shape
    xf = x.astype(jnp.float32).reshape(shp[:-1] + (groups, shp[-1] // groups))
    xf = xf * lax.rsqrt(jnp.mean(xf * xf, axis=-1, keepdims=True) + eps)
    return xf.reshape(shp) * w.astype(jnp.float32)


def _group_layer_norm(x, w, b, groups, eps):
    shp = x.shape
    xf = x.astype(jnp.float32).reshape(shp[:-1] + (groups, shp[-1] // groups))
    xc = xf - jnp.mean(xf, axis=-1, keepdims=True)
    xn = xc * lax.rsqrt(jnp.mean(xc * xc, axis=-1, keepdims=True) + eps)
    return xn.reshape(shp) * w.astype(jnp.float32) + b.astype(jnp.float32)


def _dw_conv(x, w, b):
    k = w.shape[0]
    y = lax.conv_general_dilated(x, w[:, None, :], window_strides=(1,), padding=[(k // 2, k // 2)],
                                 dimension_numbers=('NWC', 'WIO', 'NWC'), feature_group_count=x.shape[-1])
    return y + b


def _segsum(a):
    t = a.shape[-1]
    cs = jnp.cumsum(a, axis=-1)
    diff = cs[..., :, None] - cs[..., None, :]
    return jnp.where(jnp.tril(jnp.ones((t, t), dtype=bool)), diff, -jnp.inf)


def _ssd_chunked(x, a, b, c):
    bsz, t = x.shape[:2]
    nc = t // SSD_CHUNK
    x = x.reshape(bsz, nc, SSD_CHUNK, SSD_GROUPS, SSD_HG, HEAD_DIM)
    b = b.reshape(bsz, nc, SSD_CHUNK, SSD_GROUPS, SSD_STATE)
    c = c.reshape(bsz, nc, SSD_CHUNK, SSD_GROUPS, SSD_STATE)
    a = a.reshape(bsz, nc, SSD_CHUNK, SSD_GROUPS, SSD_HG).transpose(0, 3, 4, 1, 2)
    a_cs = jnp.cumsum(a, axis=-1)
    decay_in = jnp.exp(_segsum(a))
    cb = jnp.einsum('bclgn,bcsgn->bcgls', c, b)
    y_diag = jnp.einsum('bcgls,bghcls,bcsghp->bclghp', cb, decay_in, x)
    decay_states = jnp.exp(a_cs[..., -1:] - a_cs)
    states = jnp.einsum('bclgn,bghcl,bclghp->bcghpn', b, decay_states, x)
    states = jnp.concatenate([jnp.zeros_like(states[:, :1]), states], axis=1)
    chunk_tot = jnp.pad(a_cs[..., -1], ((0, 0), (0, 0), (0, 0), (1, 0)))
    decay_chunk = jnp.exp(_segsum(chunk_tot))
    states = jnp.einsum('bghzc,bcghpn->bzghpn', decay_chunk, states)[:, :-1]
    y_off = jnp.einsum('bclgn,bcghpn,bghcl->bclghp', c, states, jnp.exp(a_cs))
    return (y_diag + y_off).reshape(bsz, t, SSD_GROUPS, SSD_HG, HEAD_DIM)


def _ssd_mixer(z, xbc, dt_raw, conv_w, conv_b, dt_bias, a_log, d_skip, norm_w):
    bsz, t = z.shape[:2]
    xbc = jax.nn.silu(_dw_conv(xbc, conv_w, conv_b)).astype(jnp.float32)
    xs, bs, cs = jnp.split(xbc, [D_SSD, D_SSD + SSD_GROUPS * SSD_STATE], axis=-1)
    xs = xs.reshape(bsz, t, SSD_GROUPS, SSD_HG, HEAD_DIM)
    bs = bs.reshape(bsz, t, SSD_GROUPS, SSD_STATE)
    cs = cs.reshape(bsz, t, SSD_GROUPS, SSD_STATE)
    dt = jax.nn.softplus((dt_raw.reshape(bsz, t, 2, SSD_HEADS) + dt_bias).astype(jnp.float32))
    da = (dt * -jnp.exp(a_log.astype(jnp.float32))).reshape(bsz, t, 2, SSD_GROUPS, SSD_HG)
    dt = dt.reshape(bsz, t, 2, SSD_GROUPS, SSD_HG)
    y_fwd = _ssd_chunked(xs * dt[:, :, 0, :, :, None], da[:, :, 0], bs, cs)
    flip = lambda u: jnp.flip(u, axis=1)
    y_bwd = flip(_ssd_chunked(flip(xs * dt[:, :, 1, :, :, None]), flip(da[:, :, 1]), flip(bs), flip(cs)))
    y = y_fwd + y_bwd + d_skip.astype(jnp.float32).reshape(SSD_GROUPS, SSD_HG, 1) * xs
    y = y.reshape(bsz, t, D_SSD) * jax.nn.silu(z.astype(jnp.float32))
    return _group_rms_norm(y, norm_w, SSD_GROUPS).astype(z.dtype)


def _delta_scan(r, w, k, v, a, b, reverse):
    def step(s, inp):
        r_t, w_t, k_t, v_t, a_t, b_t = inp
        sa = jnp.einsum('bhij,bhj->bhi', s, a_t)
        s = s * w_t[:, :, None, :] + sa[..., None] * b_t[:, :, None, :] + v_t[..., None] * k_t[:, :, None, :]
        return s, jnp.einsum('bhij,bhj->bhi', s, r_t)
    s0 = jnp.zeros(r.shape[1:] + (r.shape[-1],), jnp.float32)
    _, y = lax.scan(step, s0, (r, w, k, v, a, b), reverse=reverse)
    return y


def _rwkv_mixer(p, mu, w0, w2, a0, a2, g2, k_k, k_a, r_k, ln_w, ln_b):
    bsz, t = p.shape[:2]
    pf = p.astype(jnp.float32)
    prev = jnp.pad(pf[:, :-1], ((0, 0), (1, 0), (0, 0)))
    nxt = jnp.pad(pf[:, 1:], ((0, 0), (0, 1), (0, 0)))
    pf = pf + mu.astype(jnp.float32) * (0.5 * (prev + nxt) - pf)
    r, k, v, cw, ca, cg = jnp.split(pf, [D_RWKV, 2 * D_RWKV, 3 * D_RWKV, 3 * D_RWKV + DECAY_RANK,
                                          3 * D_RWKV + DECAY_RANK + ICL_RANK], axis=-1)
    a = jax.nn.sigmoid(a0.astype(jnp.float32) + ca @ a2.astype(jnp.float32))
    g = jax.nn.sigmoid(cg) @ g2.astype(jnp.float32)
    wlog = -jax.nn.softplus(-(w0.astype(jnp.float32)[:, None, None, :]
                              + jnp.einsum('btr,drc->dbtc', jnp.tanh(cw), w2.astype(jnp.float32)))) - 0.5
    decay = jnp.exp(-jnp.exp(wlog))
    kk = (k * k_k.astype(jnp.float32)).reshape(bsz, t, RWKV_HEADS, HEAD_DIM)
    kk = kk / jnp.maximum(jnp.sqrt(jnp.sum(kk * kk, axis=-1, keepdims=True)), 1e-12)
    k = k * (1.0 + (a - 1.0) * k_a.astype(jnp.float32))
    heads = lambda u: u.reshape(bsz, t, RWKV_HEADS, HEAD_DIM).transpose(1, 0, 2, 3)
    rs, ks, vs, a_h = heads(r), heads(k), heads(v), heads(a)
    kk_t = kk.transpose(1, 0, 2, 3)
    ia, ib = -kk_t, kk_t * a_h
    y_fwd = _delta_scan(rs, heads(decay[0]), ks, vs, ia, ib, reverse=False)
    y_bwd = _delta_scan(rs, heads(decay[1]), ks, vs, ia, ib, reverse=True)
    y = (y_fwd + y_bwd).transpose(1, 0, 2, 3).reshape(bsz, t, D_RWKV)
    y = _group_layer_norm(y, ln_w, ln_b, RWKV_HEADS, RWKV_LN_EPS)
    bonus = jnp.sum((r * k).reshape(bsz, t, RWKV_HEADS, HEAD_DIM) * r_k.astype(jnp.float32), axis=-1, keepdims=True)
    y = (y + (bonus * v.reshape(bsz, t, RWKV_HEADS, HEAD_DIM)).reshape(bsz, t, D_RWKV)) * g
    return y.astype(p.dtype)


def _even_mixer(h, w_in, w_out, ssd_conv_w, ssd_conv_b, ssd_dt_bias, ssd_a_log, ssd_d, ssd_norm_w,
                rwkv_mu, rwkv_w0, rwkv_w2, rwkv_a0, rwkv_a2, rwkv_g2, rwkv_k_k, rwkv_k_a, rwkv_r_k,
                rwkv_ln_w, rwkv_ln_b):
    u = h @ w_in
    z, xbc, dt_raw, p_rwkv = jnp.split(u, [D_SSD, D_SSD + SSD_CONV_CH, D_SSD + SSD_CONV_CH + 2 * SSD_HEADS], axis=-1)
    y_a = _ssd_mixer(z, xbc, dt_raw, ssd_conv_w, ssd_conv_b, ssd_dt_bias, ssd_a_log, ssd_d, ssd_norm_w)
    y_b = _rwkv_mixer(p_rwkv, rwkv_mu, rwkv_w0, rwkv_w2, rwkv_a0, rwkv_a2, rwkv_g2, rwkv_k_k, rwkv_k_a,
                      rwkv_r_k, rwkv_ln_w, rwkv_ln_b)
    return jnp.concatenate([y_a, y_b], axis=-1) @ w_out


def _conv_module(c_in, dw_w, dw_b, ln_w, ln_b):
    val, gate = jnp.split(c_in, 2, axis=-1)
    u = val * jax.nn.sigmoid(gate)
    u = _dw_conv(u, dw_w, dw_b)
    u = _group_layer_norm(u, ln_w, ln_b, 1, 1e-5)
    return jax.nn.silu(u).astype(c_in.dtype)


def _t5_bucket(rel):
    nb = REL_BUCKETS // 2
    max_exact = nb // 2
    n = jnp.abs(rel)
    nf = jnp.maximum(n, 1).astype(jnp.float32)
    large = max_exact + (jnp.log(nf / max_exact) / math.log(REL_MAX_DIST / max_exact)
                         * (nb - max_exact)).astype(jnp.int32)
    large = jnp.minimum(large, nb - 1)
    return (rel > 0).astype(jnp.int32) * nb + jnp.where(n < max_exact, n, large)


def _window_attention(q, k, v, q_norm_w, k_norm_w, sink, rel_bias):
    bsz, t = q.shape[:2]
    nblk = t // ATT_BLOCK
    q = _rms_norm(q, q_norm_w)
    k = _rms_norm(k, k_norm_w)
    qb = q.reshape(bsz, nblk, ATT_BLOCK, ATT_KV_HEADS, ATT_GQ, HEAD_DIM).transpose(1, 0, 2, 3, 4, 5)
    pad = ((0, 0), (ATT_BLOCK, ATT_BLOCK), (0, 0), (0, 0))
    kp, vp = jnp.pad(k, pad), jnp.pad(v, pad)
    rel = jnp.arange(3 * ATT_BLOCK)[None, :] - ATT_BLOCK - jnp.arange(ATT_BLOCK)[:, None]
    bias = rel_bias.astype(jnp.float32)[_t5_bucket(rel)]
    bias = bias.transpose(2, 0, 1).reshape(ATT_KV_HEADS, ATT_GQ, ATT_BLOCK, 3 * ATT_BLOCK)
    inband = jnp.abs(rel) <= WINDOW
    sink_l = sink.astype(jnp.float32).reshape(ATT_KV_HEADS, ATT_GQ, 1, 1)
    scale = HEAD_DIM ** -0.5

    def block(args):
        q_blk, i = args
        start = i * ATT_BLOCK
        k_blk = lax.dynamic_slice_in_dim(kp, start, 3 * ATT_BLOCK, axis=1)
        v_blk = lax.dynamic_slice_in_dim(vp, start, 3 * ATT_BLOCK, axis=1)
        kpos = start - ATT_BLOCK + jnp.arange(3 * ATT_BLOCK)
        mask = inband & ((kpos >= 0) & (kpos < t))[None, :]
        logits = jnp.einsum('bqkgd,bskd->bkgqs', q_blk, k_blk).astype(jnp.float32) * scale + bias
        logits = jnp.where(mask, logits, -jnp.inf)
        m = jnp.maximum(jnp.max(logits, axis=-1, keepdims=True), sink_l)
        p = jnp.exp(logits - m)
        denom = jnp.sum(p, axis=-1, keepdims=True) + jnp.exp(sink_l - m)
        o = jnp.einsum('bkgqs,bskd->bkgqd', p.astype(v.dtype), v_blk).astype(jnp.float32) / denom
        return o.astype(v.dtype).transpose(0, 3, 1, 2, 4)

    out = lax.map(block, (qb, jnp.arange(nblk, dtype=jnp.int32)))
    return out.transpose(1, 0, 2, 3, 4, 5).reshape(bsz, t, ATT_Q_HEADS * HEAD_DIM)


def _odd_mixer(h, w_in, w_out, conv_dw_w, conv_dw_b, conv_ln_w, conv_ln_b, att_q_norm_w, att_k_norm_w,
               att_sink, rel_bias):
    bsz, t = h.shape[:2]
    u = h @ w_in
    c_in, q, k, v = jnp.split(u, [2 * D_CONV, 2 * D_CONV + ATT_Q_HEADS * HEAD_DIM,
                                  2 * D_CONV + (ATT_Q_HEADS + ATT_KV_HEADS) * HEAD_DIM], axis=-1)
    y_c = _conv_module(c_in, conv_dw_w, conv_dw_b, conv_ln_w, conv_ln_b)
    y_d = _window_attention(q.reshape(bsz, t, ATT_Q_HEADS, HEAD_DIM), k.reshape(bsz, t, ATT_KV_HEADS, HEAD_DIM),
                            v.reshape(bsz, t, ATT_KV_HEADS, HEAD_DIM), att_q_norm_w, att_k_norm_w, att_sink, rel_bias)
    return jnp.concatenate([y_c, y_d], axis=-1) @ w_out


def _swiglu(h, w_in, w_out):
    gate, up = jnp.split(h @ w_in, 2, axis=-1)
    return (jax.nn.silu(gate) * up) @ w_out


def _trunk(x, rel_bias, norm_mix_w, norm_ffn_w, ffn_w_in, ffn_w_out, ev_w_in, ev_w_out, ssd_conv_w, ssd_conv_b,
           ssd_dt_bias, ssd_a_log, ssd_d, ssd_norm_w, rwkv_mu, rwkv_w0, rwkv_w2, rwkv_a0, rwkv_a2, rwkv_g2,
           rwkv_k_k, rwkv_k_a, rwkv_r_k, rwkv_ln_w, rwkv_ln_b, od_w_in, od_w_out, conv_dw_w, conv_dw_b,
           conv_ln_w, conv_ln_b, att_q_norm_w, att_k_norm_w, att_sink):
    for layer in range(DEPTH):
        h = _rms_norm(x, norm_mix_w[layer])
        if layer % 2 == 0:
            e = layer // 2
            x = x + _even_mixer(h, ev_w_in[e], ev_w_out[e], ssd_conv_w[e], ssd_conv_b[e], ssd_dt_bias[e],
                                ssd_a_log[e], ssd_d[e], ssd_norm_w[e], rwkv_mu[e], rwkv_w0[e], rwkv_w2[e],
                                rwkv_a0[e], rwkv_a2[e], rwkv_g2[e], rwkv_k_k[e], rwkv_k_a[e], rwkv_r_k[e],
                                rwkv_ln_w[e], rwkv_ln_b[e])
        else:
            o = layer // 2
            x = x + _odd_mixer(h, od_w_in[o], od_w_out[o], conv_dw_w[o], conv_dw_b[o], conv_ln_w[o], conv_ln_b[o],
                               att_q_norm_w[o], att_k_norm_w[o], att_sink[o], rel_bias)
        x = x + _swiglu(_rms_norm(x, norm_ffn_w[layer]), ffn_w_in[layer], ffn_w_out[layer])
    return x


def setup_inputs(seed: int = 0) -> dict:
    key = jax.random.key(seed)
    ks = iter(jax.random.split(key, 48))
    nrm = lambda shape, scale: scale * jax.random.normal(next(ks), shape, jnp.float32)
    uni = lambda shape, lo, hi: jax.random.uniform(next(ks), shape, jnp.float32, lo, hi)
    E, O = N_EVEN, N_ODD
    dt0 = jnp.exp(uni((E, 2, SSD_HEADS), math.log(1e-3), math.log(1e-1)))
    return {
        'x_prompt': nrm((BATCH, SEQ, D_MODEL), 1.0),
        'x_sample': nrm((DEC_BATCH, DEC_SEQ, D_MODEL), 1.0),
        'rel_bias': nrm((REL_BUCKETS, ATT_Q_HEADS), 0.5),
        'norm_mix_w': 1.0 + nrm((DEPTH, D_MODEL), 0.02),
        'norm_ffn_w': 1.0 + nrm((DEPTH, D_MODEL), 0.02),
        'ffn_w_in': nrm((DEPTH, D_MODEL, 2 * D_FF), D_MODEL ** -0.5),
        'ffn_w_out': nrm((DEPTH, D_FF, D_MODEL), D_FF ** -0.5),
        'ev_w_in': nrm((E, D_MODEL, EV_IN), D_MODEL ** -0.5),
        'ev_w_out': nrm((E, EV_OUT, D_MODEL), EV_OUT ** -0.5),
        'ssd_conv_w': nrm((E, SSD_CONV, SSD_CONV_CH), SSD_CONV ** -0.5),
        'ssd_conv_b': nrm((E, SSD_CONV_CH), 0.02),
        'ssd_dt_bias': dt0 + jnp.log(-jnp.expm1(-dt0)),
        'ssd_a_log': jnp.log(uni((E, 2, SSD_HEADS), 1.0, 16.0)),
        'ssd_d': 1.0 + nrm((E, SSD_HEADS), 0.1),
        'ssd_norm_w': 1.0 + nrm((E, D_SSD), 0.02),
        'rwkv_mu': uni((E, RWKV_PROJ), 0.0, 1.0),
        'rwkv_w0': uni((E, 2, D_RWKV), -6.0, -1.0),
        'rwkv_w2': nrm((E, 2, DECAY_RANK, D_RWKV), 0.5 * DECAY_RANK ** -0.5),
        'rwkv_a0': nrm((E, D_RWKV), 0.1),
        'rwkv_a2': nrm((E, ICL_RANK, D_RWKV), ICL_RANK ** -0.5),
        'rwkv_g2': nrm((E, GATE_RANK, D_RWKV), GATE_RANK ** -0.5),
        'rwkv_k_k': 0.85 + nrm((E, D_RWKV), 0.05),
        'rwkv_k_a': 1.0 + nrm((E, D_RWKV), 0.05),
        'rwkv_r_k': nrm((E, RWKV_HEADS, HEAD_DIM), 0.1),
        'rwkv_ln_w': 1.0 + nrm((E, D_RWKV), 0.02),
        'rwkv_ln_b': nrm((E, D_RWKV), 0.02),
        'od_w_in': nrm((O, D_MODEL, OD_IN), D_MODEL ** -0.5),
        'od_w_out': nrm((O, OD_OUT, D_MODEL), OD_OUT ** -0.5),
        'conv_dw_w': nrm((O, CONV_WIDTH, D_CONV), CONV_WIDTH ** -0.5),
        'conv_dw_b': nrm((O, D_CONV), 0.02),
        'conv_ln_w': 1.0 + nrm((O, D_CONV), 0.02),
        'conv_ln_b': nrm((O, D_CONV), 0.02),
        'att_q_norm_w': 1.0 + nrm((O, HEAD_DIM), 0.02),
        'att_k_norm_w': 1.0 + nrm((O, HEAD_DIM), 0.02),
        'att_sink': nrm((O, ATT_Q_HEADS), 1.0),
    }


def reference(x_prompt, x_sample, rel_bias, norm_mix_w, norm_ffn_w, ffn_w_in, ffn_w_out, ev_w_in, ev_w_out,
              ssd_conv_w, ssd_conv_b, ssd_dt_bias, ssd_a_log, ssd_d, ssd_norm_w, rwkv_mu, rwkv_w0, rwkv_w2,
              rwkv_a0, rwkv_a2, rwkv_g2, rwkv_k_k, rwkv_k_a, rwkv_r_k, rwkv_ln_w, rwkv_ln_b, od_w_in, od_w_out,
              conv_dw_w, conv_dw_b, conv_ln_w, conv_ln_b, att_q_norm_w, att_k_norm_w, att_sink):
    weights = (rel_bias, norm_mix_w, norm_ffn_w, ffn_w_in, ffn_w_out, ev_w_in, ev_w_out, ssd_conv_w, ssd_conv_b,
               ssd_dt_bias, ssd_a_log, ssd_d, ssd_norm_w, rwkv_mu, rwkv_w0, rwkv_w2, rwkv_a0, rwkv_a2, rwkv_g2,
               rwkv_k_k, rwkv_k_a, rwkv_r_k, rwkv_ln_w, rwkv_ln_b, od_w_in, od_w_out, conv_dw_w, conv_dw_b,
               conv_ln_w, conv_ln_b, att_q_norm_w, att_k_norm_w, att_sink)
    y_prompt = _trunk(x_prompt, *weights)
    y_sample = _trunk(x_sample, *weights)
    return (y_prompt, y_sample)
```

```python
import math
from contextlib import ExitStack
import numpy as np
import concourse.bass as bass
import concourse.mybir as mybir
from concourse.bass_utils import run_bass_kernel_spmd

F32 = mybir.dt.float32
BF16 = mybir.dt.bfloat16
AF = mybir.ActivationFunctionType
ALU = mybir.AluOpType
AX = mybir.AxisListType

D = 1024
DFF = 2816
EV_IN = 5920
OD_IN = 3584
ENGS = ('pe', 'act', 'dve', 'pool', 'sp')


class Buf:
    __slots__ = ('name', 'writers', 'readers', 'dsem', 'excl')

    def __init__(self, name, excl=False):
        self.name = name
        self.excl = excl
        self.writers = {}
        self.readers = {}
        self.dsem = None


class Sched:
    def __init__(self, nc, stack):
        self.nc = nc
        self.stack = stack
        self.q = {e: [] for e in ENGS}
        self.seq = {}
        self.known = {e: {} for e in ENGS}
        self.semh = {}
        self.ninst = 0
        self.free_keys = []
        self.phase_keys = []
        for e in ('pe', 'act', 'dve', 'pool'):
            self.semh[e] = stack.enter_context(nc.semaphore('s_' + e))
            self.seq[e] = 0

    def _dsem(self, buf):
        if buf.dsem is None:
            if self.free_keys:
                k = self.free_keys.pop()
            else:
                k = 'd%d' % len(self.semh)
                self.semh[k] = self.stack.enter_context(self.nc.semaphore(k))
                self.seq[k] = 0
            buf.dsem = k
            self.phase_keys.append(k)
        return buf.dsem

    def end_phase(self):
        self.barrier()
        self.flush()
        self.free_keys.extend(self.phase_keys)
        self.phase_keys = []

    def op(self, eng, fn, reads=(), writes=(), dma=None):
        if dma is not None:
            key = self._dsem(dma)
            inc = 16
        else:
            key = eng
            inc = 1
        deps = {}
        for b in reads:
            for k, v in b.writers.items():
                if deps.get(k, 0) < v:
                    deps[k] = v
            if b.excl:
                for k, v in b.readers.items():
                    if k != key and deps.get(k, 0) < v:
                        deps[k] = v
        for b in writes:
            for k, v in b.writers.items():
                if k == key:
                    continue
                if deps.get(k, 0) < v:
                    deps[k] = v
            for k, v in b.readers.items():
                if k == key and dma is None:
                    continue
                if deps.get(k, 0) < v:
                    deps[k] = v
        kn = self.known[eng]
        waits = []
        for k, v in deps.items():
            if kn.get(k, 0) < v:
                kn[k] = v
                waits.append((k, v))
        self.seq[key] += inc
        tok = self.seq[key]
        self.q[eng].append((waits, fn, key, inc))
        self.ninst += 1 + len(waits)
        for b in reads:
            if b.readers.get(key, 0) < tok:
                b.readers[key] = tok
        for b in writes:
            b.writers[key] = tok
            b.readers = {}

    def barrier(self):
        for e in ENGS:
            kn = self.known[e]
            waits = []
            for k, v in self.seq.items():
                if kn.get(k, 0) < v:
                    kn[k] = v
                    waits.append((k, v))
            if waits:
                self.q[e].append((waits, None, None, 0))

    def flush(self):
        nc = self.nc
        semh = self.semh
        q = self.q

        def replay(name, eng):
            for waits, fn, key, inc in q[name]:
                for k, v in waits:
                    eng.wait_ge(semh[k], v)
                if fn is not None:
                    fn(eng).then_inc(semh[key], inc)

        with nc.Block() as block:
            @block.tensor
            def _(e):
                replay('pe', e)

            @block.scalar
            def _(e):
                replay('act', e)

            @block.vector
            def _(e):
                replay('dve', e)

            @block.gpsimd
            def _(e):
                replay('pool', e)

            @block.sync
            def _(e):
                replay('sp', e)
        self.q = {e: [] for e in ENGS}


class Ctx:
    pass


_UID = [0]


def _sbt(nc, name, shape, dt):
    _UID[0] += 1
    return nc.sbuf_tensor('%s_%d' % (name, _UID[0]), shape, dt)


def _mm(S, out, lhsT, rhs, start, stop, reads, writes):
    S.op('pe', lambda e: e.matmul(out, lhsT, rhs, start=start, stop=stop), reads=reads, writes=writes)


def build_program(cfg):
    N = cfg['N']
    debug = cfg.get('debug', False)
    layers = cfg.get('layers', [0, 1, 2, 3])
    nc = bass.Bass("TRN2", target_bir_lowering=False)
    C = Ctx()
    C.nc = nc
    C.N = N
    C.cfg = cfg
    TT = 512
    assert N % TT == 0
    NT = N // TT

    def din(name, shape, dt=F32):
        return nc.dram_tensor(name, list(shape), dt, kind="ExternalInput").ap()

    def dscr(name, shape, dt=F32):
        return nc.dram_tensor(name, list(shape), dt, kind="ExternalOutput" if debug else "Internal").ap()

    xin = din('xin', [N, D])
    flag = din('flag', [128, 1])
    W = {}
    wshapes = dict(
        rel_bias=(32, 16), norm_mix_w=(4, D), norm_ffn_w=(4, D), ffn_w_in=(4, D, 2 * DFF), ffn_w_out=(4, DFF, D),
        ev_w_in=(2, D, EV_IN), ev_w_out=(2, 2048, D), ssd_conv_w=(2, 5, 1536), ssd_conv_b=(2, 1536),
        ssd_dt_bias=(2, 2, 16), ssd_a_log=(2, 2, 16), ssd_d=(2, 16), ssd_norm_w=(2, D), rwkv_mu=(2, 3328),
        rwkv_w0=(2, 2, D), rwkv_w2=(2, 2, 64, D), rwkv_a0=(2, D), rwkv_a2=(2, 64, D), rwkv_g2=(2, 128, D),
        rwkv_k_k=(2, D), rwkv_k_a=(2, D), rwkv_r_k=(2, 16, 64), rwkv_ln_w=(2, D), rwkv_ln_b=(2, D),
        od_w_in=(2, D, OD_IN), od_w_out=(2, 2048, D), conv_dw_w=(2, 31, D), conv_dw_b=(2, D), conv_ln_w=(2, D),
        conv_ln_b=(2, D), att_q_norm_w=(2, 64), att_k_norm_w=(2, 64), att_sink=(2, 16))
    for k, shp in wshapes.items():
        W[k] = din(k, shp)
    yout = nc.dram_tensor('yout', [N, D], F32, kind="ExternalOutput").ap()
    XA = dscr('XA', [D, N])
    XB = dscr('XB', [D, N])
    UF = dscr('UF', [38 * 128, N])
    UT = dscr('UT', [N, 1056])
    Y = dscr('Y', [2048, N], BF16)
    H = dscr('H', [DFF, N], BF16)
    C.xin, C.flag, C.W, C.yout = xin, flag, W, yout
    C.XA, C.XB, C.UF, C.UT, C.Y, C.H = XA, XB, UF, UT, Y, H
    C.bXA, C.bXB, C.bUF, C.bUT, C.bY, C.bH = Buf('XA'), Buf('XB'), Buf('UF'), Buf('UT'), Buf('Y'), Buf('H')
    C.bIN = Buf('in')
    C.bOUT = Buf('yout')

    with ExitStack() as top:
        S = Sched(nc, top)
        C.S = S
        C.ps = []
        for i in range(8):
            C.ps.append((top.enter_context(nc.psum_tensor('ps%d' % i, [128, 512], F32)), Buf('ps%d' % i, excl=True)))
        C.psi = 0
        ident = top.enter_context(_sbt(nc, 'ident', [128, 128], F32))
        identb = top.enter_context(_sbt(nc, 'identb', [128, 128], BF16))
        onesb = top.enter_context(_sbt(nc, 'onesb', [128, 128], BF16))
        C.ident, C.identb, C.onesb = ident, identb, onesb
        C.bconst = Buf('const')
        identd = din('c_ident', [128, 128])
        C.const_inputs = {'c_ident': np.eye(128, dtype=np.float32)}
        S.op('sp', lambda e: e.dma_start(out=ident[:], in_=identd), writes=[C.bconst], dma=C.bconst)
        S.op('dve', lambda e: e.tensor_copy(identb[:], ident[:]), reads=[C.bconst], writes=[C.bconst])
        S.op('dve', lambda e: e.memset(onesb[:], 1.0), writes=[C.bconst])
        C.epsD = top.enter_context(_sbt(nc, 'epsD', [128, 4], F32))
        S.op('dve', lambda e: e.memset(C.epsD[:, 0:1], float(D * 1e-6)), writes=[C.bconst])
        S.op('dve', lambda e: e.memset(C.epsD[:, 1:2], 1e-6), writes=[C.bconst])
        S.op('dve', lambda e: e.memset(C.epsD[:, 2:3], 1e-5), writes=[C.bconst])
        S.op('dve', lambda e: e.memset(C.epsD[:, 3:4], 64e-5), writes=[C.bconst])
        C.flagt = top.enter_context(_sbt(nc, 'flagt', [128, 1], F32))
        S.op('sp', lambda e: e.dma_start(out=C.flagt[:], in_=flag), writes=[C.bconst], dma=C.bconst)
        C.c_oh = din('c_oh', [33, 765])
        C.c_anti = din('c_anti', [128, 128])
        C.c_bd = din('c_bd', [128, 128])
        C.const_inputs['c_oh'] = t5_tables()
        C.const_inputs['c_anti'] = np.ascontiguousarray(np.eye(128, dtype=np.float32)[::-1])
        C.const_inputs['c_bd'] = np.kron(np.eye(2, dtype=np.float32), np.ones((64, 64), np.float32))
        C.D2 = nc.dram_tensor('D2', [16, 765], F32, kind="Internal")
        C.c_tri = din('c_tri', [128, 2, 128])
        C.c_nm = din('c_nm', [128, 2, 128])
        ii = np.arange(128)
        tri = np.zeros((128, 2, 128), np.float32)
        tri[:, 0, :] = (ii[:, None] <= ii[None, :])
        tri[:, 1, :] = (ii[:, None] >= ii[None, :])
        nm = np.zeros((128, 2, 128), np.float32)
        nm[:, 0, :] = np.where(ii[:, None] > ii[None, :], -30000.0, 0.0)
        nm[:, 1, :] = np.where(ii[:, None] < ii[None, :], -30000.0, 0.0)
        C.const_inputs['c_tri'] = tri
        C.c_tri2 = din('c_tri2', [128, 2, 128])
        tri2 = np.zeros((128, 2, 128), np.float32)
        tri2[:, 0, :] = (ii[:, None] < ii[None, :])
        tri2[:, 1, :] = (ii[:, None] > ii[None, :])
        C.const_inputs['c_tri2'] = tri2
        C.const_inputs['c_nm'] = nm
        C.onec = top.enter_context(_sbt(nc, 'onec', [128, 1], F32))
        S.op('dve', lambda e: e.memset(C.onec[:], 1.0), writes=[C.bconst])
        C.YS = dscr('YS', [N, D])
        if cfg.get('dbg'):
            C.DBG = nc.dram_tensor('DBG', [17, 128, 8192], F32, kind='ExternalOutput').ap()
        C.bYS = Buf('YS')
        S.barrier()
        S.flush()
        S.phase_keys = []

        phase_in(C)
        cur, nxt = (XA, C.bXA), (XB, C.bXB)
        for li, layer in enumerate(layers):
            last = (li == len(layers) - 1)
            if layer % 2 == 0:
                e = layer // 2
                phase_inproj_even(C, cur, layer, e)
                if cfg.get('mixers', True):
                    phase_mix_even(C, e)
            else:
                o = layer // 2
                phase_inproj_odd(C, cur, layer, o)
                if cfg.get('mixers', True):
                    phase_mix_odd(C, o)
            wout = W['ev_w_out'][layer // 2] if layer % 2 == 0 else W['od_w_out'][layer // 2]
            phase_outproj(C, cur, nxt, wout)
            phase_ffn_in(C, nxt, layer)
            phase_ffn_out(C, nxt, cur, layer, last)
        if not layers:
            phase_out_only(C, cur)
        S.barrier()
        S.flush()
    return nc, C


def next_ps(C):
    t, b = C.ps[C.psi]
    C.psi = (C.psi + 1) % 8
    return t, b


def phase_in(C):
    nc, S, N = C.nc, C.S, C.N
    with ExitStack() as st:
        xt = [st.enter_context(_sbt(nc, 'pi_x%d' % i, [128, D], F32)) for i in range(2)]
        bx = [Buf('pi_x%d' % i) for i in range(2)]
        ot = [st.enter_context(_sbt(nc, 'pi_o%d' % i, [128, 8, 512], F32)) for i in range(2)]
        bo = [Buf('pi_o%d' % i) for i in range(2)]
        nsub = N // 128
        for t512 in range(N // 512):
            o, bob = ot[t512 % 2], bo[t512 % 2]
            for s4 in range(4):
                sub = t512 * 4 + s4
                x, bxb = xt[sub % 2], bx[sub % 2]
                S.op('sp', lambda e, x=x, sub=sub: e.dma_start(out=x[:], in_=C.xin[sub * 128:(sub + 1) * 128, :]),
                     reads=[C.bIN], writes=[bxb], dma=bxb)
                for half in range(2):
                    ps, bps = next_ps(C)
                    for j in range(4):
                        c = half * 4 + j
                        _mm(S, ps[:, j * 128:(j + 1) * 128], x[:, c * 128:(c + 1) * 128], C.ident[:], True, True,
                            [bxb, C.bconst], [bps])
                    eng = 'act' if half == 0 else 'dve'
                    dst = o[:, half * 4:half * 4 + 4, s4 * 128:(s4 + 1) * 128]
                    src = ps[:, :].rearrange("p (j t) -> p j t", j=4)
                    if eng == 'act':
                        S.op('act', lambda e, dst=dst, src=src: e.copy(dst, src), reads=[bps], writes=[bob])
                    else:
                        S.op('dve', lambda e, dst=dst, src=src: e.tensor_copy(dst, src), reads=[bps], writes=[bob])
            dst = C.XA.rearrange("(c p) n -> p c n", p=128)[:, :, t512 * 512:(t512 + 1) * 512]
            S.op('pool', lambda e, dst=dst, o=o: e.dma_start(out=dst, in_=o[:]), reads=[bob], writes=[C.bXA], dma=bob)
        S.end_phase()


def load_weight_bf16(C, wt, bw, wdram, K, col0, ncols, dcol0=0):
    S = C.S
    kc = K // 128
    src = wdram.rearrange("(c p) m -> p c m", p=128)
    step = 1024
    for c in range(kc):
        for m0 in range(0, ncols, step):
            m1 = min(ncols, m0 + step)
            S.op('pool', lambda e, c=c, m0=m0, m1=m1: e.dma_start(
                out=wt[:, c, dcol0 + m0:dcol0 + m1], in_=src[:, c, col0 + m0:col0 + m1]),
                writes=[bw], dma=bw)


def load_cols(C, dst_tile, bdst, vec_dram, nchunk):
    S = C.S
    src = vec_dram.rearrange("(c p) -> p c", p=128)
    C.nc
    S.op('sp', lambda e: e.dma_start(out=dst_tile, in_=src, allow_slow_non_contiguous=True), writes=[bdst], dma=Buf('lc'))


def rmsnorm_tile(C, xT, bx, hT, bh, w32, bw, sq, bsq, rstd, brs):
    S = C.S
    S.op('act', lambda e: e.activation(sq[:], xT[:], AF.Square), reads=[bx], writes=[bsq])
    ps, bps = next_ps(C)
    for c in range(8):
        _mm(S, ps[:, :], C.onesb[:], sq[:, c, :], c == 0, c == 7, [bsq, C.bconst], [bps])
    S.op('act', lambda e: e.activation(rstd[:], ps[:, :], AF.Sqrt, bias=C.epsD[:, 0:1], scale=1.0),
         reads=[bps, C.bconst], writes=[brs])
    S.op('dve', lambda e: e.reciprocal(rstd[:], rstd[:]), reads=[brs], writes=[brs])
    for c in range(8):
        S.op('dve', lambda e, c=c: e.scalar_tensor_tensor(hT[:, c, :], xT[:, c, :], w32[:, c:c + 1], rstd[:],
                                                         ALU.mult, ALU.mult),
             reads=[bx, brs, bw], writes=[bh])


def evac(C, idx, dst, src, bsrc, bdst, extra_reads=()):
    S = C.S
    if idx % 2 == 0:
        S.op('act', lambda e: e.copy(dst, src), reads=[bsrc] + list(extra_reads), writes=[bdst])
    else:
        S.op('dve', lambda e: e.tensor_copy(dst, src), reads=[bsrc] + list(extra_reads), writes=[bdst])


def phase_inproj(C, cur, norm_w, wdram, M, wcols_extra, fm_chunks, tm_groups, name):
    nc, S, N = C.nc, C.S, C.N
    X, bX = cur
    Mtot = M + sum(n for _, n, _ in wcols_extra)
    with ExitStack() as st:
        wt = st.enter_context(_sbt(nc, name + '_w', [128, 8, Mtot], BF16))
        bw = Buf(name + '_w')
        load_weight_bf16(C, wt, bw, wdram, D, 0, M)
        for (s0, n, d0) in wcols_extra:
            load_weight_bf16(C, wt, bw, wdram, D, s0, n, d0)
        nw = st.enter_context(_sbt(nc, name + '_nw', [128, 8], F32))
        bnw = Buf(name + '_nw')
        load_cols(C, nw[:], bnw, norm_w, 8)
        S.op('dve', lambda e: e.tensor_scalar(nw[:], nw[:], 32.0, None, ALU.mult), reads=[bnw], writes=[bnw])
        xT = [st.enter_context(_sbt(nc, name + '_x%d' % i, [128, 8, 512], F32)) for i in range(2)]
        bx = [Buf(name + '_x%d' % i) for i in range(2)]
        hT = [st.enter_context(_sbt(nc, name + '_h%d' % i, [128, 8, 512], BF16)) for i in range(2)]
        bh = [Buf(name + '_h%d' % i) for i in range(2)]
        sq = st.enter_context(_sbt(nc, name + '_sq', [128, 8, 512], BF16))
        bsq = Buf('sq')
        rstd = st.enter_context(_sbt(nc, name + '_rs', [128, 512], F32))
        brs = Buf('rs')
        NSTG = 4
        stg = [st.enter_context(_sbt(nc, name + '_st%d' % i, [128, 2, 512], F32)) for i in range(NSTG)]
        bst = [Buf(name + '_st%d' % i) for i in range(NSTG)]
        Xv = X.rearrange("(c p) n -> p c n", p=128)
        si = 0
        ei = 0
        for t in range(N // 512):
            x, bxx, h, bhh = xT[t % 2], bx[t % 2], hT[t % 2], bh[t % 2]
            for half in range(2):
                S.op('sp', lambda e, x=x, t=t, half=half: e.dma_start(
                    out=x[:, half * 4:half * 4 + 4, :], in_=Xv[:, half * 4:half * 4 + 4, t * 512:(t + 1) * 512]),
                    reads=[bX], writes=[bxx], dma=bxx)
            rmsnorm_tile(C, x, bxx, h, bhh, nw, bnw, sq, bsq, rstd, brs)
            for i in range(0, len(fm_chunks), 2):
                grp = fm_chunks[i:i + 2]
                sg, bsg = stg[si % NSTG], bst[si % NSTG]
                si += 1
                for j, (wc0, ur0) in enumerate(grp):
                    ps, bps = next_ps(C)
                    for kc in range(8):
                        _mm(S, ps[:, :], wt[:, kc, wc0:wc0 + 128], h[:, kc, :], kc == 0, kc == 7, [bw, bhh], [bps])
                    evac(C, ei, sg[:, j, :], ps[:, :], bps, bsg)
                    ei += 1
                contiguous = len(grp) == 2 and grp[1][1] == grp[0][1] + 128
                if contiguous:
                    ur0 = grp[0][1]
                    dst = C.UF[ur0:ur0 + 256, t * 512:(t + 1) * 512].rearrange("(c p) n -> p c n", p=128)
                    S.op('pool', lambda e, dst=dst, sg=sg: e.dma_start(out=dst, in_=sg[:]),
                         reads=[bsg], writes=[C.bUF], dma=bsg)
                else:
                    for j, (wc0, ur0) in enumerate(grp):
                        dst = C.UF[ur0:ur0 + 128, t * 512:(t + 1) * 512]
                        S.op('pool', lambda e, dst=dst, sg=sg, j=j: e.dma_start(out=dst, in_=sg[:, j, :]),
                             reads=[bsg], writes=[C.bUF], dma=bsg)
            for s4 in range(4):
                for (wc0, ncol, uc0) in tm_groups:
                    sg, bsg = stg[si % NSTG], bst[si % NSTG]
                    si += 1
                    ps, bps = next_ps(C)
                    for kc in range(8):
                        _mm(S, ps[:, 0:ncol], h[:, kc, s4 * 128:(s4 + 1) * 128], wt[:, kc, wc0:wc0 + ncol],
                            kc == 0, kc == 7, [bw, bhh], [bps])
                    sgv = sg[:].rearrange("p a b -> p (a b)")[:, 0:ncol]
                    evac(C, ei, sgv, ps[:, 0:ncol], bps, bsg)
                    ei += 1
                    dst = C.UT[t * 512 + s4 * 128: t * 512 + (s4 + 1) * 128, uc0:uc0 + ncol]
                    S.op('pool', lambda e, dst=dst, sgv=sgv: e.dma_start(out=dst, in_=sgv),
                         reads=[bsg], writes=[C.bUT], dma=bsg)
        S.end_phase()


def phase_inproj_even(C, cur, layer, e):
    W = C.W
    fm = [(1024 + i * 128, i * 128) for i in range(12)] + [(2592 + i * 128, 1536 + i * 128) for i in range(26)]
    tm = [(0, 512, 0), (512, 512, 512), (2560, 32, 1024)]
    phase_inproj(C, cur, W['norm_mix_w'][layer], W['ev_w_in'][e], EV_IN, [], fm, tm, 'ie%d' % layer)


def phase_inproj_odd(C, cur, layer, o):
    W = C.W
    extra = []
    for hk in range(4):
        extra.append((3072 + hk * 64, 64, OD_IN + hk * 128))
        extra.append((3072 + hk * 64, 64, OD_IN + hk * 128 + 64))
    fm = [(i * 128, i * 128) for i in range(24)] + [(OD_IN + i * 128, 3072 + i * 128) for i in range(4)]
    tm = [(3328, 256, 0)]
    phase_inproj(C, cur, W['norm_mix_w'][layer], W['od_w_in'][o], OD_IN, extra, fm, tm, 'io%d' % layer)


def phase_outproj(C, cur, nxt, wdram):
    nc, S, N = C.nc, C.S, C.N
    X, bX = cur
    X2, bX2 = nxt
    with ExitStack() as st:
        wt = st.enter_context(_sbt(nc, 'op_w', [128, 16, D], BF16))
        bw = Buf('op_w')
        load_weight_bf16(C, wt, bw, wdram, 2048, 0, D)
        xT = [st.enter_context(_sbt(nc, 'op_x%d' % i, [128, 8, 512], F32)) for i in range(2)]
        bx = [Buf('op_x%d' % i) for i in range(2)]
        yT = [st.enter_context(_sbt(nc, 'op_y%d' % i, [128, 16, 512], BF16)) for i in range(2)]
        by = [Buf('op_y%d' % i) for i in range(2)]
        Xv = X.rearrange("(c p) n -> p c n", p=128)
        X2v = X2.rearrange("(c p) n -> p c n", p=128)
        Yv = C.Y.rearrange("(c p) n -> p c n", p=128)
        for t in range(N // 512):
            x, bxx, y, byy = xT[t % 2], bx[t % 2], yT[t % 2], by[t % 2]
            sl = slice(t * 512, (t + 1) * 512)
            for half in range(2):
                S.op('sp', lambda e, x=x, sl=sl, half=half: e.dma_start(
                    out=x[:, half * 4:half * 4 + 4, :], in_=Xv[:, half * 4:half * 4 + 4, sl]),
                    reads=[bX], writes=[bxx], dma=bxx)
                S.op('sp', lambda e, y=y, sl=sl, half=half: e.dma_start(
                    out=y[:, half * 8:half * 8 + 8, :], in_=Yv[:, half * 8:half * 8 + 8, sl]),
                    reads=[C.bY], writes=[byy], dma=byy)
            for oc in range(8):
                ps, bps = next_ps(C)
                for kc in range(16):
                    _mm(S, ps[:, :], wt[:, kc, oc * 128:(oc + 1) * 128], y[:, kc, :], kc == 0, kc == 15, [bw, byy], [bps])
                S.op('dve', lambda e, x=x, ps=ps, oc=oc: e.tensor_tensor(x[:, oc, :], x[:, oc, :], ps[:, :], ALU.add),
                     reads=[bps, bxx], writes=[bxx])
            for half in range(2):
                S.op('pool', lambda e, x=x, sl=sl, half=half: e.dma_start(
                    out=X2v[:, half * 4:half * 4 + 4, sl], in_=x[:, half * 4:half * 4 + 4, :]),
                    reads=[bxx], writes=[bX2], dma=bxx)
        S.end_phase()


def phase_ffn_in(C, cur, layer):
    nc, S, N = C.nc, C.S, C.N
    X, bX = cur
    W = C.W
    name = 'fi'
    with ExitStack() as st:
        wt = st.enter_context(_sbt(nc, 'fi_w', [128, 8, 2 * DFF], BF16))
        bw = Buf('fi_w')
        load_weight_bf16(C, wt, bw, W['ffn_w_in'][layer], D, 0, 2 * DFF)
        nw = st.enter_context(_sbt(nc, 'fi_nw', [128, 8], F32))
        bnw = Buf('fi_nw')
        load_cols(C, nw[:], bnw, W['norm_ffn_w'][layer], 8)
        S.op('dve', lambda e: e.tensor_scalar(nw[:], nw[:], 32.0, None, ALU.mult), reads=[bnw], writes=[bnw])
        xT = [st.enter_context(_sbt(nc, 'fi_x%d' % i, [128, 8, 512], F32)) for i in range(2)]
        bx = [Buf('fi_x%d' % i) for i in range(2)]
        hT = [st.enter_context(_sbt(nc, 'fi_h%d' % i, [128, 8, 512], BF16)) for i in range(2)]
        bh = [Buf('fi_h%d' % i) for i in range(2)]
        sq = st.enter_context(_sbt(nc, 'fi_sq', [128, 8, 512], BF16))
        bsq = Buf('sq')
        rstd = st.enter_context(_sbt(nc, 'fi_rs', [128, 512], F32))
        brs = Buf('rs')
        hid = [st.enter_context(_sbt(nc, 'fi_hid%d' % i, [128, 22, 512], BF16)) for i in range(2)]
        bhid = [Buf('fi_hid%d' % i) for i in range(2)]
        sg = [st.enter_context(_sbt(nc, 'fi_sg%d' % i, [128, 512], F32)) for i in range(2)]
        bsg = [Buf('fi_sg%d' % i) for i in range(2)]
        Xv = X.rearrange("(c p) n -> p c n", p=128)
        Hv = C.H.rearrange("(c p) n -> p c n", p=128)
        for t in range(N // 512):
            x, bxx, h, bhh = xT[t % 2], bx[t % 2], hT[t % 2], bh[t % 2]
            hd, bhd = hid[t % 2], bhid[t % 2]
            sl = slice(t * 512, (t + 1) * 512)
            for half in range(2):
                S.op('sp', lambda e, x=x, sl=sl, half=half: e.dma_start(
                    out=x[:, half * 4:half * 4 + 4, :], in_=Xv[:, half * 4:half * 4 + 4, sl]),
                    reads=[bX], writes=[bxx], dma=bxx)
            rmsnorm_tile(C, x, bxx, h, bhh, nw, bnw, sq, bsq, rstd, brs)
            for fc in range(22):
                psg, bpsg = next_ps(C)
                for kc in range(8):
                    _mm(S, psg[:, :], wt[:, kc, fc * 128:(fc + 1) * 128], h[:, kc, :], kc == 0, kc == 7, [bw, bhh], [bpsg])
                psu, bpsu = next_ps(C)
                for kc in range(8):
                    _mm(S, psu[:, :], wt[:, kc, DFF + fc * 128:DFF + (fc + 1) * 128], h[:, kc, :], kc == 0, kc == 7,
                        [bw, bhh], [bpsu])
                s_, bs_ = sg[fc % 2], bsg[fc % 2]
                S.op('act', lambda e, s_=s_, psg=psg: e.activation(s_[:], psg[:, :], AF.Silu), reads=[bpsg], writes=[bs_])
                S.op('dve', lambda e, s_=s_, psu=psu, hd=hd, fc=fc: e.tensor_tensor(hd[:, fc, :], s_[:], psu[:, :], ALU.mult),
                     reads=[bs_, bpsu], writes=[bhd])
            for (c0, c1) in ((0, 8), (8, 16), (16, 22)):
                S.op('pool', lambda e, hd=hd, sl=sl, c0=c0, c1=c1: e.dma_start(out=Hv[:, c0:c1, sl], in_=hd[:, c0:c1, :]),
                     reads=[bhd], writes=[C.bH], dma=bhd)
        S.end_phase()


def phase_ffn_out(C, cur, nxt, layer, last):
    nc, S, N = C.nc, C.S, C.N
    X, bX = cur
    X2, bX2 = nxt
    W = C.W
    with ExitStack() as st:
        wt = st.enter_context(_sbt(nc, 'fo_w', [128, 22, D], BF16))
        bw = Buf('fo_w')
        load_weight_bf16(C, wt, bw, W['ffn_w_out'][layer], DFF, 0, D)
        xT = [st.enter_context(_sbt(nc, 'fo_x%d' % i, [128, 8, 512], F32)) for i in range(2)]
        bx = [Buf('fo_x%d' % i) for i in range(2)]
        hd = [st.enter_context(_sbt(nc, 'fo_h%d' % i, [128, 22, 512], BF16)) for i in range(2)]
        bhd = [Buf('fo_h%d' % i) for i in range(2)]
        ot = [st.enter_context(_sbt(nc, 'fo_o%d' % i, [128, D], F32)) for i in range(2)]
        bo = [Buf('fo_o%d' % i) for i in range(2)]
        Xv = X.rearrange("(c p) n -> p c n", p=128)
        X2v = X2.rearrange("(c p) n -> p c n", p=128)
        Hv = C.H.rearrange("(c p) n -> p c n", p=128)
        oi = 0
        for t in range(N // 512):
            x, bxx, h, bhh = xT[t % 2], bx[t % 2], hd[t % 2], bhd[t % 2]
            sl = slice(t * 512, (t + 1) * 512)
            for half in range(2):
                S.op('sp', lambda e, x=x, sl=sl, half=half: e.dma_start(
                    out=x[:, half * 4:half * 4 + 4, :], in_=Xv[:, half * 4:half * 4 + 4, sl]),
                    reads=[bX], writes=[bxx], dma=bxx)
            for (c0, c1) in ((0, 8), (8, 16), (16, 22)):
                S.op('sp', lambda e, h=h, sl=sl, c0=c0, c1=c1: e.dma_start(out=h[:, c0:c1, :], in_=Hv[:, c0:c1, sl]),
                     reads=[C.bH], writes=[bhh], dma=bhh)
            for oc in range(8):
                ps, bps = next_ps(C)
                for kc in range(22):
                    _mm(S, ps[:, :], wt[:, kc, oc * 128:(oc + 1) * 128], h[:, kc, :], kc == 0, kc == 21, [bw, bhh], [bps])
                S.op('dve', lambda e, x=x, ps=ps, oc=oc: e.tensor_tensor(x[:, oc, :], x[:, oc, :], ps[:, :], ALU.add),
                     reads=[bps, bxx], writes=[bxx])
            if not last:
                for half in range(2):
                    S.op('pool', lambda e, x=x, sl=sl, half=half: e.dma_start(
                        out=X2v[:, half * 4:half * 4 + 4, sl], in_=x[:, half * 4:half * 4 + 4, :]),
                        reads=[bxx], writes=[bX2], dma=bxx)
            else:
                for s4 in range(4):
                    o, bob = ot[oi % 2], bo[oi % 2]
                    oi += 1
                    for half in range(2):
                        ps, bps = next_ps(C)
                        for j in range(4):
                            c = half * 4 + j
                            _mm(S, ps[:, j * 128:(j + 1) * 128], x[:, c, s4 * 128:(s4 + 1) * 128], C.ident[:], True, True,
                                [bxx, C.bconst], [bps])
                        evac(C, half, o[:, half * 512:(half + 1) * 512], ps[:, :], bps, bob)
                    r0 = t * 512 + s4 * 128
                    S.op('pool', lambda e, o=o, r0=r0: e.dma_start(out=C.yout[r0:r0 + 128, :], in_=o[:]),
                         reads=[bob], writes=[C.bOUT], dma=bob)
        S.end_phase()


def phase_out_only(C, cur):
    raise NotImplementedError


def phase_mix_even(C, e):
    phase_ssd(C, e)
    if C.cfg.get('rwkv', True):
        phase_rwkv(C, e)


def bcast_row(C, dst, vec_ap_1d, n, bdst):
    src = bass.AP(vec_ap_1d.tensor, vec_ap_1d.offset, [[0, 128], [1, n]])
    C.S.op('sp', lambda e: e.dma_start(out=dst, in_=src), writes=[bdst], dma=Buf('bc'))


def phase_ssd(C, e):
    nc, S, N, W = C.nc, C.S, C.N, C.W
    TT = 512
    with ExitStack() as st:
        sb = lambda name, shape, dt=F32: st.enter_context(_sbt(nc, name, list(shape), dt))
        bset = Buf('ss_setup')
        wnat = sb('ss_wnat', [5, 1536])
        S.op('sp', lambda e_: e_.dma_start(out=wnat[:], in_=W['ssd_conv_w'][e]), writes=[bset], dma=Buf('x'))
        wcol = sb('ss_wcol', [128, 12, 5])
        for c in range(12):
            ps, bps = next_ps(C)
            _mm(S, ps[:, 0:5], wnat[0:5, c * 128:(c + 1) * 128], C.ident[0:5, 0:5], True, True, [bset, C.bconst], [bps])
            S.op('dve', lambda e_, c=c, ps=ps: e_.tensor_copy(wcol[:, c, :], ps[:, 0:5]), reads=[bps], writes=[bset])
        dg = sb('ss_dg', [128, 12, 5, 128], BF16)
        for c in range(12):
            for k in range(5):
                S.op('dve', lambda e_, c=c, k=k: e_.tensor_scalar(dg[:, c, k, :], C.identb[:], wcol[:, c, k:k + 1], None, ALU.mult),
                     reads=[bset, C.bconst], writes=[bset])
        cvb = sb('ss_cvb', [128, 12])
        load_cols(C, cvb[:], bset, W['ssd_conv_b'][e], 12)
        nrw = sb('ss_nrw', [128, 8])
        load_cols(C, nrw[:], bset, W['ssd_norm_w'][e], 8)
        dtb = sb('ss_dtb', [128, 32])
        bcast_row(C, dtb[:], W['ssd_dt_bias'][e].rearrange("a b -> (a b)"), 32, bset)
        Ab = sb('ss_Ab', [128, 32])
        bcast_row(C, Ab[:], W['ssd_a_log'][e].rearrange("a b -> (a b)"), 32, bset)
        S.op('act', lambda e_: e_.activation(Ab[:], Ab[:], AF.Exp), reads=[bset], writes=[bset])
        S.op('dve', lambda e_: e_.tensor_scalar(Ab[:], Ab[:], -1.0, None, ALU.mult), reads=[bset], writes=[bset])
        dsk = sb('ss_dsk', [128, 16])
        bcast_row(C, dsk[:], W['ssd_d'][e], 16, bset)
        onesf = sb('ss_onesf', [128, 128])
        S.op('dve', lambda e_: e_.memset(onesf[:], 1.0), writes=[bset])
        tri = sb('ss_tri', [128, 2, 128])
        S.op('sp', lambda e_: e_.dma_start(out=tri[:], in_=C.c_tri), writes=[bset], dma=Buf('x'))
        nmf = sb('ss_nmf', [128, 2, 128])
        S.op('sp', lambda e_: e_.dma_start(out=nmf[:], in_=C.c_nm), writes=[bset], dma=Buf('x'))
        nm4 = sb('ss_nm4', [128, 2, 4, 128], BF16)
        for d in range(2):
            for j in range(4):
                S.op('dve', lambda e_, d=d, j=j: e_.tensor_copy(nm4[:, d, j, :], nmf[:, d, :]), reads=[bset], writes=[bset])

        xin = [sb('ss_xin%d' % i, [128, 12, 516]) for i in range(2)]
        bxin = [Buf('ss_xin%d' % i) for i in range(2)]
        xbf = sb('ss_xbf', [128, 12, 516], BF16)
        bxbf = Buf('ss_xbf')
        xcf = sb('ss_xcf', [128, 8, TT])
        bxcf = Buf('ss_xcf')
        bcf = sb('ss_bcf', [128, 4, TT], BF16)
        bbcf = Buf('ss_bcf')
        xT = sb('ss_xT', [128, 4, 1024])
        bxT = Buf('ss_xT')
        BT = sb('ss_BT', [128, 4, 256], BF16)
        bBT = Buf('ss_BT')
        Sst = sb('ss_S', [128, 1024])
        Sb = sb('ss_Sb', [128, 1024], BF16)
        bS = Buf('ss_S')
        smts = [sb('ss_sm%d' % i, [128, 12, 4, 16]) for i in range(2)]
        bsms = [Buf('ss_sm%d' % i) for i in range(2)]
        R = sb('ss_R', [128, 16, 128])
        bR = Buf('ss_R')
        Dm = sb('ss_Dm', [128, 16, 128], BF16)
        bDm = Buf('ss_Dm')
        cb = sb('ss_cb', [128, 2, 128], BF16)
        bcb = Buf('ss_cb')
        Mt = sb('ss_Mt', [128, 16, 128], BF16)
        bMt = Buf('ss_Mt')
        xdt = sb('ss_xdt', [128, 1024], BF16)
        xdt2 = sb('ss_xdt2', [128, 1024], BF16)
        bxdt = Buf('ss_xdt')
        tmp = sb('ss_tmp', [128, 1024])
        btmp = Buf('ss_tmp')
        yac = [sb('ss_yac%d' % i, [128, 1024]) for i in range(2)]
        byac = [Buf('ss_yac%d' % i) for i in range(2)]
        zt = [sb('ss_z%d' % i, [128, 1024]) for i in range(2)]
        bzt = [Buf('ss_z%d' % i) for i in range(2)]
        dtr = [sb('ss_dtr%d' % i, [128, 4, 32]) for i in range(2)]
        bdtr = [Buf('ss_dtr%d' % i) for i in range(2)]
        ynb = sb('ss_ynb', [128, 1024], BF16)
        bynb = Buf('ss_ynb')
        yo = [sb('ss_yo%d' % i, [128, 8, TT], BF16) for i in range(2)]
        byo = [Buf('ss_yo%d' % i) for i in range(2)]
        Yv = C.Y.rearrange("(c p) n -> p c n", p=128)
        UFv = C.UF.rearrange("(c p) n -> p c n", p=128)
        bYS = C.bYS
        nt = N // TT
        it = 0
        ci = 0
        ti = 0
        for d in range(2):
            order = list(range(nt)) if d == 0 else list(range(nt - 1, -1, -1))
            for t in order:
                t0 = t * TT
                kl, kr = bkind(C, t0), bkind(C, t0 + TT)
                lo = 2 if kl == 'hard' else 0
                hi = 514 if kr == 'hard' else 516
                xi, bxi = xin[it % 2], bxin[it % 2]
                it += 1
                for (c0_, c1_) in ((0, 6), (6, 12)):
                    S.op('sp', lambda e_, xi=xi, c0_=c0_, c1_=c1_, t0=t0, lo=lo, hi=hi: e_.dma_start(
                        out=xi[:, c0_:c1_, lo:hi], in_=UFv[:, c0_:c1_, t0 - 2 + lo:t0 - 2 + hi]),
                        reads=[C.bUF], writes=[bxi], dma=bxi)
                if lo > 0:
                    S.op('pool', lambda e_: e_.memset(xbf[:, :, 0:2], 0.0), writes=[bxbf])
                if hi < 516:
                    S.op('pool', lambda e_: e_.memset(xbf[:, :, 514:516], 0.0), writes=[bxbf])
                S.op('act', lambda e_, xi=xi, lo=lo, hi=hi: e_.copy(xbf[:, :, lo:hi], xi[:, :, lo:hi]), reads=[bxi], writes=[bxbf])
                if kl == 'soft':
                    S.op('dve', lambda e_: e_.tensor_scalar(xbf[:, :, 0:2], xbf[:, :, 0:2], C.flagt[:, 0:1], None, ALU.mult),
                         reads=[bxbf, C.bconst], writes=[bxbf])
                if kr == 'soft':
                    S.op('dve', lambda e_: e_.tensor_scalar(xbf[:, :, 514:516], xbf[:, :, 514:516], C.flagt[:, 0:1], None, ALU.mult),
                         reads=[bxbf, C.bconst], writes=[bxbf])
                for c in range(12):
                    ps, bps = next_ps(C)
                    for k in range(5):
                        _mm(S, ps[:, :], dg[:, c, k, :], xbf[:, c, k:k + TT], k == 0, k == 4, [bset, bxbf], [bps])
                    if c < 8:
                        S.op('act', lambda e_, c=c, ps=ps: e_.activation(xcf[:, c, :], ps[:, :], AF.Silu, bias=cvb[:, c:c + 1], scale=1.0),
                             reads=[bps, bset], writes=[bxcf])
                    else:
                        S.op('act', lambda e_, c=c, ps=ps: e_.activation(bcf[:, c - 8, :], ps[:, :], AF.Silu, bias=cvb[:, c:c + 1], scale=1.0),
                             reads=[bps, bset], writes=[bbcf])
                for s4 in range(4):
                    for half in range(2):
                        ps, bps = next_ps(C)
                        for j in range(4):
                            c = half * 4 + j
                            _mm(S, ps[:, j * 128:(j + 1) * 128], xcf[:, c, s4 * 128:(s4 + 1) * 128], C.ident[:], True, True,
                                [bxcf, C.bconst], [bps])
                        evac(C, half, xT[:, s4, half * 512:(half + 1) * 512], ps[:, :], bps, bxT)
                    ps, bps = next_ps(C)
                    for g in range(2):
                        _mm(S, ps[:, g * 128:(g + 1) * 128], bcf[:, g, s4 * 128:(s4 + 1) * 128], C.identb[:], True, True,
                            [bbcf, C.bconst], [bps])
                    S.op('dve', lambda e_, s4=s4, ps=ps: e_.tensor_copy(BT[:, s4, :], ps[:, 0:256]), reads=[bps], writes=[bBT])
                smt, bsm = smts[ti % 2], bsms[ti % 2]
                dt_, bdt_ = dtr[ti % 2], bdtr[ti % 2]
                ti += 1
                dsl = slice(d * 16, d * 16 + 16)
                S.op('sp', lambda e_, dt_=dt_, t0=t0: e_.dma_start(
                    out=dt_[:], in_=C.UT[t0:t0 + TT, 1024:1056].rearrange("(s p) c -> p s c", p=128)),
                    reads=[C.bUT], writes=[bdt_], dma=bdt_)
                S.op('dve', lambda e_, dt_=dt_, dsl=dsl, smt=smt: e_.tensor_tensor(
                    smt[:, 0, :, :], dt_[:, :, dsl], dtb[:, dsl].unsqueeze(1).to_broadcast([128, 4, 16]), ALU.add),
                    reads=[bdt_, bset], writes=[bsm])
                S.op('act', lambda e_, smt=smt: e_.activation(smt[:, 1, :, :], smt[:, 0, :, :], AF.Abs), reads=[bsm], writes=[bsm])
                S.op('act', lambda e_, smt=smt: e_.activation(smt[:, 1, :, :], smt[:, 1, :, :], AF.Exp, scale=-1.0), reads=[bsm], writes=[bsm])
                S.op('act', lambda e_, smt=smt: e_.activation(smt[:, 1, :, :], smt[:, 1, :, :], AF.Ln, bias=C.onec[:, 0:1], scale=1.0),
                     reads=[bsm, C.bconst], writes=[bsm])
                S.op('dve', lambda e_, smt=smt: e_.scalar_tensor_tensor(smt[:, 2, :, :], smt[:, 0, :, :], 0.0, smt[:, 1, :, :], ALU.max, ALU.add),
                     reads=[bsm], writes=[bsm])
                S.op('dve', lambda e_, dsl=dsl, smt=smt: e_.tensor_tensor(
                    smt[:, 3, :, :], smt[:, 2, :, :], Ab[:, dsl].unsqueeze(1).to_broadcast([128, 4, 16]), ALU.mult),
                    reads=[bsm, bset], writes=[bsm])
                psc, bpsc = next_ps(C)
                for s4 in range(4):
                    _mm(S, psc[:, s4 * 32:s4 * 32 + 16], tri[:, d, :], smt[:, 3, s4, :], True, True, [bset, bsm], [bpsc])
                    _mm(S, psc[:, s4 * 32 + 16:s4 * 32 + 32], onesf[:], smt[:, 3, s4, :], True, True, [bset, bsm], [bpsc])
                pv = psc[:, 0:128].rearrange("p (s a h) -> p s a h", s=4, a=2)
                S.op('dve', lambda e_, pv=pv, smt=smt: e_.tensor_copy(smt[:, 4, :, :], pv[:, :, 0, :]), reads=[bpsc], writes=[bsm])
                S.op('dve', lambda e_, pv=pv, smt=smt: e_.tensor_scalar(smt[:, 5, :, :], pv[:, :, 0, :], -1.0, None, ALU.mult), reads=[bpsc], writes=[bsm])
                S.op('dve', lambda e_, pv=pv, smt=smt: e_.tensor_tensor(smt[:, 7, :, :], pv[:, :, 1, :], smt[:, 4, :, :], ALU.subtract),
                     reads=[bpsc, bsm], writes=[bsm])
                S.op('act', lambda e_, pv=pv, smt=smt: e_.activation(smt[:, 6, :, :], pv[:, :, 0, :], AF.Exp), reads=[bpsc], writes=[bsm])
                S.op('act', lambda e_, pv=pv, smt=smt: e_.activation(smt[:, 9, :, :], pv[:, :, 1, :], AF.Exp), reads=[bpsc], writes=[bsm])
                S.op('act', lambda e_, smt=smt: e_.activation(smt[:, 8, :, :], smt[:, 7, :, :], AF.Exp), reads=[bsm], writes=[bsm])
                S.op('dve', lambda e_, smt=smt: e_.tensor_tensor(smt[:, 8, :, :], smt[:, 8, :, :], smt[:, 2, :, :], ALU.mult), reads=[bsm], writes=[bsm])
                chunks = list(range(4)) if d == 0 else [3, 2, 1, 0]
                yo_, byo_ = yo[t % 2], byo[t % 2]
                for s4 in chunks:
                    c0 = t0 + s4 * 128
                    bpos = c0 if d == 0 else c0 + 128
                    bk = bkind(C, bpos)
                    if bk == 'hard':
                        S.op('pool', lambda e_: e_.memset(Sst[:], 0.0), writes=[bS])
                        S.op('pool', lambda e_: e_.memset(Sb[:], 0.0), writes=[bS])
                    elif bk == 'soft':
                        S.op('dve', lambda e_: e_.tensor_scalar(Sst[:], Sst[:], C.flagt[:, 0:1], None, ALU.mult),
                             reads=[bS, C.bconst], writes=[bS])
                        S.op('dve', lambda e_: e_.tensor_scalar(Sb[:], Sb[:], C.flagt[:, 0:1], None, ALU.mult),
                             reads=[bS, C.bconst], writes=[bS])
                    sm = smt[:, :, s4, :]
                    z_, bz_ = zt[ci % 2], bzt[ci % 2]
                    ya, bya = yac[ci % 2], byac[ci % 2]
                    ci += 1
                    S.op('dve', lambda e_, sm=sm, d=d: e_.tensor_tensor(
                        R[:], tri[:, d:d + 1, :].to_broadcast([128, 16, 128]),
                        sm[:, 3, :].unsqueeze(2).to_broadcast([128, 16, 128]), ALU.mult),
                        reads=[bset, bsm], writes=[bR])
                    pcb, bpcb = next_ps(C)
                    for g in range(2):
                        _mm(S, pcb[:, g * 128:(g + 1) * 128], bcf[:, g, s4 * 128:(s4 + 1) * 128],
                            bcf[:, 2 + g, s4 * 128:(s4 + 1) * 128], True, True, [bbcf], [bpcb])
                    S.op('act', lambda e_, sm=sm, pcb=pcb: e_.copy(cb[:].rearrange("p g l -> p (g l)"), pcb[:, 0:256]), reads=[bpcb], writes=[bcb])
                    for q4 in range(4):
                        ps, bps = next_ps(C)
                        _mm(S, ps[:, :], onesf[:], R[:, q4 * 4:(q4 + 1) * 4, :].rearrange("p h l -> p (h l)"), True, False, [bset, bR], [bps])
                        _mm(S, ps[:, :], C.identb[:], nm4[:, d, :, :].rearrange("p j l -> p (j l)"), False, True, [bset, C.bconst], [bps])
                        for j in range(4):
                            h = q4 * 4 + j
                            S.op('act', lambda e_, sm=sm, ps=ps, j=j, h=h: e_.activation(Dm[:, h, :], ps[:, j * 128:(j + 1) * 128], AF.Exp,
                                                                               bias=sm[:, 5, h:h + 1], scale=1.0),
                                 reads=[bps, bsm], writes=[bDm])
                    for g in range(2):
                        S.op('dve', lambda e_, sm=sm, g=g: e_.tensor_tensor(Mt[:, g * 8:(g + 1) * 8, :], Dm[:, g * 8:(g + 1) * 8, :],
                                                                   cb[:, g:g + 1, :].to_broadcast([128, 8, 128]), ALU.mult),
                             reads=[bDm, bcb], writes=[bMt])
                    xv = xT[:, s4, :].rearrange("p (h q) -> p h q", h=16)
                    S.op('dve', lambda e_, sm=sm, xv=xv: e_.tensor_tensor(xdt[:].rearrange("p (h q) -> p h q", h=16), xv,
                                                                 sm[:, 2, :].unsqueeze(2).to_broadcast([128, 16, 64]), ALU.mult),
                         reads=[bxT, bsm], writes=[bxdt])
                    S.op('dve', lambda e_, sm=sm, xv=xv: e_.tensor_tensor(xdt2[:].rearrange("p (h q) -> p h q", h=16), xv,
                                                                 sm[:, 8, :].unsqueeze(2).to_broadcast([128, 16, 64]), ALU.mult),
                         reads=[bxT, bsm], writes=[bxdt])
                    pof = []
                    for g in range(2):
                        ps, bps = next_ps(C)
                        _mm(S, ps[:, :], bcf[:, 2 + g, s4 * 128:(s4 + 1) * 128], Sb[:, g * 512:(g + 1) * 512], True, True, [bbcf, bS], [bps])
                        pof.append((ps, bps))
                    for g in range(2):
                        ps, bps = pof[g]
                        S.op('dve', lambda e_, sm=sm, g=g, ps=ps: e_.tensor_tensor(
                            tmp[:, g * 512:(g + 1) * 512].rearrange("p (h q) -> p h q", h=8),
                            ps[:, :].rearrange("p (h q) -> p h q", h=8),
                            sm[:, 6, g * 8:(g + 1) * 8].unsqueeze(2).to_broadcast([128, 8, 64]), ALU.mult),
                            reads=[bps, bsm], writes=[btmp])
                    pyd = []
                    for g in range(2):
                        ps, bps = next_ps(C)
                        for j in range(8):
                            h = g * 8 + j
                            _mm(S, ps[:, j * 64:(j + 1) * 64], Mt[:, h, :], xdt[:, h * 64:(h + 1) * 64], True, True, [bMt, bxdt], [bps])
                        pyd.append((ps, bps))
                    if d == 0:
                        for g in range(2):
                            ps, bps = pyd[g]
                            S.op('dve', lambda e_, sm=sm, g=g, ps=ps, ya=ya: e_.tensor_tensor(ya[:, g * 512:(g + 1) * 512], ps[:, :],
                                                                                   tmp[:, g * 512:(g + 1) * 512], ALU.add),
                                 reads=[bps, btmp], writes=[bya])
                        S.op('dve', lambda e_, sm=sm, xv=xv: e_.tensor_tensor(tmp[:].rearrange("p (h q) -> p h q", h=16), xv,
                                                                     dsk[:, :].unsqueeze(2).to_broadcast([128, 16, 64]), ALU.mult),
                             reads=[bxT, bset], writes=[btmp])
                        S.op('dve', lambda e_, sm=sm, ya=ya: e_.tensor_tensor(ya[:], ya[:], tmp[:], ALU.add), reads=[bya, btmp], writes=[bya])
                        S.op('pool', lambda e_, sm=sm, ya=ya, c0=c0: e_.dma_start(out=C.YS[c0:c0 + 128, :], in_=ya[:]),
                             reads=[bya], writes=[bYS], dma=bya)
                    else:
                        S.op('sp', lambda e_, sm=sm, ya=ya, c0=c0: e_.dma_start(out=ya[:], in_=C.YS[c0:c0 + 128, :]),
                             reads=[bYS], writes=[bya], dma=bya)
                        S.op('sp', lambda e_, sm=sm, z_=z_, c0=c0: e_.dma_start(out=z_[:], in_=C.UT[c0:c0 + 128, 0:1024]),
                             reads=[C.bUT], writes=[bz_], dma=bz_)
                        S.op('dve', lambda e_, sm=sm, ya=ya: e_.tensor_tensor(ya[:], ya[:], tmp[:], ALU.add), reads=[bya, btmp], writes=[bya])
                        for g in range(2):
                            ps, bps = pyd[g]
                            S.op('dve', lambda e_, sm=sm, g=g, ps=ps, ya=ya: e_.tensor_tensor(ya[:, g * 512:(g + 1) * 512], ps[:, :],
                                                                                   ya[:, g * 512:(g + 1) * 512], ALU.add),
                                 reads=[bps, bya], writes=[bya])
                        S.op('act', lambda e_, sm=sm, z_=z_: e_.activation(z_[:], z_[:], AF.Silu), reads=[bz_], writes=[bz_])
                        S.op('dve', lambda e_, sm=sm, ya=ya, z_=z_: e_.tensor_tensor(ya[:], ya[:], z_[:], ALU.mult), reads=[bya, bz_], writes=[bya])
                        S.op('act', lambda e_, sm=sm, ya=ya: e_.activation(tmp[:], ya[:], AF.Square), reads=[bya], writes=[btmp])
                        S.op('dve', lambda e_, sm=sm: e_.reduce_sum(sm[:, 10, 0:2], tmp[:].rearrange("p (g f) -> p g f", g=2), AX.X),
                             reads=[btmp], writes=[bsm])
                        S.op('act', lambda e_, sm=sm: e_.activation(sm[:, 10, 0:2], sm[:, 10, 0:2], AF.Sqrt, bias=C.epsD[:, 1:2], scale=1.0 / 512),
                             reads=[bsm, C.bconst], writes=[bsm])
                        S.op('dve', lambda e_, sm=sm: e_.reciprocal(sm[:, 10, 0:2], sm[:, 10, 0:2]), reads=[bsm], writes=[bsm])
                        S.op('dve', lambda e_, sm=sm, ya=ya: e_.tensor_tensor(ynb[:].rearrange("p (g f) -> p g f", g=2),
                                                                     ya[:].rearrange("p (g f) -> p g f", g=2),
                                                                     sm[:, 10, 0:2].unsqueeze(2).to_broadcast([128, 2, 512]), ALU.mult),
                             reads=[bya, bsm], writes=[bynb])
                        for half in range(2):
                            ps, bps = next_ps(C)
                            for j in range(4):
                                c = half * 4 + j
                                _mm(S, ps[:, j * 128:(j + 1) * 128], ynb[:, c * 128:(c + 1) * 128], C.identb[:], True, True,
                                    [bynb, C.bconst], [bps])
                            S.op('dve', lambda e_, sm=sm, half=half, ps=ps, yo_=yo_, s4=s4: e_.tensor_tensor(
                                yo_[:, half * 4:(half + 1) * 4, s4 * 128:(s4 + 1) * 128],
                                ps[:, :].rearrange("p (j l) -> p j l", j=4),
                                nrw[:, half * 4:(half + 1) * 4].unsqueeze(2).to_broadcast([128, 4, 128]), ALU.mult),
                                reads=[bps, bset], writes=[byo_])
                    for g in range(2):
                        ps, bps = next_ps(C)
                        _mm(S, ps[:, :], BT[:, s4, g * 128:(g + 1) * 128], xdt2[:, g * 512:(g + 1) * 512], True, True, [bBT, bxdt], [bps])
                        S.op('dve', lambda e_, sm=sm, g=g: e_.tensor_tensor(
                            Sst[:, g * 512:(g + 1) * 512].rearrange("p (h q) -> p h q", h=8),
                            Sst[:, g * 512:(g + 1) * 512].rearrange("p (h q) -> p h q", h=8),
                            sm[:, 9, g * 8:(g + 1) * 8].unsqueeze(2).to_broadcast([128, 8, 64]), ALU.mult),
                            reads=[bS, bsm], writes=[bS])
                        S.op('dve', lambda e_, sm=sm, g=g, ps=ps: e_.tensor_tensor(Sst[:, g * 512:(g + 1) * 512], Sst[:, g * 512:(g + 1) * 512],
                                                                         ps[:, :], ALU.add),
                             reads=[bS, bps], writes=[bS])
                    S.op('act', lambda e_, sm=sm: e_.copy(Sb[:], Sst[:]), reads=[bS], writes=[bS])
                if d == 1:
                    S.op('pool', lambda e_, sm=sm, yo_=yo_, t0=t0: e_.dma_start(out=Yv[:, 0:8, t0:t0 + TT], in_=yo_[:]),
                         reads=[byo_], writes=[C.bY], dma=byo_)
        S.end_phase()


def phase_rwkv(C, e):
    nc, S, N, W = C.nc, C.S, C.N, C.W
    TT = 512
    KD = 0.6065306597126334
    stage = C.cfg.get('rw_stage', 99)
    with ExitStack() as st:
        sb = lambda name, shape, dt=F32: st.enter_context(_sbt(nc, name, list(shape), dt))
        bset = Buf('rw_setup')
        mu = sb('rw_mu', [128, 26])
        load_cols(C, mu[:], bset, W['rwkv_mu'][e], 26)
        hm = sb('rw_hm', [128, 2, 26])
        S.op('dve', lambda e_: e_.tensor_scalar(hm[:, 0, :], mu[:], 0.5, None, ALU.mult), reads=[bset], writes=[bset])
        S.op('dve', lambda e_: e_.tensor_scalar(hm[:, 1, :], mu[:], -1.0, 1.0, ALU.mult, ALU.add), reads=[bset], writes=[bset])
        dgs = sb('rw_dgs', [128, 26, 2, 128], BF16)
        for c in range(26):
            for k in range(2):
                S.op('dve', lambda e_, c=c, k=k: e_.tensor_scalar(dgs[:, c, k, :], C.identb[:], hm[:, k, c:c + 1], None, ALU.mult),
                     reads=[bset, C.bconst], writes=[bset])
        w2b = sb('rw_w2b', [128, 2, 1024], BF16)
        a2p = sb('rw_a2p', [128, 1024], BF16)
        S.op('pool', lambda e_: e_.dma_start(out=a2p[64:128, :], in_=W['rwkv_a2'][e]), writes=[bset], dma=Buf('x'))
        g2b = sb('rw_g2b', [128, 1024], BF16)
        S.op('pool', lambda e_: e_.dma_start(out=g2b[:, :], in_=W['rwkv_g2'][e]), writes=[bset], dma=Buf('x'))
        cols = sb('rw_cols', [128, 7, 8])
        load_cols(C, cols[:, 0, :], bset, W['rwkv_a0'][e], 8)
        load_cols(C, cols[:, 1, :], bset, W['rwkv_k_k'][e], 8)
        load_cols(C, cols[:, 2, :], bset, W['rwkv_k_a'][e], 8)
        load_cols(C, cols[:, 4, :], bset, W['rwkv_r_k'][e].rearrange("a b -> (a b)"), 8)
        load_cols(C, cols[:, 5, :], bset, W['rwkv_ln_w'][e], 8)
        load_cols(C, cols[:, 6, :], bset, W['rwkv_ln_b'][e], 8)
        S.op('dve', lambda e_: e_.tensor_scalar(cols[:, 3, :], cols[:, 2, :], -1.0, None, ALU.mult), reads=[bset], writes=[bset])
        tri = sb('rw_tri', [128, 2, 128])
        S.op('sp', lambda e_: e_.dma_start(out=tri[:], in_=C.c_tri), writes=[bset], dma=Buf('x'))
        tri2 = sb('rw_tri2', [128, 2, 128])
        S.op('sp', lambda e_: e_.dma_start(out=tri2[:], in_=C.c_tri2), writes=[bset], dma=Buf('x'))
        m4 = sb('rw_m4', [128, 2, 4, 128], BF16)
        for d in range(2):
            for j in range(4):
                srcm = tri2 if j % 2 == 0 else tri
                S.op('dve', lambda e_, d=d, j=j, srcm=srcm: e_.tensor_copy(m4[:, d, j, :], srcm[:, d, :]), reads=[bset], writes=[bset])
        bd = sb('rw_bd', [128, 128])
        S.op('sp', lambda e_: e_.dma_start(out=bd[:], in_=C.c_bd), writes=[bset], dma=Buf('x'))

        st0 = ExitStack()
        w0f = st0.enter_context(_sbt(nc, 'rw_w0f', [128, 2, 2, 1024], F32))
        for d in range(2):
            S.op('pool', lambda e_, d=d: e_.dma_start(out=w2b[0:64, d, :], in_=W['rwkv_w2'][e][d]), writes=[bset], dma=Buf('x'))
            S.op('sp', lambda e_, d=d: e_.dma_start(out=w0f[64:65, d, 0, :], in_=W['rwkv_w0'][e][d:d + 1, :]), writes=[bset], dma=Buf('x'))
            S.op('sp', lambda e_, d=d: e_.dma_start(out=w0f[65:66, d, 0, :], in_=W['rwkv_w0'][e][d:d + 1, :]), writes=[bset], dma=Buf('x'))
        tmpb = st0.enter_context(_sbt(nc, 'rw_tmpb', [128, 2, 1024], BF16))
        S.op('dve', lambda e_: e_.tensor_copy(tmpb[64:66, :, :], w0f[64:66, :, 0, :]), reads=[bset], writes=[bset])
        S.op('dve', lambda e_: e_.tensor_copy(w2b[64:66, :, :], tmpb[64:66, :, :]), reads=[bset], writes=[bset])
        S.op('dve', lambda e_: e_.tensor_copy(w0f[64:66, :, 1, :], tmpb[64:66, :, :]), reads=[bset], writes=[bset])
        S.op('dve', lambda e_: e_.tensor_tensor(w0f[64:66, :, 0, :], w0f[64:66, :, 0, :], w0f[64:66, :, 1, :], ALU.subtract),
             reads=[bset], writes=[bset])
        S.op('dve', lambda e_: e_.tensor_copy(tmpb[64:66, :, :], w0f[64:66, :, 0, :]), reads=[bset], writes=[bset])
        S.op('sp', lambda e_: e_.dma_start(out=w2b[65:66, :, :], in_=tmpb[65:66, :, :]), reads=[bset], writes=[bset], dma=Buf('x'))
        S.barrier()
        S.flush()
        st0.close()
        pxr = [sb('rw_px%d' % i, [128, 514]) for i in range(2)]
        bpxr = [Buf('rw_px%d' % i) for i in range(2)]
        pbb = [sb('rw_pb%d' % i, [128, 514], BF16) for i in range(2)]
        bpbb = [Buf('rw_pb%d' % i) for i in range(2)]
        tcw = sb('rw_tcw', [128, TT], BF16)
        btcw = Buf('rw_tcw')
        S.op('pool', lambda e_: e_.memset(tcw[64:66, :], 1.0), writes=[btcw])
        cab = sb('rw_cab', [128, TT], BF16)
        bcab = Buf('rw_cab')
        scg = sb('rw_scg', [128, TT], BF16)
        bscg = Buf('rw_scg')
        tmpn = ['al', 'kf', 'kkr', 'sq', 'kk32', 't1', 'bon']
        tm = {n: sb('rw_t_' + n, [128, TT]) for n in tmpn}
        btm = {n: Buf('rw_t_' + n) for n in tmpn}
        nkT = sb('rw_nkT', [128, 8, TT], BF16)
        bT = sb('rw_bT', [128, 8, TT], BF16)
        kmT = sb('rw_kmT', [128, 8, TT], BF16)
        rTb = sb('rw_rTb', [128, 8, TT], BF16)
        vTb = sb('rw_vTb', [128, 8, TT], BF16)
        bvT = sb('rw_bv', [128, 8, TT], BF16)
        gT = sb('rw_gT', [128, 8, TT], BF16)
        bprep = Buf('rw_prep')
        Vt = sb('rw_Vt', [128, 1024], BF16)
        bVt = Buf('rw_Vt')
        sig = sb('rw_sig', [128, 1024])
        bsig = Buf('rw_sig')
        eLi = sb('rw_eLi', [128, 8, 128])
        eLn = sb('rw_eLn', [128, 8, 128])
        eLx = sb('rw_eLx', [128, 8, 128])
        beL = Buf('rw_eL')
        AR = sb('rw_AR', [128, 8, 2, 128], BF16)
        BK = sb('rw_BK', [128, 2, 8, 128], BF16)
        bARBK = Buf('rw_ARBK')
        bkTp = sb('rw_bkTp', [128, 2, 2, 8, 128], BF16)
        bbkT = Buf('rw_bkT')
        S.op('pool', lambda e_: e_.memset(bkTp[:], 0.0), writes=[bbkT])
        AB4 = sb('rw_AB4', [128, 16, 512], BF16)
        bAB4 = Buf('rw_AB4')
        Xm = sb('rw_X', [128, 16, 128], BF16)
        bXm = [Buf('rw_X%d' % i) for i in range(4)]
        Pp = [sb('rw_P%d' % i, [128, 2, 16, 128], BF16) for i in range(2)]
        bPp = [[Buf('rw_P%d_%d' % (i, g)) for g in range(4)] for i in range(2)]
        Gs = sb('rw_Gs', [128, 1024], BF16)
        bGs = Buf('rw_Gs')
        Us = sb('rw_Us', [128, 1024], BF16)
        bUs = Buf('rw_Us')
        Sst = sb('rw_S', [128, 8, 64])
        Sbb = sb('rw_Sb', [128, 8, 64], BF16)
        bS = Buf('rw_S')
        ya = sb('rw_ya', [128, 1024])
        bya = Buf('rw_ya')
        yfl = sb('rw_yf', [128, 1024])
        byfl = Buf('rw_yf')
        stt = sb('rw_st', [128, 2, 16])
        bstt = Buf('rw_st')
        ynb = sb('rw_ynb', [128, 1024], BF16)
        bynb = Buf('rw_ynb')
        fin = sb('rw_fin', [128, 8, 128])
        bfin = Buf('rw_fin')
        yo = sb('rw_yo', [128, 8, TT], BF16)
        byo = Buf('rw_yo')
        Yv = C.Y.rearrange("(c p) n -> p c n", p=128)
        UFv = C.UF.rearrange("(c p) n -> p c n", p=128)
        nt = N // TT
        cnt = {'px': 0, 'pb': 0, 'ev': 0}

        def conv_chunk(ch, t0, kl, kr, lo, hi):
            px, bpx = pxr[cnt['px'] % 2], bpxr[cnt['px'] % 2]
            cnt['px'] += 1
            pb, bpb = pbb[cnt['pb'] % 2], bpbb[cnt['pb'] % 2]
            cnt['pb'] += 1
            S.op('sp', lambda e_: e_.dma_start(out=px[:, lo:hi], in_=UFv[:, 12 + ch, t0 - 1 + lo:t0 - 1 + hi]),
                 reads=[C.bUF], writes=[bpx], dma=bpx)
            if lo > 0:
                S.op('pool', lambda e_: e_.memset(pb[:, 0:1], 0.0), writes=[bpb])
            if hi < 514:
                S.op('pool', lambda e_: e_.memset(pb[:, 513:514], 0.0), writes=[bpb])
            S.op('pool', lambda e_: e_.tensor_copy(pb[:, lo:hi], px[:, lo:hi]), reads=[bpx], writes=[bpb])
            if kl == 'soft':
                S.op('pool', lambda e_: e_.tensor_scalar(pb[:, 0:1], pb[:, 0:1], C.flagt[:, 0:1], None, ALU.mult),
                     reads=[bpb, C.bconst], writes=[bpb])
            if kr == 'soft':
                S.op('pool', lambda e_: e_.tensor_scalar(pb[:, 513:514], pb[:, 513:514], C.flagt[:, 0:1], None, ALU.mult),
                     reads=[bpb, C.bconst], writes=[bpb])
            ps, bps = next_ps(C)
            _mm(S, ps[:, :], dgs[:, ch, 0, :], pb[:, 0:TT], True, False, [bset, bpb], [bps])
            _mm(S, ps[:, :], dgs[:, ch, 1, :], pb[:, 1:TT + 1], False, False, [bset, bpb], [bps])
            _mm(S, ps[:, :], dgs[:, ch, 0, :], pb[:, 2:TT + 2], False, True, [bset, bpb], [bps])
            return ps, bps

        for d in range(2):
            if stage < 1:
                break
            order = list(range(nt)) if d == 0 else list(range(nt - 1, -1, -1))
            last = 127 if d == 0 else 0
            for t in order:
                t0 = t * TT
                kl, kr = bkind(C, t0), bkind(C, t0 + TT)
                lo = 1 if kl == 'hard' else 0
                hi = 513 if kr == 'hard' else 514
                ps, bps = conv_chunk(24, t0, kl, kr, lo, hi)
                S.op('act', lambda e_, ps=ps: e_.activation(tcw[0:64, :], ps[0:64, :], AF.Tanh), reads=[bps], writes=[btcw])
                S.op('dve', lambda e_, ps=ps: e_.tensor_copy(cab[64:128, :], ps[64:128, :]), reads=[bps], writes=[bcab])
                if d == 1:
                    ps, bps = conv_chunk(25, t0, kl, kr, lo, hi)
                    S.op('act', lambda e_, ps=ps: e_.activation(scg[:], ps[:, :], AF.Sigmoid), reads=[bps], writes=[bscg])
                    for c in range(8):
                        ps, bps = next_ps(C)
                        _mm(S, ps[:, :], g2b[:, c * 128:(c + 1) * 128], scg[:], True, True, [bset, bscg], [bps])
                        evac(C, c, gT[:, c, :], ps[:, :], bps, bprep)
                for c in range(8):
                    if stage < 1.2:
                        break
                    ps, bps = next_ps(C)
                    _mm(S, ps[:, :], a2p[64:128, c * 128:(c + 1) * 128], cab[64:128, :], True, True, [bset, bcab], [bps])
                    S.op('act', lambda e_, c=c, ps=ps: e_.activation(tm['al'][:], ps[:, :], AF.Sigmoid, bias=cols[:, 0, c:c + 1], scale=1.0),
                         reads=[bps, bset], writes=[btm['al']])
                    if stage < 1.3:
                        continue
                    ps, bps = conv_chunk(8 + c, t0, kl, kr, lo, hi)
                    S.op('act', lambda e_, ps=ps: e_.copy(tm['kf'][:], ps[:, :]), reads=[bps], writes=[btm['kf']])
                    S.op('dve', lambda e_, c=c, ps=ps: e_.tensor_scalar(tm['kkr'][:], ps[:, :], cols[:, 1, c:c + 1], None, ALU.mult),
                         reads=[bps, bset], writes=[btm['kkr']])
                    if stage < 1.31:
                        continue
                    S.op('act', lambda e_: e_.activation(tm['sq'][:], tm['kkr'][:], AF.Square), reads=[btm['kkr']], writes=[btm['sq']])
                    ps2, bps2 = next_ps(C)
                    _mm(S, ps2[:, :], bd[:], tm['sq'][:], True, True, [bset, btm['sq']], [bps2])
                    S.op('act', lambda e_, ps2=ps2: e_.activation(tm['sq'][:], ps2[:, :], AF.Sqrt), reads=[bps2], writes=[btm['sq']])
                    if stage < 1.32:
                        continue
                    S.op('dve', lambda e_: e_.tensor_scalar(tm['sq'][:], tm['sq'][:], 1e-12, None, ALU.max), reads=[btm['sq']], writes=[btm['sq']])
                    S.op('dve', lambda e_: e_.reciprocal(tm['sq'][:], tm['sq'][:]), reads=[btm['sq']], writes=[btm['sq']])
                    S.op('dve', lambda e_: e_.tensor_tensor(tm['kk32'][:], tm['kkr'][:], tm['sq'][:], ALU.mult),
                         reads=[btm['kkr'], btm['sq']], writes=[btm['kk32']])
                    if stage < 1.33:
                        continue
                    S.op('act', lambda e_, c=c: e_.activation(nkT[:, c, :], tm['kk32'][:], AF.Identity, scale=-1.0), reads=[btm['kk32']], writes=[bprep])
                    S.op('dve', lambda e_, c=c: e_.tensor_tensor(bT[:, c, :], tm['kk32'][:], tm['al'][:], ALU.mult),
                         reads=[btm['kk32'], btm['al']], writes=[bprep])
                    if stage < 1.34:
                        continue
                    S.op('dve', lambda e_: e_.tensor_scalar(tm['t1'][:], tm['al'][:], -1.0, None, ALU.add),
                         reads=[btm['al']], writes=[btm['t1']])
                    S.op('dve', lambda e_, c=c: e_.tensor_scalar(tm['t1'][:], tm['t1'][:], cols[:, 2, c:c + 1], None, ALU.mult),
                         reads=[btm['t1'], bset], writes=[btm['t1']])
                    S.op('dve', lambda e_: e_.scalar_tensor_tensor(tm['t1'][:], tm['t1'][:], 1.0, tm['kf'][:], ALU.add, ALU.mult),
                         reads=[btm['t1'], btm['kf']], writes=[btm['t1']])
                    S.op('act', lambda e_, c=c: e_.copy(kmT[:, c, :], tm['t1'][:]), reads=[btm['t1']], writes=[bprep])
                    if stage < 1.4:
                        continue
                    ps, bps = conv_chunk(c, t0, kl, kr, lo, hi)
                    S.op('act', lambda e_, c=c, ps=ps: e_.copy(rTb[:, c, :], ps[:, :]), reads=[bps], writes=[bprep])
                    if d == 1:
                        S.op('dve', lambda e_, c=c, ps=ps: e_.scalar_tensor_tensor(tm['kk32'][:], ps[:, :], cols[:, 4, c:c + 1], tm['t1'][:],
                                                                                ALU.mult, ALU.mult),
                             reads=[bps, bset, btm['t1']], writes=[btm['kk32']])
                        ps3, bps3 = next_ps(C)
                        _mm(S, ps3[:, :], bd[:], tm['kk32'][:], True, True, [bset, btm['kk32']], [bps3])
                        S.op('act', lambda e_, ps3=ps3: e_.copy(tm['bon'][:], ps3[:, :]), reads=[bps3], writes=[btm['bon']])
                    if stage < 1.5:
                        continue
                    ps, bps = conv_chunk(16 + c, t0, kl, kr, lo, hi)
                    S.op('act', lambda e_, c=c, ps=ps: e_.copy(vTb[:, c, :], ps[:, :]), reads=[bps], writes=[bprep])
                    if d == 1:
                        S.op('dve', lambda e_, c=c, ps=ps: e_.tensor_tensor(bvT[:, c, :], ps[:, :], tm['bon'][:], ALU.mult),
                             reads=[bps, btm['bon']], writes=[bprep])
                chunks = list(range(4)) if d == 0 else [3, 2, 1, 0]
                if stage < 2:
                    chunks = []
                for s4 in chunks:
                    c0 = t0 + s4 * 128
                    ts = slice(s4 * 128, (s4 + 1) * 128)
                    bpos = c0 if d == 0 else c0 + 128
                    bk = bkind(C, bpos)
                    if bk == 'hard':
                        S.op('pool', lambda e_: e_.memset(Sst[:], 0.0), writes=[bS])
                        S.op('pool', lambda e_: e_.memset(Sbb[:], 0.0), writes=[bS])
                    elif bk == 'soft':
                        S.op('dve', lambda e_: e_.tensor_scalar(Sst[:], Sst[:], C.flagt[:, 0:1], None, ALU.mult),
                             reads=[bS, C.bconst], writes=[bS])
                        S.op('dve', lambda e_: e_.tensor_scalar(Sbb[:], Sbb[:], C.flagt[:, 0:1], None, ALU.mult),
                             reads=[bS, C.bconst], writes=[bS])
                    if d == 1:
                        S.op('sp', lambda e_, c0=c0: e_.dma_start(out=yfl[:], in_=C.YS[c0:c0 + 128, :]),
                             reads=[C.bYS], writes=[byfl], dma=byfl)
                    for half in range(2):
                        ps, bps = next_ps(C)
                        for j in range(4):
                            c = half * 4 + j
                            _mm(S, ps[:, j * 128:(j + 1) * 128], vTb[:, c, ts], C.identb[:], True, True, [bprep, C.bconst], [bps])
                        evac(C, half, Vt[:, half * 512:(half + 1) * 512], ps[:, :], bps, bVt)
                    for half in range(2):
                        ps, bps = next_ps(C)
                        _mm(S, ps[:, :], tcw[0:66, ts], w2b[0:66, d, half * 512:(half + 1) * 512], True, True, [btcw, bset], [bps])
                        S.op('act', lambda e_, half=half, ps=ps: e_.activation(sig[:, half * 512:(half + 1) * 512], ps[:, :], AF.Sigmoid),
                             reads=[bps], writes=[bsig])
                    if stage < 3:
                        continue
                    pli = []
                    for half in range(2):
                        ps, bps = next_ps(C)
                        for j in range(4):
                            c = half * 4 + j
                            _mm(S, ps[:, j * 128:(j + 1) * 128], sig[:, c * 128:(c + 1) * 128], tri[:, d, :], True, True, [bsig, bset], [bps])
                        pli.append((ps, bps))
                    plx = []
                    for half in range(2):
                        ps, bps = next_ps(C)
                        for j in range(4):
                            c = half * 4 + j
                            _mm(S, ps[:, j * 128:(j + 1) * 128], sig[:, c * 128:(c + 1) * 128], tri2[:, d, :], True, True, [bsig, bset], [bps])
                        plx.append((ps, bps))
                    for half in range(2):
                        ps, bps = pli[half]
                        dst = lambda tl: tl[:, half * 4:(half + 1) * 4, :].rearrange("p c l -> p (c l)")
                        S.op('act', lambda e_, ps=ps, o=dst(eLi): e_.activation(o, ps[:, :], AF.Exp, scale=-KD), reads=[bps], writes=[beL])
                        S.op('act', lambda e_, ps=ps, o=dst(eLn): e_.activation(o, ps[:, :], AF.Exp, scale=KD), reads=[bps], writes=[beL])
                        ps, bps = plx[half]
                        S.op('act', lambda e_, ps=ps, o=dst(eLx): e_.activation(o, ps[:, :], AF.Exp, scale=-KD), reads=[bps], writes=[beL])
                    if stage < 4:
                        continue
                    S.op('dve', lambda e_, ts=ts: e_.tensor_tensor(AR[:, :, 0, :], nkT[:, :, ts], eLx[:], ALU.mult), reads=[bprep, beL], writes=[bARBK])
                    S.op('dve', lambda e_, ts=ts: e_.tensor_tensor(AR[:, :, 1, :], rTb[:, :, ts], eLi[:], ALU.mult), reads=[bprep, beL], writes=[bARBK])
                    S.op('pool', lambda e_, ts=ts: e_.tensor_tensor(BK[:, 0, :, :], bT[:, :, ts], eLn[:], ALU.mult), reads=[bprep, beL], writes=[bARBK])
                    S.op('pool', lambda e_, ts=ts: e_.tensor_tensor(BK[:, 1, :, :], kmT[:, :, ts], eLn[:], ALU.mult), reads=[bprep, beL], writes=[bARBK])
                    if stage < 5:
                        continue
                    for q in range(2):
                        for half in range(2):
                            ps, bps = next_ps(C)
                            for j in range(4):
                                c = half * 4 + j
                                _mm(S, ps[:, j * 128:(j + 1) * 128], BK[:, q, c, :], C.identb[:], True, True, [bARBK, C.bconst], [bps])
                            pv = ps[:, :].rearrange("p (c a j) -> p c a j", c=4, a=2)
                            S.op('act', lambda e_, q=q, half=half, pv=pv: e_.copy(bkTp[:, 0, q, half * 4:(half + 1) * 4, 0:64], pv[:, :, 0, :]),
                                 reads=[bps], writes=[bbkT])
                            S.op('dve', lambda e_, q=q, half=half, pv=pv: e_.tensor_copy(bkTp[:, 1, q, half * 4:(half + 1) * 4, 64:128], pv[:, :, 1, :]),
                                 reads=[bps], writes=[bbkT])
                    if stage < 5.2:
                        continue
                    for h in range(16):
                        c, pb = h // 2, (h % 2) * 64
                        sl_ = (h % 2) * 8 + h // 2
                        ps, bps = next_ps(C)
                        arv = AR[pb:pb + 64, c, :, :].rearrange("p a l -> p (a l)")
                        _mm(S, ps[:, 0:256], BK[pb:pb + 64, 0, c, :], arv, True, True, [bARBK], [bps])
                        _mm(S, ps[:, 256:512], BK[pb:pb + 64, 1, c, :], arv, True, True, [bARBK], [bps])
                        mv = m4[:, d, :, :].rearrange("p a l -> p (a l)")
                        if h % 2 == 0:
                            S.op('dve', lambda e_, h=sl_, ps=ps, mv=mv: e_.tensor_tensor(AB4[:, h, :], ps[:, :], mv, ALU.mult),
                                 reads=[bps, bset], writes=[bAB4])
                        else:
                            S.op('act', lambda e_, h=sl_, ps=ps: e_.copy(AB4[:, h, :], ps[:, :]), reads=[bps], writes=[bAB4])
                            S.op('pool', lambda e_, h=sl_, mv=mv: e_.tensor_tensor(AB4[:, h, :], AB4[:, h, :], mv, ALU.mult),
                                 reads=[bAB4, bset], writes=[bAB4])
                    if stage < 5.5:
                        continue
                    P0, bP0 = Pp[0], bPp[0]
                    for hg in range(4):
                        ps, bps = next_ps(C)
                        for j in range(4):
                            slot = hg * 4 + j
                            c, pb = slot % 8, (slot // 8) * 64
                            _mm(S, ps[:, j * 128:(j + 1) * 128], AR[pb:pb + 64, c, 0, :], BK[pb:pb + 64, 0, c, :], True, True, [bARBK], [bps])
                        S.op('dve', lambda e_, hg=hg, ps=ps, d=d: e_.tensor_tensor(
                            P0[:, 1, hg * 4:(hg + 1) * 4, :], ps[:, :].rearrange("p (j l) -> p j l", j=4),
                            tri2[:, 1 - d:2 - d, :].to_broadcast([128, 4, 128]), ALU.mult),
                            reads=[bps, bset], writes=[bP0[hg]])
                        if stage < 5.6:
                            continue
                        S.op('pool', lambda e_, hg=hg: e_.tensor_copy(P0[:, 0, hg * 4:(hg + 1) * 4, :], AB4[:, hg * 4:(hg + 1) * 4, 0:128]),
                             reads=[bAB4], writes=[bP0[hg]])
                        if stage < 5.7:
                            continue
                        S.op('pool', lambda e_, hg=hg: e_.tensor_tensor(Xm[:, hg * 4:(hg + 1) * 4, :], AB4[:, hg * 4:(hg + 1) * 4, 0:128],
                                                                      C.identb[:, :].unsqueeze(1).to_broadcast([128, 4, 128]), ALU.add),
                             reads=[bAB4, C.bconst], writes=[bXm[hg]])
                    if stage < 7:
                        continue
                    dbg_now = C.cfg.get('dbg') and d == 0 and t == 0 and s4 == 0
                    def dump2(idx, tile_ap, n, rd):
                        S.op('pool', lambda e_: e_.dma_start(out=C.DBG[idx, :, 0:n], in_=tile_ap), reads=rd, writes=[Buf('dbg2')], dma=Buf('x'))
                    if dbg_now:
                        dump2(13, Pp[0][:].rearrange("p a b c -> p (a b c)"), 4096, bPp[0] + bXm)
                        dump2(14, Xm[:].rearrange("p a b -> p (a b)"), 2048, bXm)
                    for k in range(1, 7):
                        if dbg_now and k == 2:
                            dump2(15, Pp[1][:].rearrange("p a b c -> p (a b c)"), 4096, bPp[1] + bXm)
                            dump2(16, Xm[:].rearrange("p a b -> p (a b)"), 2048, bXm)
                        Pa, bPa = Pp[(k - 1) % 2], bPp[(k - 1) % 2]
                        Pn, bPn = Pp[k % 2], bPp[k % 2]
                        for hg in range(4):
                            if k < 6:
                                ps, bps = next_ps(C)
                                for j in range(4):
                                    h = hg * 4 + j
                                    _mm(S, ps[:, j * 128:(j + 1) * 128], Pa[:, 1, h, :], Pa[:, 0, h, :], True, True, [bPa[hg]], [bps])
                                S.op('act', lambda e_, hg=hg, ps=ps, Pn=Pn: e_.copy(
                                    Pn[:, 0, hg * 4:(hg + 1) * 4, :].rearrange("p j l -> p (j l)"), ps[:, :]), reads=[bps], writes=[bPn[hg]])
                            ps, bps = next_ps(C)
                            for j in range(4):
                                h = hg * 4 + j
                                _mm(S, ps[:, j * 128:(j + 1) * 128], Pa[:, 0, h, :], Pa[:, 1, h, :], True, True, [bPa[hg]], [bps])
                            S.op('act', lambda e_, hg=hg, ps=ps, Pn=Pn: e_.copy(
                                Pn[:, 1, hg * 4:(hg + 1) * 4, :].rearrange("p j l -> p (j l)"), ps[:, :]), reads=[bps], writes=[bPn[hg]])
                        for hg in range(4):
                            ps, bps = next_ps(C)
                            for j in range(4):
                                h = hg * 4 + j
                                _mm(S, ps[:, j * 128:(j + 1) * 128], Pn[:, 1, h, :], Xm[:, h, :], True, True, [bPn[hg], bXm[hg]], [bps])
                            S.op('dve', lambda e_, hg=hg, ps=ps: e_.tensor_tensor(
                                Xm[:, hg * 4:(hg + 1) * 4, :].rearrange("p j l -> p (j l)"),
                                Xm[:, hg * 4:(hg + 1) * 4, :].rearrange("p j l -> p (j l)"), ps[:, :], ALU.add),
                                reads=[bps, bXm[hg]], writes=[bXm[hg]])
                    if stage < 8:
                        continue
                    for half in range(2):
                        ps, bps = next_ps(C)
                        for j in range(8):
                            slot, h, c, pb = half * 8 + j, 2 * j + half, j, half * 64
                            _mm(S, ps[:, j * 64:(j + 1) * 64], AR[pb:pb + 64, c, 0, :], Sbb[pb:pb + 64, c, :], True, False, [bARBK, bS], [bps])
                            _mm(S, ps[:, j * 64:(j + 1) * 64], AB4[:, slot, 256:384], Vt[:, h * 64:(h + 1) * 64], False, True, [bAB4, bVt], [bps])
                        evac(C, half, Gs[:, half * 512:(half + 1) * 512], ps[:, :], bps, bGs)
                    for half in range(2):
                        ps, bps = next_ps(C)
                        for j in range(8):
                            slot = half * 8 + j
                            _mm(S, ps[:, j * 64:(j + 1) * 64], Xm[:, slot, :], Gs[:, slot * 64:(slot + 1) * 64], True, True, [bXm[slot // 4], bGs], [bps])
                        evac(C, half, Us[:, half * 512:(half + 1) * 512], ps[:, :], bps, bUs)
                    yav = ya[:].rearrange("p (c a i) -> p c a i", c=8, a=2)
                    yfv = yfl[:].rearrange("p (c a i) -> p c a i", c=8, a=2)
                    for half in range(2):
                        ps, bps = next_ps(C)
                        for j in range(8):
                            slot, h, c, pb = half * 8 + j, 2 * j + half, j, half * 64
                            o = ps[:, j * 64:(j + 1) * 64]
                            _mm(S, o, AR[pb:pb + 64, c, 1, :], Sbb[pb:pb + 64, c, :], True, False, [bARBK, bS], [bps])
                            _mm(S, o, AB4[:, slot, 128:256], Us[:, slot * 64:(slot + 1) * 64], False, False, [bAB4, bUs], [bps])
                            _mm(S, o, AB4[:, slot, 384:512], Vt[:, h * 64:(h + 1) * 64], False, True, [bAB4, bVt], [bps])
                        psv = ps[:, :].rearrange("p (c i) -> p c i", c=8)
                        if d == 0:
                            evac(C, half, yav[:, :, half, :], psv, bps, bya)
                        else:
                            S.op('dve', lambda e_, half=half, psv=psv: e_.tensor_tensor(yav[:, :, half, :], psv, yfv[:, :, half, :], ALU.add),
                                 reads=[bps, byfl], writes=[bya])
                    if d == 0:
                        S.op('pool', lambda e_, c0=c0: e_.dma_start(out=C.YS[c0:c0 + 128, :], in_=ya[:]), reads=[bya], writes=[C.bYS], dma=bya)
                    ps, bps = next_ps(C)
                    for c in range(8):
                        o = ps[:, c * 64:(c + 1) * 64]
                        for par in range(2):
                            h = 2 * c + par
                            slot = par * 8 + c
                            _mm(S, o, bkTp[:, par, 0, c, :], Us[:, slot * 64:(slot + 1) * 64], par == 0, False, [bbkT, bUs], [bps])
                            _mm(S, o, bkTp[:, par, 1, c, :], Vt[:, h * 64:(h + 1) * 64], False, par == 1, [bbkT, bVt], [bps])
                    S.op('dve', lambda e_, ps=ps: e_.tensor_tensor(Sst[:].rearrange("p c i -> p (c i)"), ps[:, :],
                                                                Sst[:].rearrange("p c i -> p (c i)"), ALU.add),
                         reads=[bps, bS], writes=[bS])
                    S.op('dve', lambda e_, last=last: e_.tensor_tensor(Sst[:], Sst[:], eLi[:, :, last:last + 1].to_broadcast([128, 8, 64]), ALU.mult),
                         reads=[bS, beL], writes=[bS])
                    S.op('act', lambda e_: e_.copy(Sbb[:], Sst[:]), reads=[bS], writes=[bS])
                    if stage < 9:
                        continue
                    if C.cfg.get('dbg') and d == 0 and t == 0 and s4 == 0:
                        dbgb = Buf('dbg')
                        def dump(idx, tile_ap, n):
                            S.op('pool', lambda e_: e_.dma_start(out=C.DBG[idx, :, 0:n], in_=tile_ap), reads=[bAB4, bXm[0], bXm[1], bXm[2], bXm[3], bGs, bUs, bya, bARBK, beL, bsig, bVt, bS, bbkT, bprep],
                                 writes=[dbgb], dma=Buf('x'))
                        dump(0, AB4[:].rearrange("p a b -> p (a b)"), 8192)
                        dump(1, Xm[:].rearrange("p a b -> p (a b)"), 2048)
                        dump(2, Gs[:], 1024)
                        dump(3, Us[:], 1024)
                        dump(4, ya[:], 1024)
                        dump(5, AR[:].rearrange("p a b c -> p (a b c)"), 2048)
                        dump(6, BK[:].rearrange("p a b c -> p (a b c)"), 2048)
                        dump(7, eLi[:].rearrange("p a b -> p (a b)"), 1024)
                        dump(8, eLn[:].rearrange("p a b -> p (a b)"), 1024)
                        dump(9, eLx[:].rearrange("p a b -> p (a b)"), 1024)
                        dump(10, sig[:], 1024)
                        dump(11, Vt[:], 1024)
                        dump(12, Sst[:].rearrange("p a b -> p (a b)"), 512)
                        dump(13, nkT[:, :, 0:128].rearrange("p a b -> p (a b)"), 1024) if False else None
                    if d == 1:
                        yv = ya[:].rearrange("p (h i) -> p h i", h=16)
                        S.op('dve', lambda e_, yv=yv: e_.reduce_sum(stt[:, 0, :], yv, AX.X), reads=[bya], writes=[bstt])
                        S.op('dve', lambda e_: e_.tensor_scalar(stt[:, 0, :], stt[:, 0, :], 1.0 / 64, None, ALU.mult), reads=[bstt], writes=[bstt])
                        S.op('dve', lambda e_, yv=yv: e_.tensor_tensor(yv, yv, stt[:, 0, :].unsqueeze(2).to_broadcast([128, 16, 64]), ALU.subtract),
                             reads=[bya, bstt], writes=[bya])
                        S.op('act', lambda e_: e_.activation(sig[:], ya[:], AF.Square), reads=[bya], writes=[bsig])
                        S.op('dve', lambda e_: e_.reduce_sum(stt[:, 1, :], sig[:].rearrange("p (h i) -> p h i", h=16), AX.X),
                             reads=[bsig], writes=[bstt])
                        S.op('act', lambda e_: e_.activation(stt[:, 1, :], stt[:, 1, :], AF.Sqrt, bias=C.epsD[:, 3:4], scale=1.0 / 64),
                             reads=[bstt, C.bconst], writes=[bstt])
                        S.op('dve', lambda e_: e_.reciprocal(stt[:, 1, :], stt[:, 1, :]), reads=[bstt], writes=[bstt])
                        S.op('dve', lambda e_, yv=yv: e_.tensor_tensor(ynb[:].rearrange("p (h i) -> p h i", h=16), yv,
                                                                     stt[:, 1, :].unsqueeze(2).to_broadcast([128, 16, 64]), ALU.mult),
                             reads=[bya, bstt], writes=[bynb])
                        for half in range(2):
                            ps, bps = next_ps(C)
                            for j in range(4):
                                c = half * 4 + j
                                _mm(S, ps[:, j * 128:(j + 1) * 128], ynb[:, c * 128:(c + 1) * 128], C.identb[:], True, True,
                                    [bynb, C.bconst], [bps])
                            for j in range(4):
                                c = half * 4 + j
                                S.op('act', lambda e_, c=c, j=j, ps=ps: e_.activation(fin[:, c, :], ps[:, j * 128:(j + 1) * 128], AF.Identity,
                                                                                   bias=cols[:, 6, c:c + 1], scale=cols[:, 5, c:c + 1]),
                                     reads=[bps, bset], writes=[bfin])
                        S.op('dve', lambda e_, ts=ts: e_.tensor_tensor(fin[:], fin[:], bvT[:, :, ts], ALU.add), reads=[bfin, bprep], writes=[bfin])
                        S.op('dve', lambda e_, ts=ts: e_.tensor_tensor(yo[:, :, ts], fin[:], gT[:, :, ts], ALU.mult), reads=[bfin, bprep], writes=[byo])
                if d == 1:
                    S.op('pool', lambda e_, t0=t0: e_.dma_start(out=Yv[:, 8:16, t0:t0 + TT], in_=yo[:]), reads=[byo], writes=[C.bY], dma=byo)
        S.end_phase()


def bkind(C, pos):
    if pos <= 0 or pos >= C.N:
        return 'hard'
    return C.cfg.get('bounds', {}).get(pos)


def t5_tables():
    oh = np.zeros((33, 3, 255), np.float32)
    for dl in range(3):
        for i in range(255):
            rel = i - 127 + 128 * (dl - 1)
            if abs(rel) > 128:
                oh[32, dl, i] = -30000.0
                continue
            n = abs(rel)
            if n < 8:
                b = n
            else:
                v = np.float32(np.log(np.float32(n) / np.float32(8.0))) / np.float32(math.log(16.0)) * np.float32(8.0)
                b = min(8 + int(np.float32(v)), 15)
            if rel > 0:
                b += 16
            oh[b, dl, i] = 1.0
    return oh.reshape(33, 765)


def phase_mix_odd(C, o):
    nc, S, N, W = C.nc, C.S, C.N, C.W
    TT = 512
    with ExitStack() as st:
        sb = lambda name, shape, dt=F32: st.enter_context(_sbt(nc, name, list(shape), dt))
        bset = Buf('mo_setup')
        rba = sb('mo_rba', [33, 16])
        S.op('dve', lambda e: e.memset(rba[:], 1.0), writes=[bset])
        S.op('sp', lambda e: e.dma_start(out=rba[0:32, :], in_=W['rel_bias']), writes=[bset], dma=Buf('mo_sd2'))
        oh = sb('mo_oh', [33, 765])
        S.op('sp', lambda e: e.dma_start(out=oh[:], in_=C.c_oh), writes=[bset], dma=Buf('mo_sd3'))
        d2 = sb('mo_d2', [16, 765])
        for (a, b) in ((0, 510), (510, 765)):
            ps, bps = next_ps(C)
            _mm(S, ps[0:16, 0:b - a], rba[:, :], oh[:, a:b], True, True, [bset], [bps])
            S.op('dve', lambda e, a=a, b=b, ps=ps: e.tensor_copy(d2[:, a:b], ps[0:16, 0:b - a]), reads=[bps], writes=[bset])
        bD2 = Buf('D2')
        S.op('sp', lambda e: e.dma_start(out=C.D2.ap(), in_=d2[:]), reads=[bset], writes=[bD2], dma=Buf('mo_sd4'))
        hkf = sb('mo_hkf', [128, 16, 3, 128])
        for h in range(16):
            src = bass.AP(C.D2, h * 765, [[1, 128], [255, 3], [1, 128]])
            S.op('sp', lambda e, h=h, src=src: e.dma_start(out=hkf[:, h, :, :], in_=src), reads=[bD2], writes=[bset], dma=Buf('mo_sd5'))
        hk = sb('mo_hk', [128, 16, 3, 128], BF16)
        S.op('dve', lambda e: e.tensor_copy(hk[:], hkf[:]), reads=[bset], writes=[bset])
        jf = sb('mo_jf', [128, 128])
        S.op('sp', lambda e: e.dma_start(out=jf[:], in_=C.c_anti), writes=[bset], dma=Buf('mo_sd6'))
        jb = sb('mo_jb', [128, 128], BF16)
        S.op('dve', lambda e: e.tensor_copy(jb[:], jf[:]), reads=[bset], writes=[bset])
        bd = sb('mo_bd', [128, 128])
        S.op('sp', lambda e: e.dma_start(out=bd[:], in_=C.c_bd), writes=[bset], dma=Buf('mo_sd7'))
        opad = sb('mo_opad', [128, 2, 128], BF16)
        S.op('dve', lambda e: e.memset(opad[:], 0.0), writes=[bset])
        S.op('dve', lambda e: e.memset(opad[:, 0, 0:64], 1.0), writes=[bset])
        S.op('dve', lambda e: e.memset(opad[:, 1, 64:128], 1.0), writes=[bset])
        opadF = sb('mo_opadF', [128, 2, 128], BF16)
        S.op('dve', lambda e: e.tensor_scalar(opadF[:], opad[:], C.flagt[:, 0:1], None, ALU.mult),
             reads=[bset, C.bconst], writes=[bset])
        esk = sb('mo_esk', [128, 8])
        for hh in range(2):
            src = bass.AP(W['att_sink'].tensor, W['att_sink'][o].offset + hh, [[0, 64], [2, 8]])
            S.op('sp', lambda e, hh=hh, src=src: e.dma_start(out=esk[hh * 64:(hh + 1) * 64, :], in_=src, allow_slow_non_contiguous=True),
                 writes=[bset], dma=Buf('mo_sd8'))
        S.op('act', lambda e: e.activation(esk[:], esk[:], AF.Exp), reads=[bset], writes=[bset])
        wq = sb('mo_wq', [128, 2])
        for hh in range(2):
            S.op('sp', lambda e, hh=hh: e.dma_start(out=wq[hh * 64:(hh + 1) * 64, 0:1],
                                                    in_=W['att_q_norm_w'][o].rearrange("(p one) -> p one", one=1)),
                 writes=[bset], dma=Buf('mo_sd9'))
            S.op('sp', lambda e, hh=hh: e.dma_start(out=wq[hh * 64:(hh + 1) * 64, 1:2],
                                                    in_=W['att_k_norm_w'][o].rearrange("(p one) -> p one", one=1)),
                 writes=[bset], dma=Buf('mo_sd10'))
        S.op('dve', lambda e: e.tensor_scalar(wq[:, 0:1], wq[:, 0:1], 0.125, None, ALU.mult), reads=[bset], writes=[bset])

        st1 = ExitStack()
        sb = lambda name, shape, dt=F32, _s=st1: _s.enter_context(_sbt(nc, name, list(shape), dt))
        wnat = sb('mo_wnat', [31, D])
        S.op('sp', lambda e: e.dma_start(out=wnat[:], in_=W['conv_dw_w'][o]), writes=[bset], dma=Buf('mo_sd1'))
        wcol = sb('mo_wcol', [128, 8, 31])
        for c in range(8):
            ps, bps = next_ps(C)
            _mm(S, ps[:, 0:31], wnat[0:31, c * 128:(c + 1) * 128], C.ident[0:31, 0:31], True, True, [bset, C.bconst], [bps])
            S.op('dve', lambda e, c=c, ps=ps: e.tensor_copy(wcol[:, c, :], ps[:, 0:31]), reads=[bps], writes=[bset])
        dg = sb('mo_dg', [128, 8, 31, 128], BF16)
        for c in range(8):
            for k in range(31):
                S.op('dve', lambda e, c=c, k=k: e.tensor_scalar(dg[:, c, k, :], C.identb[:], wcol[:, c, k:k + 1], None, ALU.mult),
                     reads=[bset, C.bconst], writes=[bset])
        cvb = sb('mo_cvb', [128, 8])
        lnw = sb('mo_lnw', [128, 8])
        lnb = sb('mo_lnb', [128, 8])
        load_cols(C, cvb[:], bset, W['conv_dw_b'][o], 8)
        load_cols(C, lnw[:], bset, W['conv_ln_w'][o], 8)
        load_cols(C, lnb[:], bset, W['conv_ln_b'][o], 8)
        onesf = sb('mo_onesf', [128, 128])
        S.op('dve', lambda e: e.memset(onesf[:], 1.0), writes=[bset])
        NB = 2
        vg = [sb('mo_vg%d' % i, [128, 2, 544]) for i in range(NB)]
        bvg = [Buf('mo_vg%d' % i) for i in range(NB)]
        sgm = sb('mo_sgm', [128, 544])
        bsgm = Buf('mo_sgm')
        ub = [sb('mo_u%d' % i, [128, 544], BF16) for i in range(NB)]
        bub = [Buf('mo_u%d' % i) for i in range(NB)]
        cv = sb('mo_cv', [128, 8, TT])
        bcv = Buf('mo_cv')
        sqf = sb('mo_sqf', [128, 8, TT])
        bsqf = Buf('mo_sqf')
        mean = sb('mo_mean', [128, TT])
        rstd = sb('mo_rstd', [128, TT])
        bst = Buf('mo_stat')
        t1 = [sb('mo_t1%d' % i, [128, TT]) for i in range(2)]
        bt1 = [Buf('mo_t1%d' % i) for i in range(2)]
        yc = [sb('mo_yc%d' % i, [128, 8, TT], BF16) for i in range(2)]
        byc = [Buf('mo_yc%d' % i) for i in range(2)]
        Yv = C.Y.rearrange("(c p) n -> p c n", p=128)
        it = 0
        for t in range(N // TT):
            t0 = t * TT
            kl, kr = bkind(C, t0), bkind(C, t0 + TT)
            lo = 15 if kl == 'hard' else 0
            hi = 527 if kr == 'hard' else 542
            for c in range(8):
                v, bv = vg[it % NB], bvg[it % NB]
                u, bu = ub[it % NB], bub[it % NB]
                it += 1
                for j in range(2):
                    src = C.UF[j * 1024 + c * 128: j * 1024 + (c + 1) * 128, t0 - 15 + lo: t0 - 15 + hi]
                    S.op('sp', lambda e, v=v, j=j, src=src, lo=lo, hi=hi: e.dma_start(out=v[:, j, lo:hi], in_=src),
                         reads=[C.bUF], writes=[bv], dma=bv)
                S.op('act', lambda e, v=v, lo=lo, hi=hi: e.activation(sgm[:, lo:hi], v[:, 1, lo:hi], AF.Sigmoid),
                     reads=[bv], writes=[bsgm])
                if lo > 0:
                    S.op('pool', lambda e, u=u: e.memset(u[:, 0:15], 0.0), writes=[bu])
                if hi < 542:
                    S.op('pool', lambda e, u=u: e.memset(u[:, 527:542], 0.0), writes=[bu])
                S.op('dve', lambda e, v=v, u=u, lo=lo, hi=hi: e.tensor_tensor(u[:, lo:hi], v[:, 0, lo:hi], sgm[:, lo:hi], ALU.mult),
                     reads=[bv, bsgm], writes=[bu])
                if kl == 'soft':
                    S.op('dve', lambda e, u=u: e.tensor_scalar(u[:, 0:15], u[:, 0:15], C.flagt[:, 0:1], None, ALU.mult),
                         reads=[bu, C.bconst], writes=[bu])
                if kr == 'soft':
                    S.op('dve', lambda e, u=u: e.tensor_scalar(u[:, 527:542], u[:, 527:542], C.flagt[:, 0:1], None, ALU.mult),
                         reads=[bu, C.bconst], writes=[bu])
                ps, bps = next_ps(C)
                for k in range(31):
                    _mm(S, ps[:, :], dg[:, c, k, :], u[:, k:k + TT], k == 0, k == 30, [bset, bu], [bps])
                S.op('act', lambda e, c=c, ps=ps: e.activation(cv[:, c, :], ps[:, :], AF.Identity, bias=cvb[:, c:c + 1], scale=1.0),
                     reads=[bps, bset], writes=[bcv])
            S.op('act', lambda e: e.activation(sqf[:], cv[:], AF.Square), reads=[bcv], writes=[bsqf])
            ps1, bps1 = next_ps(C)
            for c in range(8):
                _mm(S, ps1[:, :], onesf[:], cv[:, c, :], c == 0, c == 7, [bset, bcv], [bps1])
            ps2, bps2 = next_ps(C)
            for c in range(8):
                _mm(S, ps2[:, :], onesf[:], sqf[:, c, :], c == 0, c == 7, [bset, bsqf], [bps2])
            S.op('dve', lambda e, ps1=ps1: e.tensor_scalar(mean[:], ps1[:, :], 1.0 / D, None, ALU.mult), reads=[bps1], writes=[bst])
            S.op('dve', lambda e: e.tensor_tensor(rstd[:], mean[:], mean[:], ALU.mult), reads=[bst], writes=[bst])
            S.op('dve', lambda e, ps2=ps2: e.scalar_tensor_tensor(rstd[:], ps2[:, :], 1.0 / D, rstd[:], ALU.mult, ALU.subtract),
                 reads=[bps2, bst], writes=[bst])
            S.op('act', lambda e: e.activation(rstd[:], rstd[:], AF.Sqrt, bias=C.epsD[:, 2:3], scale=1.0),
                 reads=[bst, C.bconst], writes=[bst])
            S.op('dve', lambda e: e.reciprocal(rstd[:], rstd[:]), reads=[bst], writes=[bst])
            y, by = yc[t % 2], byc[t % 2]
            for c in range(8):
                tt, btt = t1[c % 2], bt1[c % 2]
                S.op('dve', lambda e, c=c, tt=tt: e.tensor_tensor(tt[:], cv[:, c, :], mean[:], ALU.subtract),
                     reads=[bcv, bst], writes=[btt])
                S.op('dve', lambda e, tt=tt: e.tensor_tensor(tt[:], tt[:], rstd[:], ALU.mult), reads=[btt, bst], writes=[btt])
                S.op('act', lambda e, c=c, tt=tt, y=y: e.activation(y[:, c, :], tt[:], AF.Silu, bias=lnb[:, c:c + 1],
                                                                     scale=lnw[:, c:c + 1]),
                     reads=[btt, bset], writes=[by])
            S.op('pool', lambda e, y=y, t0=t0: e.dma_start(out=Yv[:, 0:8, t0:t0 + TT], in_=y[:]),
                 reads=[by], writes=[C.bY], dma=by)

        S.barrier()
        S.flush()
        st1.close()
        st2 = ExitStack()
        sb = lambda name, shape, dt=F32, _s=st2: _s.enter_context(_sbt(nc, name, list(shape), dt))
        qf = [sb('mo_qf%d' % i, [128, 8, TT]) for i in range(2)]
        bqf = [Buf('mo_qf%d' % i) for i in range(2)]
        kf = [sb('mo_kf%d' % i, [128, 4, 768]) for i in range(2)]
        bkf = [Buf('mo_kf%d' % i) for i in range(2)]
        vf = [sb('mo_vf%d' % i, [128, 6, 256]) for i in range(2)]
        bvf = [Buf('mo_vf%d' % i) for i in range(2)]
        sq2 = sb('mo_sq2', [128, 768])
        bsq2 = Buf('mo_sq2')
        rs2 = sb('mo_rs2', [128, 768])
        brs2 = Buf('mo_rs2')
        qn = [sb('mo_qn%d' % i, [128, 8, TT], BF16) for i in range(2)]
        bqn = [Buf('mo_qn%d' % i) for i in range(2)]
        kn = [sb('mo_kn%d' % i, [128, 4, 768], BF16) for i in range(2)]
        bkn = [Buf('mo_kn%d' % i) for i in range(2)]
        vp = [sb('mo_vp%d' % i, [128, 2, 6, 4, 128], BF16) for i in range(2)]
        bvp = [Buf('mo_vp%d' % i) for i in range(2)]
        for i in range(2):
            S.op('pool', lambda e, i=i: e.memset(vp[i][:], 0.0), writes=[bvp[i]])
        vpF = sb('mo_vpF', [128, 2, 2, 4, 128], BF16)
        bvpF = Buf('mo_vpF')
        pt = [sb('mo_pt%d' % i, [128, 384], BF16) for i in range(3)]
        bpt = [Buf('mo_pt%d' % i) for i in range(3)]
        dn = [sb('mo_dn%d' % i, [128, 128]) for i in range(2)]
        bdn = [Buf('mo_dn%d' % i) for i in range(2)]
        yd = [sb('mo_yd%d' % i, [128, 8, TT], BF16) for i in range(2)]
        byd = [Buf('mo_yd%d' % i) for i in range(2)]
        pi = 0
        di = 0
        for t in range(N // TT):
            t0 = t * TT
            kl, kr = bkind(C, t0), bkind(C, t0 + TT)
            kb_lo = 1 if kl == 'hard' else 0
            kb_hi = 5 if kr == 'hard' else 6
            q_, bq_, k_, bk_, v_, bv_ = qf[t % 2], bqf[t % 2], kf[t % 2], bkf[t % 2], vf[t % 2], bvf[t % 2]
            qn_, bqn_, kn_, bkn_, vp_, bvp_ = qn[t % 2], bqn[t % 2], kn[t % 2], bkn[t % 2], vp[t % 2], bvp[t % 2]
            UFv = C.UF.rearrange("(c p) n -> p c n", p=128)
            for half in range(2):
                S.op('sp', lambda e, q_=q_, half=half, t0=t0: e.dma_start(
                    out=q_[:, half * 4:half * 4 + 4, :], in_=UFv[:, 16 + half * 4:16 + half * 4 + 4, t0:t0 + TT]),
                    reads=[C.bUF], writes=[bq_], dma=bq_)
            c0, c1 = kb_lo * 128, kb_hi * 128
            S.op('sp', lambda e, k_=k_, c0=c0, c1=c1, t0=t0: e.dma_start(
                out=k_[:, :, c0:c1], in_=UFv[:, 24:28, t0 - 128 + c0:t0 - 128 + c1]),
                reads=[C.bUF], writes=[bk_], dma=bk_)
            S.op('sp', lambda e, v_=v_, t0=t0, kb_lo=kb_lo, kb_hi=kb_hi: e.dma_start(
                out=v_[:, kb_lo:kb_hi, :],
                in_=C.UT[t0 - 128 + kb_lo * 128:t0 - 128 + kb_hi * 128, 0:256].rearrange("(b p) c -> p b c", p=128)),
                reads=[C.bUT], writes=[bv_], dma=bv_)
            for c in range(8):
                S.op('act', lambda e, c=c, q_=q_: e.activation(sq2[:, 0:TT], q_[:, c, :], AF.Square), reads=[bq_], writes=[bsq2])
                ps, bps = next_ps(C)
                _mm(S, ps[:, :], bd[:], sq2[:, 0:TT], True, True, [bset, bsq2], [bps])
                S.op('act', lambda e, ps=ps: e.activation(rs2[:, 0:TT], ps[:, :], AF.Sqrt, bias=C.epsD[:, 1:2], scale=1.0 / 64),
                     reads=[bps, C.bconst], writes=[brs2])
                S.op('dve', lambda e: e.reciprocal(rs2[:, 0:TT], rs2[:, 0:TT]), reads=[brs2], writes=[brs2])
                S.op('dve', lambda e, c=c, q_=q_, qn_=qn_: e.scalar_tensor_tensor(qn_[:, c, :], q_[:, c, :], wq[:, 0:1], rs2[:, 0:TT],
                                                                                 ALU.mult, ALU.mult),
                     reads=[bq_, brs2, bset], writes=[bqn_])
            for c in range(4):
                S.op('act', lambda e, c=c, k_=k_, c0=c0, c1=c1: e.activation(sq2[:, c0:c1], k_[:, c, c0:c1], AF.Square),
                     reads=[bk_], writes=[bsq2])
                for (a, b) in ((c0, min(c1, c0 + 512)), (c0 + 512, c1)):
                    if b <= a:
                        continue
                    ps, bps = next_ps(C)
                    _mm(S, ps[:, 0:b - a], bd[:], sq2[:, a:b], True, True, [bset, bsq2], [bps])
                    S.op('act', lambda e, ps=ps, a=a, b=b: e.activation(rs2[:, a:b], ps[:, 0:b - a], AF.Sqrt, bias=C.epsD[:, 1:2],
                                                                        scale=1.0 / 64),
                         reads=[bps, C.bconst], writes=[brs2])
                S.op('dve', lambda e, c0=c0, c1=c1: e.reciprocal(rs2[:, c0:c1], rs2[:, c0:c1]), reads=[brs2], writes=[brs2])
                S.op('dve', lambda e, c=c, k_=k_, kn_=kn_, c0=c0, c1=c1: e.scalar_tensor_tensor(
                    kn_[:, c, c0:c1], k_[:, c, c0:c1], wq[:, 1:2], rs2[:, c0:c1], ALU.mult, ALU.mult),
                    reads=[bk_, brs2, bset], writes=[bkn_])
            vsrc = v_[:, kb_lo:kb_hi, :].rearrange("p b (h d) -> p b h d", h=4)
            S.op('dve', lambda e, vp_=vp_, vsrc=vsrc, kb_lo=kb_lo, kb_hi=kb_hi: e.tensor_copy(vp_[:, 0, kb_lo:kb_hi, :, 0:64], vsrc),
                 reads=[bv_], writes=[bvp_])
            S.op('pool', lambda e, vp_=vp_, vsrc=vsrc, kb_lo=kb_lo, kb_hi=kb_hi: e.tensor_copy(vp_[:, 1, kb_lo:kb_hi, :, 64:128], vsrc),
                 reads=[bv_], writes=[bvp_])
            if kl == 'soft':
                S.op('dve', lambda e, vp_=vp_: e.tensor_scalar(vpF[:, :, 0, :, :], vp_[:, :, 0, :, :], C.flagt[:, 0:1], None, ALU.mult),
                     reads=[bvp_, C.bconst], writes=[bvpF])
            if kr == 'soft':
                S.op('dve', lambda e, vp_=vp_: e.tensor_scalar(vpF[:, :, 1, :, :], vp_[:, :, 5, :, :], C.flagt[:, 0:1], None, ALU.mult),
                     reads=[bvp_, C.bconst], writes=[bvpF])
            y, by = yd[t % 2], byd[t % 2]
            for qb in range(4):
                for pair in range(8):
                    po, bpo = next_ps(C)
                    pd, bpd = next_ps(C)
                    dls = [dl for dl in range(3) if kb_lo <= qb + dl < kb_hi]
                    nmm = 2 * len(dls)
                    im = 0
                    for par in range(2):
                        hq = pair * 2 + par
                        hkv = hq // 4
                        pb = par * 64
                        ps, bps = next_ps(C)
                        for dl in dls:
                            kb = qb + dl
                            _mm(S, ps[:, dl * 128:(dl + 1) * 128], kn_[pb:pb + 64, hkv, kb * 128:(kb + 1) * 128],
                                qn_[pb:pb + 64, pair, qb * 128:(qb + 1) * 128], True, False, [bkn_, bqn_], [bps])
                            _mm(S, ps[:, dl * 128:(dl + 1) * 128], hk[:, hq, dl, :], jb[:], False, True, [bset], [bps])
                        p_, bp_ = pt[pi % 3], bpt[pi % 3]
                        pi += 1
                        a, b = dls[0] * 128, (dls[-1] + 1) * 128
                        S.op('act', lambda e, p_=p_, ps=ps, a=a, b=b: e.activation(p_[:, a:b], ps[:, a:b], AF.Exp),
                             reads=[bps], writes=[bp_])
                        for dl in dls:
                            kb = qb + dl
                            soft = (kb == 0 and kl == 'soft') or (kb == 5 and kr == 'soft')
                            if soft:
                                vl = vpF[:, par, 0 if kb == 0 else 1, hkv, :]
                                ol = opadF[:, par, :]
                                rd = [bvpF, bset]
                            else:
                                vl = vp_[:, par, kb, hkv, :]
                                ol = opad[:, par, :]
                                rd = [bvp_, bset]
                            _mm(S, po[:, 0:128], vl, p_[:, dl * 128:(dl + 1) * 128], im == 0, im == nmm - 1, rd + [bp_], [bpo])
                            _mm(S, pd[:, 0:128], ol, p_[:, dl * 128:(dl + 1) * 128], im == 0, im == nmm - 1, rd + [bp_], [bpd])
                            im += 1
                    d_, bd_ = dn[di % 2], bdn[di % 2]
                    di += 1
                    S.op('dve', lambda e, d_=d_, pd=pd, pair=pair: e.tensor_scalar(d_[:], pd[:, 0:128], esk[:, pair:pair + 1], None, ALU.add),
                         reads=[bpd, bset], writes=[bd_])
                    S.op('dve', lambda e, d_=d_: e.reciprocal(d_[:], d_[:]), reads=[bd_], writes=[bd_])
                    S.op('dve', lambda e, d_=d_, po=po, y=y, pair=pair, qb=qb: e.tensor_tensor(
                        y[:, pair, qb * 128:(qb + 1) * 128], po[:, 0:128], d_[:], ALU.mult),
                        reads=[bpo, bd_], writes=[by])
            S.op('pool', lambda e, y=y, t0=t0: e.dma_start(out=Yv[:, 8:16, t0:t0 + TT], in_=y[:]),
                 reads=[by], writes=[C.bY], dma=by)
        S.end_phase()
        st2.close()


_WNAMES = ['rel_bias', 'norm_mix_w', 'norm_ffn_w', 'ffn_w_in', 'ffn_w_out', 'ev_w_in', 'ev_w_out', 'ssd_conv_w', 'ssd_conv_b',
           'ssd_dt_bias', 'ssd_a_log', 'ssd_d', 'ssd_norm_w', 'rwkv_mu', 'rwkv_w0', 'rwkv_w2', 'rwkv_a0', 'rwkv_a2', 'rwkv_g2',
           'rwkv_k_k', 'rwkv_k_a', 'rwkv_r_k', 'rwkv_ln_w', 'rwkv_ln_b', 'od_w_in', 'od_w_out', 'conv_dw_w', 'conv_dw_b',
           'conv_ln_w', 'conv_ln_b', 'att_q_norm_w', 'att_k_norm_w', 'att_sink']
_PROG = {}


def kernel(x_prompt, x_sample, **weights):
    SEG = 4096
    NCORE = 8
    x_prompt = np.asarray(x_prompt, dtype=np.float32)
    x_sample = np.asarray(x_sample, dtype=np.float32)
    assert x_prompt.shape == (16, SEG, D) and x_sample.shape == (4, 2 * SEG, D)
    if 'full' not in _PROG:
        cfg = dict(N=3 * SEG, bounds={SEG: 'soft', 2 * SEG: 'hard'}, layers=[0, 1, 2, 3], debug=False, mixers=True)
        _PROG['full'] = build_program(cfg)
    nc, C = _PROG['full']
    wmap = {k: np.ascontiguousarray(np.asarray(weights[k], dtype=np.float32)) for k in _WNAMES}
    in_maps = []
    layout = []
    for c in range(NCORE):
        if c < 4:
            xin = np.concatenate([x_sample[c], x_prompt[c]], axis=0)
            fl = 1.0
            layout.append([('s', c), ('p', c)])
        else:
            ids = [4 + 3 * (c - 4) + j for j in range(3)]
            xin = np.concatenate([x_prompt[i] for i in ids], axis=0)
            fl = 0.0
            layout.append([('p', i) for i in ids])
        m = {'xin': np.ascontiguousarray(xin), 'flag': np.full((128, 1), fl, np.float32)}
        m.update(wmap)
        m.update(C.const_inputs)
        in_maps.append(m)
    res = run_bass_kernel_spmd(nc, in_maps, core_ids=list(range(NCORE)))
    y_prompt = np.empty((16, SEG, D), np.float32)
    y_sample = np.empty((4, 2 * SEG, D), np.float32)
    for c in range(NCORE):
        y = np.asarray(res.results[c]['yout'])
        pos = 0
        for kind, i in layout[c]:
            if kind == 's':
                y_sample[i] = y[pos:pos + 2 * SEG]
                pos += 2 * SEG
            else:
                y_prompt[i] = y[pos:pos + SEG]
                pos += SEG
    return (y_prompt, y_sample)
```

```python
import math
from contextlib import ExitStack
import numpy as np
import concourse.bass as bass
import concourse.mybir as mybir
from concourse.bass_utils import run_bass_kernel_spmd

F32 = mybir.dt.float32
BF16 = mybir.dt.bfloat16
AF = mybir.ActivationFunctionType
ALU = mybir.AluOpType
AX = mybir.AxisListType

D = 1024
DFF = 2816
EV_IN = 5920
OD_IN = 3584
ENGS = ('pe', 'act', 'dve', 'pool', 'sp')


class Buf:
    __slots__ = ('name', 'writers', 'readers', 'dsem', 'excl')

    def __init__(self, name, excl=False):
        self.name = name
        self.excl = excl
        self.writers = {}
        self.readers = {}
        self.dsem = None


class Sched:
    def __init__(self, nc, stack):
        self.nc = nc
        self.stack = stack
        self.q = {e: [] for e in ENGS}
        self.seq = {}
        self.known = {e: {} for e in ENGS}
        self.semh = {}
        self.ninst = 0
        self.free_keys = []
        self.phase_keys = []
        for e in ('pe', 'act', 'dve', 'pool'):
            self.semh[e] = stack.enter_context(nc.semaphore('s_' + e))
            self.seq[e] = 0

    def _dsem(self, buf):
        if buf.dsem is None:
            if self.free_keys:
                k = self.free_keys.pop()
            else:
                k = 'd%d' % len(self.semh)
                self.semh[k] = self.stack.enter_context(self.nc.semaphore(k))
                self.seq[k] = 0
            buf.dsem = k
            self.phase_keys.append(k)
        return buf.dsem

    def end_phase(self):
        self.barrier()
        self.flush()
        self.free_keys.extend(self.phase_keys)
        self.phase_keys = []

    def op(self, eng, fn, reads=(), writes=(), dma=None):
        if dma is not None:
            key = self._dsem(dma)
            inc = 16
        else:
            key = eng
            inc = 1
        deps = {}
        for b in reads:
            for k, v in b.writers.items():
                if deps.get(k, 0) < v:
                    deps[k] = v
            if b.excl:
                for k, v in b.readers.items():
                    if k != key and deps.get(k, 0) < v:
                        deps[k] = v
        for b in writes:
            for k, v in b.writers.items():
                if k == key:
                    continue
                if deps.get(k, 0) < v:
                    deps[k] = v
            for k, v in b.readers.items():
                if k == key and dma is None:
                    continue
                if deps.get(k, 0) < v:
                    deps[k] = v
        kn = self.known[eng]
        waits = []
        for k, v in deps.items():
            if kn.get(k, 0) < v:
                kn[k] = v
                waits.append((k, v))
        self.seq[key] += inc
        tok = self.seq[key]
        self.q[eng].append((waits, fn, key, inc))
        self.ninst += 1 + len(waits)
        for b in reads:
            if b.readers.get(key, 0) < tok:
                b.readers[key] = tok
        for b in writes:
            b.writers[key] = tok
            b.readers = {}

    def barrier(self):
        for e in ENGS:
            kn = self.known[e]
            waits = []
            for k, v in self.seq.items():
                if kn.get(k, 0) < v:
                    kn[k] = v
                    waits.append((k, v))
            if waits:
                self.q[e].append((waits, None, None, 0))

    def flush(self):
        nc = self.nc
        semh = self.semh
        q = self.q

        def replay(name, eng):
            for waits, fn, key, inc in q[name]:
                for k, v in waits:
                    eng.wait_ge(semh[k], v)
                if fn is not None:
                    fn(eng).then_inc(semh[key], inc)

        with nc.Block() as block:
            @block.tensor
            def _(e):
                replay('pe', e)

            @block.scalar
            def _(e):
                replay('act', e)

            @block.vector
            def _(e):
                replay('dve', e)

            @block.gpsimd
            def _(e):
                replay('pool', e)

            @block.sync
            def _(e):
                replay('sp', e)
        self.q = {e: [] for e in ENGS}


class Ctx:
    pass


_UID = [0]


def _sbt(nc, name, shape, dt):
    _UID[0] += 1
    return nc.sbuf_tensor('%s_%d' % (name, _UID[0]), shape, dt)


def _mm(S, out, lhsT, rhs, start, stop, reads, writes):
    S.op('pe', lambda e: e.matmul(out, lhsT, rhs, start=start, stop=stop), reads=reads, writes=writes)


def build_program(cfg):
    N = cfg['N']
    debug = cfg.get('debug', False)
    layers = cfg.get('layers', [0, 1, 2, 3])
    nc = bass.Bass("TRN2", target_bir_lowering=False)
    C = Ctx()
    C.nc = nc
    C.N = N
    C.cfg = cfg
    TT = 512
    assert N % TT == 0
    NT = N // TT

    def din(name, shape, dt=F32):
        return nc.dram_tensor(name, list(shape), dt, kind="ExternalInput").ap()

    def dscr(name, shape, dt=F32):
        return nc.dram_tensor(name, list(shape), dt, kind="ExternalOutput" if debug else "Internal").ap()

    xin = din('xin', [N, D])
    flag = din('flag', [128, 1])
    W = {}
    wshapes = dict(
        rel_bias=(32, 16), norm_mix_w=(4, D), norm_ffn_w=(4, D), ffn_w_in=(4, D, 2 * DFF), ffn_w_out=(4, DFF, D),
        ev_w_in=(2, D, EV_IN), ev_w_out=(2, 2048, D), ssd_conv_w=(2, 5, 1536), ssd_conv_b=(2, 1536),
        ssd_dt_bias=(2, 2, 16), ssd_a_log=(2, 2, 16), ssd_d=(2, 16), ssd_norm_w=(2, D), rwkv_mu=(2, 3328),
        rwkv_w0=(2, 2, D), rwkv_w2=(2, 2, 64, D), rwkv_a0=(2, D), rwkv_a2=(2, 64, D), rwkv_g2=(2, 128, D),
        rwkv_k_k=(2, D), rwkv_k_a=(2, D), rwkv_r_k=(2, 16, 64), rwkv_ln_w=(2, D), rwkv_ln_b=(2, D),
        od_w_in=(2, D, OD_IN), od_w_out=(2, 2048, D), conv_dw_w=(2, 31, D), conv_dw_b=(2, D), conv_ln_w=(2, D),
        conv_ln_b=(2, D), att_q_norm_w=(2, 64), att_k_norm_w=(2, 64), att_sink=(2, 16))
    for k, shp in wshapes.items():
        W[k] = din(k, shp)
    yout = nc.dram_tensor('yout', [N, D], F32, kind="ExternalOutput").ap()
    XA = dscr('XA', [D, N])
    XB = dscr('XB', [D, N])
    UF = dscr('UF', [38 * 128, N])
    UT = dscr('UT', [N, 1056])
    C.UFb = dscr('UFb', [26 * 128, N], BF16)
    Y = dscr('Y', [2048, N], BF16)
    H = dscr('H', [DFF, N], BF16)
    C.xin, C.flag, C.W, C.yout = xin, flag, W, yout
    C.XA, C.XB, C.UF, C.UT, C.Y, C.H = XA, XB, UF, UT, Y, H
    C.bXA, C.bXB, C.bUF, C.bUT, C.bY, C.bH = Buf('XA'), Buf('XB'), Buf('UF'), Buf('UT'), Buf('Y'), Buf('H')
    C.bIN = Buf('in')
    C.bOUT = Buf('yout')

    with ExitStack() as top:
        S = Sched(nc, top)
        C.S = S
        C.ps = []
        for i in range(8):
            C.ps.append((top.enter_context(nc.psum_tensor('ps%d' % i, [128, 512], F32)), Buf('ps%d' % i, excl=True)))
        C.psi = 0
        ident = top.enter_context(_sbt(nc, 'ident', [128, 128], F32))
        identb = top.enter_context(_sbt(nc, 'identb', [128, 128], BF16))
        onesb = top.enter_context(_sbt(nc, 'onesb', [128, 128], BF16))
        C.ident, C.identb, C.onesb = ident, identb, onesb
        C.bconst = Buf('const')
        identd = din('c_ident', [128, 128])
        C.const_inputs = {'c_ident': np.eye(128, dtype=np.float32)}
        S.op('sp', lambda e: e.dma_start(out=ident[:], in_=identd), writes=[C.bconst], dma=C.bconst)
        S.op('dve', lambda e: e.tensor_copy(identb[:], ident[:]), reads=[C.bconst], writes=[C.bconst])
        S.op('dve', lambda e: e.memset(onesb[:], 1.0), writes=[C.bconst])
        C.epsD = top.enter_context(_sbt(nc, 'epsD', [128, 4], F32))
        S.op('dve', lambda e: e.memset(C.epsD[:, 0:1], float(D * 1e-6)), writes=[C.bconst])
        S.op('dve', lambda e: e.memset(C.epsD[:, 1:2], 1e-6), writes=[C.bconst])
        S.op('dve', lambda e: e.memset(C.epsD[:, 2:3], 1e-5), writes=[C.bconst])
        S.op('dve', lambda e: e.memset(C.epsD[:, 3:4], 64e-5), writes=[C.bconst])
        C.flagt = top.enter_context(_sbt(nc, 'flagt', [128, 1], F32))
        S.op('sp', lambda e: e.dma_start(out=C.flagt[:], in_=flag), writes=[C.bconst], dma=C.bconst)
        C.c_oh = din('c_oh', [33, 765])
        C.c_anti = din('c_anti', [128, 128])
        C.c_bd = din('c_bd', [128, 128])
        C.const_inputs['c_oh'] = t5_tables()
        C.const_inputs['c_anti'] = np.ascontiguousarray(np.eye(128, dtype=np.float32)[::-1])
        C.const_inputs['c_bd'] = np.kron(np.eye(2, dtype=np.float32), np.ones((64, 64), np.float32))
        C.D2 = nc.dram_tensor('D2', [16, 765], F32, kind="Internal")
        C.c_tri = din('c_tri', [128, 2, 128])
        C.c_nm = din('c_nm', [128, 2, 128])
        ii = np.arange(128)
        tri = np.zeros((128, 2, 128), np.float32)
        tri[:, 0, :] = (ii[:, None] <= ii[None, :])
        tri[:, 1, :] = (ii[:, None] >= ii[None, :])
        nm = np.zeros((128, 2, 128), np.float32)
        nm[:, 0, :] = np.where(ii[:, None] > ii[None, :], -30000.0, 0.0)
        nm[:, 1, :] = np.where(ii[:, None] < ii[None, :], -30000.0, 0.0)
        C.const_inputs['c_tri'] = tri
        C.c_tri2 = din('c_tri2', [128, 2, 128])
        tri2 = np.zeros((128, 2, 128), np.float32)
        tri2[:, 0, :] = (ii[:, None] < ii[None, :])
        tri2[:, 1, :] = (ii[:, None] > ii[None, :])
        C.const_inputs['c_tri2'] = tri2
        C.const_inputs['c_nm'] = nm
        C.onec = top.enter_context(_sbt(nc, 'onec', [128, 1], F32))
        S.op('dve', lambda e: e.memset(C.onec[:], 1.0), writes=[C.bconst])
        C.YS = dscr('YS', [N, D])
        if cfg.get('dbg'):
            C.DBG = nc.dram_tensor('DBG', [17, 128, 8192], F32, kind='ExternalOutput').ap()
        C.bYS = Buf('YS')
        S.barrier()
        S.flush()
        S.phase_keys = []

        phase_in(C)
        cur, nxt = (XA, C.bXA), (XB, C.bXB)
        for li, layer in enumerate(layers):
            last = (li == len(layers) - 1)
            if layer % 2 == 0:
                e = layer // 2
                phase_inproj_even(C, cur, layer, e)
                if cfg.get('mixers', True):
                    phase_mix_even(C, e)
            else:
                o = layer // 2
                phase_inproj_odd(C, cur, layer, o)
                if cfg.get('mixers', True):
                    phase_mix_odd(C, o)
            wout = W['ev_w_out'][layer // 2] if layer % 2 == 0 else W['od_w_out'][layer // 2]
            phase_outproj(C, cur, nxt, wout)
            phase_ffn_in(C, nxt, layer)
            phase_ffn_out(C, nxt, cur, layer, last)
        if not layers:
            phase_out_only(C, cur)
        S.barrier()
        S.flush()
    return nc, C


def next_ps(C):
    t, b = C.ps[C.psi]
    C.psi = (C.psi + 1) % 8
    return t, b


def phase_in(C):
    nc, S, N = C.nc, C.S, C.N
    with ExitStack() as st:
        xt = [st.enter_context(_sbt(nc, 'pi_x%d' % i, [128, D], F32)) for i in range(2)]
        bx = [Buf('pi_x%d' % i) for i in range(2)]
        ot = [st.enter_context(_sbt(nc, 'pi_o%d' % i, [128, 8, 512], F32)) for i in range(2)]
        bo = [Buf('pi_o%d' % i) for i in range(2)]
        nsub = N // 128
        for t512 in range(N // 512):
            o, bob = ot[t512 % 2], bo[t512 % 2]
            for s4 in range(4):
                sub = t512 * 4 + s4
                x, bxb = xt[sub % 2], bx[sub % 2]
                S.op('sp', lambda e, x=x, sub=sub: e.dma_start(out=x[:], in_=C.xin[sub * 128:(sub + 1) * 128, :]),
                     reads=[C.bIN], writes=[bxb], dma=bxb)
                for half in range(2):
                    ps, bps = next_ps(C)
                    for j in range(4):
                        c = half * 4 + j
                        _mm(S, ps[:, j * 128:(j + 1) * 128], x[:, c * 128:(c + 1) * 128], C.ident[:], True, True,
                            [bxb, C.bconst], [bps])
                    eng = 'act' if half == 0 else 'dve'
                    dst = o[:, half * 4:half * 4 + 4, s4 * 128:(s4 + 1) * 128]
                    src = ps[:, :].rearrange("p (j t) -> p j t", j=4)
                    if eng == 'act':
                        S.op('act', lambda e, dst=dst, src=src: e.copy(dst, src), reads=[bps], writes=[bob])
                    else:
                        S.op('dve', lambda e, dst=dst, src=src: e.tensor_copy(dst, src), reads=[bps], writes=[bob])
            dst = C.XA.rearrange("(c p) n -> p c n", p=128)[:, :, t512 * 512:(t512 + 1) * 512]
            S.op('pool', lambda e, dst=dst, o=o: e.dma_start(out=dst, in_=o[:]), reads=[bob], writes=[C.bXA], dma=bob)
        S.end_phase()


def load_weight_bf16(C, wt, bw, wdram, K, col0, ncols, dcol0=0):
    S = C.S
    kc = K // 128
    src = wdram.rearrange("(c p) m -> p c m", p=128)
    step = 1024
    for c in range(kc):
        for m0 in range(0, ncols, step):
            m1 = min(ncols, m0 + step)
            S.op('pool', lambda e, c=c, m0=m0, m1=m1: e.dma_start(
                out=wt[:, c, dcol0 + m0:dcol0 + m1], in_=src[:, c, col0 + m0:col0 + m1]),
                writes=[bw], dma=bw)


def load_cols(C, dst_tile, bdst, vec_dram, nchunk):
    S = C.S
    src = vec_dram.rearrange("(c p) -> p c", p=128)
    C.nc
    S.op('sp', lambda e: e.dma_start(out=dst_tile, in_=src, allow_slow_non_contiguous=True), writes=[bdst], dma=Buf('lc'))


def rmsnorm_tile(C, xT, bx, hT, bh, w32, bw, sq, bsq, rstd, brs):
    S = C.S
    S.op('act', lambda e: e.activation(sq[:], xT[:], AF.Square), reads=[bx], writes=[bsq])
    ps, bps = next_ps(C)
    for c in range(8):
        _mm(S, ps[:, :], C.onesb[:], sq[:, c, :], c == 0, c == 7, [bsq, C.bconst], [bps])
    S.op('act', lambda e: e.activation(rstd[:], ps[:, :], AF.Sqrt, bias=C.epsD[:, 0:1], scale=1.0),
         reads=[bps, C.bconst], writes=[brs])
    S.op('dve', lambda e: e.reciprocal(rstd[:], rstd[:]), reads=[brs], writes=[brs])
    for c in range(8):
        S.op('dve', lambda e, c=c: e.scalar_tensor_tensor(hT[:, c, :], xT[:, c, :], w32[:, c:c + 1], rstd[:],
                                                         ALU.mult, ALU.mult),
             reads=[bx, brs, bw], writes=[bh])


def evac(C, idx, dst, src, bsrc, bdst, extra_reads=()):
    S = C.S
    if idx % 2 == 0:
        S.op('act', lambda e: e.copy(dst, src), reads=[bsrc] + list(extra_reads), writes=[bdst])
    else:
        S.op('dve', lambda e: e.tensor_copy(dst, src), reads=[bsrc] + list(extra_reads), writes=[bdst])


def phase_inproj(C, cur, norm_w, wdram, M, wcols_extra, fm_chunks, tm_groups, name):
    nc, S, N = C.nc, C.S, C.N
    X, bX = cur
    Mtot = M + sum(n for _, n, _ in wcols_extra)
    with ExitStack() as st:
        wt = st.enter_context(_sbt(nc, name + '_w', [128, 8, Mtot], BF16))
        bw = Buf(name + '_w')
        load_weight_bf16(C, wt, bw, wdram, D, 0, M)
        for (s0, n, d0) in wcols_extra:
            load_weight_bf16(C, wt, bw, wdram, D, s0, n, d0)
        nw = st.enter_context(_sbt(nc, name + '_nw', [128, 8], F32))
        bnw = Buf(name + '_nw')
        load_cols(C, nw[:], bnw, norm_w, 8)
        S.op('dve', lambda e: e.tensor_scalar(nw[:], nw[:], 32.0, None, ALU.mult), reads=[bnw], writes=[bnw])
        xT = [st.enter_context(_sbt(nc, name + '_x%d' % i, [128, 8, 512], F32)) for i in range(2)]
        bx = [Buf(name + '_x%d' % i) for i in range(2)]
        hT = [st.enter_context(_sbt(nc, name + '_h%d' % i, [128, 8, 512], BF16)) for i in range(2)]
        bh = [Buf(name + '_h%d' % i) for i in range(2)]
        sq = st.enter_context(_sbt(nc, name + '_sq', [128, 8, 512], BF16))
        bsq = Buf('sq')
        rstd = st.enter_context(_sbt(nc, name + '_rs', [128, 512], F32))
        brs = Buf('rs')
        NSTG = 4
        stg = [st.enter_context(_sbt(nc, name + '_st%d' % i, [128, 2, 512], F32)) for i in range(NSTG)]
        bst = [Buf(name + '_st%d' % i) for i in range(NSTG)]
        stgb = [st.enter_context(_sbt(nc, name + '_sb%d' % i, [128, 2, 512], BF16)) for i in range(NSTG)]
        bstb = [Buf(name + '_sb%d' % i) for i in range(NSTG)]
        Xv = X.rearrange("(c p) n -> p c n", p=128)
        si = 0
        ei = 0
        for t in range(N // 512):
            x, bxx, h, bhh = xT[t % 2], bx[t % 2], hT[t % 2], bh[t % 2]
            for half in range(2):
                S.op('sp', lambda e, x=x, t=t, half=half: e.dma_start(
                    out=x[:, half * 4:half * 4 + 4, :], in_=Xv[:, half * 4:half * 4 + 4, t * 512:(t + 1) * 512]),
                    reads=[bX], writes=[bxx], dma=bxx)
            rmsnorm_tile(C, x, bxx, h, bhh, nw, bnw, sq, bsq, rstd, brs)
            for i in range(0, len(fm_chunks), 2):
                grp = fm_chunks[i:i + 2]
                isb = len(grp[0]) > 2
                if isb:
                    sg, bsg = stgb[si % NSTG], bstb[si % NSTG]
                else:
                    sg, bsg = stg[si % NSTG], bst[si % NSTG]
                si += 1
                for j, gg in enumerate(grp):
                    wc0, ur0 = gg[0], gg[1]
                    ps, bps = next_ps(C)
                    for kc in range(8):
                        _mm(S, ps[:, :], wt[:, kc, wc0:wc0 + 128], h[:, kc, :], kc == 0, kc == 7, [bw, bhh], [bps])
                    evac(C, ei, sg[:, j, :], ps[:, :], bps, bsg)
                    ei += 1
                contiguous = len(grp) == 2 and grp[1][1] == grp[0][1] + 128
                if isb:
                    assert contiguous
                    ur0 = grp[0][1]
                    dst = C.UFb[ur0:ur0 + 256, t * 512:(t + 1) * 512].rearrange("(c p) n -> p c n", p=128)
                    S.op('pool', lambda e, dst=dst, sg=sg: e.dma_start(out=dst, in_=sg[:]),
                         reads=[bsg], writes=[C.bUF], dma=bsg)
                elif contiguous:
                    ur0 = grp[0][1]
                    dst = C.UF[ur0:ur0 + 256, t * 512:(t + 1) * 512].rearrange("(c p) n -> p c n", p=128)
                    S.op('pool', lambda e, dst=dst, sg=sg: e.dma_start(out=dst, in_=sg[:]),
                         reads=[bsg], writes=[C.bUF], dma=bsg)
                else:
                    for j, gg in enumerate(grp):
                        wc0, ur0 = gg[0], gg[1]
                        dst = C.UF[ur0:ur0 + 128, t * 512:(t + 1) * 512]
                        S.op('pool', lambda e, dst=dst, sg=sg, j=j: e.dma_start(out=dst, in_=sg[:, j, :]),
                             reads=[bsg], writes=[C.bUF], dma=bsg)
            for s4 in range(4):
                for (wc0, ncol, uc0) in tm_groups:
                    sg, bsg = stg[si % NSTG], bst[si % NSTG]
                    si += 1
                    ps, bps = next_ps(C)
                    for kc in range(8):
                        _mm(S, ps[:, 0:ncol], h[:, kc, s4 * 128:(s4 + 1) * 128], wt[:, kc, wc0:wc0 + ncol],
                            kc == 0, kc == 7, [bw, bhh], [bps])
                    sgv = sg[:].rearrange("p a b -> p (a b)")[:, 0:ncol]
                    evac(C, ei, sgv, ps[:, 0:ncol], bps, bsg)
                    ei += 1
                    dst = C.UT[t * 512 + s4 * 128: t * 512 + (s4 + 1) * 128, uc0:uc0 + ncol]
                    S.op('pool', lambda e, dst=dst, sgv=sgv: e.dma_start(out=dst, in_=sgv),
                         reads=[bsg], writes=[C.bUT], dma=bsg)
        S.end_phase()


def phase_inproj_even(C, cur, layer, e):
    W = C.W
    fm = [(1024 + i * 128, i * 128) for i in range(12)] + [(2592 + i * 128, i * 128, 'b') for i in range(26)]
    tm = [(0, 512, 0), (512, 512, 512), (2560, 32, 1024)]
    phase_inproj(C, cur, W['norm_mix_w'][layer], W['ev_w_in'][e], EV_IN, [], fm, tm, 'ie%d' % layer)


def phase_inproj_odd(C, cur, layer, o):
    W = C.W
    extra = []
    for hk in range(4):
        extra.append((3072 + hk * 64, 64, OD_IN + hk * 128))
        extra.append((3072 + hk * 64, 64, OD_IN + hk * 128 + 64))
    fm = [(i * 128, i * 128) for i in range(24)] + [(OD_IN + i * 128, 3072 + i * 128) for i in range(4)]
    tm = [(3328, 256, 0)]
    phase_inproj(C, cur, W['norm_mix_w'][layer], W['od_w_in'][o], OD_IN, extra, fm, tm, 'io%d' % layer)


def phase_outproj(C, cur, nxt, wdram):
    nc, S, N = C.nc, C.S, C.N
    X, bX = cur
    X2, bX2 = nxt
    with ExitStack() as st:
        wt = st.enter_context(_sbt(nc, 'op_w', [128, 16, D], BF16))
        bw = Buf('op_w')
        load_weight_bf16(C, wt, bw, wdram, 2048, 0, D)
        xT = [st.enter_context(_sbt(nc, 'op_x%d' % i, [128, 8, 512], F32)) for i in range(2)]
        bx = [Buf('op_x%d' % i) for i in range(2)]
        yT = [st.enter_context(_sbt(nc, 'op_y%d' % i, [128, 16, 512], BF16)) for i in range(2)]
        by = [Buf('op_y%d' % i) for i in range(2)]
        Xv = X.rearrange("(c p) n -> p c n", p=128)
        X2v = X2.rearrange("(c p) n -> p c n", p=128)
        Yv = C.Y.rearrange("(c p) n -> p c n", p=128)
        for t in range(N // 512):
            x, bxx, y, byy = xT[t % 2], bx[t % 2], yT[t % 2], by[t % 2]
            sl = slice(t * 512, (t + 1) * 512)
            for half in range(2):
                S.op('sp', lambda e, x=x, sl=sl, half=half: e.dma_start(
                    out=x[:, half * 4:half * 4 + 4, :], in_=Xv[:, half * 4:half * 4 + 4, sl]),
                    reads=[bX], writes=[bxx], dma=bxx)
                S.op('sp', lambda e, y=y, sl=sl, half=half: e.dma_start(
                    out=y[:, half * 8:half * 8 + 8, :], in_=Yv[:, half * 8:half * 8 + 8, sl]),
                    reads=[C.bY], writes=[byy], dma=byy)
            for oc in range(8):
                ps, bps = next_ps(C)
                for kc in range(16):
                    _mm(S, ps[:, :], wt[:, kc, oc * 128:(oc + 1) * 128], y[:, kc, :], kc == 0, kc == 15, [bw, byy], [bps])
                S.op('dve', lambda e, x=x, ps=ps, oc=oc: e.tensor_tensor(x[:, oc, :], x[:, oc, :], ps[:, :], ALU.add),
                     reads=[bps, bxx], writes=[bxx])
            for half in range(2):
                S.op('pool', lambda e, x=x, sl=sl, half=half: e.dma_start(
                    out=X2v[:, half * 4:half * 4 + 4, sl], in_=x[:, half * 4:half * 4 + 4, :]),
                    reads=[bxx], writes=[bX2], dma=bxx)
        S.end_phase()


def phase_ffn_in(C, cur, layer):
    nc, S, N = C.nc, C.S, C.N
    X, bX = cur
    W = C.W
    name = 'fi'
    with ExitStack() as st:
        wt = st.enter_context(_sbt(nc, 'fi_w', [128, 8, 2 * DFF], BF16))
        bw = Buf('fi_w')
        load_weight_bf16(C, wt, bw, W['ffn_w_in'][layer], D, 0, 2 * DFF)
        nw = st.enter_context(_sbt(nc, 'fi_nw', [128, 8], F32))
        bnw = Buf('fi_nw')
        load_cols(C, nw[:], bnw, W['norm_ffn_w'][layer], 8)
        S.op('dve', lambda e: e.tensor_scalar(nw[:], nw[:], 32.0, None, ALU.mult), reads=[bnw], writes=[bnw])
        xT = [st.enter_context(_sbt(nc, 'fi_x%d' % i, [128, 8, 512], F32)) for i in range(2)]
        bx = [Buf('fi_x%d' % i) for i in range(2)]
        hT = [st.enter_context(_sbt(nc, 'fi_h%d' % i, [128, 8, 512], BF16)) for i in range(2)]
        bh = [Buf('fi_h%d' % i) for i in range(2)]
        sq = st.enter_context(_sbt(nc, 'fi_sq', [128, 8, 512], BF16))
        bsq = Buf('sq')
        rstd = st.enter_context(_sbt(nc, 'fi_rs', [128, 512], F32))
        brs = Buf('rs')
        hid = [st.enter_context(_sbt(nc, 'fi_hid%d' % i, [128, 22, 512], BF16)) for i in range(2)]
        bhid = [Buf('fi_hid%d' % i) for i in range(2)]
        sg = [st.enter_context(_sbt(nc, 'fi_sg%d' % i, [128, 512], F32)) for i in range(2)]
        bsg = [Buf('fi_sg%d' % i) for i in range(2)]
        Xv = X.rearrange("(c p) n -> p c n", p=128)
        Hv = C.H.rearrange("(c p) n -> p c n", p=128)
        for t in range(N // 512):
            x, bxx, h, bhh = xT[t % 2], bx[t % 2], hT[t % 2], bh[t % 2]
            hd, bhd = hid[t % 2], bhid[t % 2]
            sl = slice(t * 512, (t + 1) * 512)
            for half in range(2):
                S.op('sp', lambda e, x=x, sl=sl, half=half: e.dma_start(
                    out=x[:, half * 4:half * 4 + 4, :], in_=Xv[:, half * 4:half * 4 + 4, sl]),
                    reads=[bX], writes=[bxx], dma=bxx)
            rmsnorm_tile(C, x, bxx, h, bhh, nw, bnw, sq, bsq, rstd, brs)
            for fc in range(22):
                psg, bpsg = next_ps(C)
                for kc in range(8):
                    _mm(S, psg[:, :], wt[:, kc, fc * 128:(fc + 1) * 128], h[:, kc, :], kc == 0, kc == 7, [bw, bhh], [bpsg])
                psu, bpsu = next_ps(C)
                for kc in range(8):
                    _mm(S, psu[:, :], wt[:, kc, DFF + fc * 128:DFF + (fc + 1) * 128], h[:, kc, :], kc == 0, kc == 7,
                        [bw, bhh], [bpsu])
                s_, bs_ = sg[fc % 2], bsg[fc % 2]
                S.op('act', lambda e, s_=s_, psg=psg: e.activation(s_[:], psg[:, :], AF.Silu), reads=[bpsg], writes=[bs_])
                S.op('dve', lambda e, s_=s_, psu=psu, hd=hd, fc=fc: e.tensor_tensor(hd[:, fc, :], s_[:], psu[:, :], ALU.mult),
                     reads=[bs_, bpsu], writes=[bhd])
            for (c0, c1) in ((0, 8), (8, 16), (16, 22)):
                S.op('pool', lambda e, hd=hd, sl=sl, c0=c0, c1=c1: e.dma_start(out=Hv[:, c0:c1, sl], in_=hd[:, c0:c1, :]),
                     reads=[bhd], writes=[C.bH], dma=bhd)
        S.end_phase()


def phase_ffn_out(C, cur, nxt, layer, last):
    nc, S, N = C.nc, C.S, C.N
    X, bX = cur
    X2, bX2 = nxt
    W = C.W
    with ExitStack() as st:
        wt = st.enter_context(_sbt(nc, 'fo_w', [128, 22, D], BF16))
        bw = Buf('fo_w')
        load_weight_bf16(C, wt, bw, W['ffn_w_out'][layer], DFF, 0, D)
        xT = [st.enter_context(_sbt(nc, 'fo_x%d' % i, [128, 8, 512], F32)) for i in range(2)]
        bx = [Buf('fo_x%d' % i) for i in range(2)]
        hd = [st.enter_context(_sbt(nc, 'fo_h%d' % i, [128, 22, 512], BF16)) for i in range(2)]
        bhd = [Buf('fo_h%d' % i) for i in range(2)]
        ot = [st.enter_context(_sbt(nc, 'fo_o%d' % i, [128, D], F32)) for i in range(2)]
        bo = [Buf('fo_o%d' % i) for i in range(2)]
        Xv = X.rearrange("(c p) n -> p c n", p=128)
        X2v = X2.rearrange("(c p) n -> p c n", p=128)
        Hv = C.H.rearrange("(c p) n -> p c n", p=128)
        oi = 0
        for t in range(N // 512):
            x, bxx, h, bhh = xT[t % 2], bx[t % 2], hd[t % 2], bhd[t % 2]
            sl = slice(t * 512, (t + 1) * 512)
            for half in range(2):
                S.op('sp', lambda e, x=x, sl=sl, half=half: e.dma_start(
                    out=x[:, half * 4:half * 4 + 4, :], in_=Xv[:, half * 4:half * 4 + 4, sl]),
                    reads=[bX], writes=[bxx], dma=bxx)
            for (c0, c1) in ((0, 8), (8, 16), (16, 22)):
                S.op('sp', lambda e, h=h, sl=sl, c0=c0, c1=c1: e.dma_start(out=h[:, c0:c1, :], in_=Hv[:, c0:c1, sl]),
                     reads=[C.bH], writes=[bhh], dma=bhh)
            for oc in range(8):
                ps, bps = next_ps(C)
                for kc in range(22):
                    _mm(S, ps[:, :], wt[:, kc, oc * 128:(oc + 1) * 128], h[:, kc, :], kc == 0, kc == 21, [bw, bhh], [bps])
                S.op('dve', lambda e, x=x, ps=ps, oc=oc: e.tensor_tensor(x[:, oc, :], x[:, oc, :], ps[:, :], ALU.add),
                     reads=[bps, bxx], writes=[bxx])
            if not last:
                for half in range(2):
                    S.op('pool', lambda e, x=x, sl=sl, half=half: e.dma_start(
                        out=X2v[:, half * 4:half * 4 + 4, sl], in_=x[:, half * 4:half * 4 + 4, :]),
                        reads=[bxx], writes=[bX2], dma=bxx)
            else:
                for s4 in range(4):
                    o, bob = ot[oi % 2], bo[oi % 2]
                    oi += 1
                    for half in range(2):
                        ps, bps = next_ps(C)
                        for j in range(4):
                            c = half * 4 + j
                            _mm(S, ps[:, j * 128:(j + 1) * 128], x[:, c, s4 * 128:(s4 + 1) * 128], C.ident[:], True, True,
                                [bxx, C.bconst], [bps])
                        evac(C, half, o[:, half * 512:(half + 1) * 512], ps[:, :], bps, bob)
                    r0 = t * 512 + s4 * 128
                    S.op('pool', lambda e, o=o, r0=r0: e.dma_start(out=C.yout[r0:r0 + 128, :], in_=o[:]),
                         reads=[bob], writes=[C.bOUT], dma=bob)
        S.end_phase()


def phase_out_only(C, cur):
    raise NotImplementedError


def phase_mix_even(C, e):
    phase_ssd(C, e)
    if C.cfg.get('rwkv', True):
        phase_rwkv(C, e)


def bcast_row(C, dst, vec_ap_1d, n, bdst):
    src = bass.AP(vec_ap_1d.tensor, vec_ap_1d.offset, [[0, 128], [1, n]])
    C.S.op('sp', lambda e: e.dma_start(out=dst, in_=src), writes=[bdst], dma=Buf('bc'))


def phase_ssd(C, e):
    nc, S, N, W = C.nc, C.S, C.N, C.W
    TT = 512
    with ExitStack() as st:
        sb = lambda name, shape, dt=F32: st.enter_context(_sbt(nc, name, list(shape), dt))
        bset = Buf('ss_setup')
        wnat = sb('ss_wnat', [5, 1536])
        S.op('sp', lambda e_: e_.dma_start(out=wnat[:], in_=W['ssd_conv_w'][e]), writes=[bset], dma=Buf('x'))
        wcol = sb('ss_wcol', [128, 12, 5])
        for c in range(12):
            ps, bps = next_ps(C)
            _mm(S, ps[:, 0:5], wnat[0:5, c * 128:(c + 1) * 128], C.ident[0:5, 0:5], True, True, [bset, C.bconst], [bps])
            S.op('dve', lambda e_, c=c, ps=ps: e_.tensor_copy(wcol[:, c, :], ps[:, 0:5]), reads=[bps], writes=[bset])
        dg = sb('ss_dg', [128, 12, 5, 128], BF16)
        for c in range(12):
            for k in range(5):
                S.op('dve', lambda e_, c=c, k=k: e_.tensor_scalar(dg[:, c, k, :], C.identb[:], wcol[:, c, k:k + 1], None, ALU.mult),
                     reads=[bset, C.bconst], writes=[bset])
        cvb = sb('ss_cvb', [128, 12])
        load_cols(C, cvb[:], bset, W['ssd_conv_b'][e], 12)
        nrw = sb('ss_nrw', [128, 8])
        load_cols(C, nrw[:], bset, W['ssd_norm_w'][e], 8)
        dtb = sb('ss_dtb', [128, 32])
        bcast_row(C, dtb[:], W['ssd_dt_bias'][e].rearrange("a b -> (a b)"), 32, bset)
        Ab = sb('ss_Ab', [128, 32])
        bcast_row(C, Ab[:], W['ssd_a_log'][e].rearrange("a b -> (a b)"), 32, bset)
        S.op('act', lambda e_: e_.activation(Ab[:], Ab[:], AF.Exp), reads=[bset], writes=[bset])
        S.op('dve', lambda e_: e_.tensor_scalar(Ab[:], Ab[:], -1.0, None, ALU.mult), reads=[bset], writes=[bset])
        dsk = sb('ss_dsk', [128, 16])
        bcast_row(C, dsk[:], W['ssd_d'][e], 16, bset)
        onesf = sb('ss_onesf', [128, 128])
        S.op('dve', lambda e_: e_.memset(onesf[:], 1.0), writes=[bset])
        tri = sb('ss_tri', [128, 2, 128])
        S.op('sp', lambda e_: e_.dma_start(out=tri[:], in_=C.c_tri), writes=[bset], dma=Buf('x'))
        nmf = sb('ss_nmf', [128, 2, 128])
        S.op('sp', lambda e_: e_.dma_start(out=nmf[:], in_=C.c_nm), writes=[bset], dma=Buf('x'))
        nm4 = sb('ss_nm4', [128, 2, 4, 128], BF16)
        for d in range(2):
            for j in range(4):
                S.op('dve', lambda e_, d=d, j=j: e_.tensor_copy(nm4[:, d, j, :], nmf[:, d, :]), reads=[bset], writes=[bset])

        xin = [sb('ss_xin%d' % i, [128, 12, 516]) for i in range(2)]
        bxin = [Buf('ss_xin%d' % i) for i in range(2)]
        xbf = sb('ss_xbf', [128, 12, 516], BF16)
        bxbf = Buf('ss_xbf')
        xcf = sb('ss_xcf', [128, 8, TT])
        bxcf = Buf('ss_xcf')
        bcf = sb('ss_bcf', [128, 4, TT], BF16)
        bbcf = Buf('ss_bcf')
        xT = sb('ss_xT', [128, 4, 1024])
        bxT = Buf('ss_xT')
        BT = sb('ss_BT', [128, 4, 256], BF16)
        bBT = Buf('ss_BT')
        Sst = sb('ss_S', [128, 1024])
        Sb = sb('ss_Sb', [128, 1024], BF16)
        bS = Buf('ss_S')
        smts = [sb('ss_sm%d' % i, [128, 12, 4, 16]) for i in range(2)]
        bsms = [Buf('ss_sm%d' % i) for i in range(2)]
        R = sb('ss_R', [128, 16, 128])
        bR = Buf('ss_R')
        Dm = sb('ss_Dm', [128, 16, 128], BF16)
        bDm = Buf('ss_Dm')
        cb = sb('ss_cb', [128, 2, 128], BF16)
        bcb = Buf('ss_cb')
        Mt = sb('ss_Mt', [128, 16, 128], BF16)
        bMt = Buf('ss_Mt')
        xdt = sb('ss_xdt', [128, 1024], BF16)
        xdt2 = sb('ss_xdt2', [128, 1024], BF16)
        bxdt = Buf('ss_xdt')
        tmp = sb('ss_tmp', [128, 1024])
        btmp = Buf('ss_tmp')
        yac = [sb('ss_yac%d' % i, [128, 1024]) for i in range(2)]
        byac = [Buf('ss_yac%d' % i) for i in range(2)]
        zt = [sb('ss_z%d' % i, [128, 1024]) for i in range(2)]
        bzt = [Buf('ss_z%d' % i) for i in range(2)]
        dtr = [sb('ss_dtr%d' % i, [128, 4, 32]) for i in range(2)]
        bdtr = [Buf('ss_dtr%d' % i) for i in range(2)]
        ynb = sb('ss_ynb', [128, 1024], BF16)
        bynb = Buf('ss_ynb')
        yo = [sb('ss_yo%d' % i, [128, 8, TT], BF16) for i in range(2)]
        byo = [Buf('ss_yo%d' % i) for i in range(2)]
        Yv = C.Y.rearrange("(c p) n -> p c n", p=128)
        UFv = C.UF.rearrange("(c p) n -> p c n", p=128)
        bYS = C.bYS
        nt = N // TT
        it = 0
        ci = 0
        ti = 0
        for d in range(2):
            order = list(range(nt)) if d == 0 else list(range(nt - 1, -1, -1))
            for t in order:
                t0 = t * TT
                kl, kr = bkind(C, t0), bkind(C, t0 + TT)
                lo = 2 if kl == 'hard' else 0
                hi = 514 if kr == 'hard' else 516
                xi, bxi = xin[it % 2], bxin[it % 2]
                it += 1
                for (c0_, c1_) in ((0, 6), (6, 12)):
                    S.op('sp', lambda e_, xi=xi, c0_=c0_, c1_=c1_, t0=t0, lo=lo, hi=hi: e_.dma_start(
                        out=xi[:, c0_:c1_, lo:hi], in_=UFv[:, c0_:c1_, t0 - 2 + lo:t0 - 2 + hi]),
                        reads=[C.bUF], writes=[bxi], dma=bxi)
                if lo > 0:
                    S.op('pool', lambda e_: e_.memset(xbf[:, :, 0:2], 0.0), writes=[bxbf])
                if hi < 516:
                    S.op('pool', lambda e_: e_.memset(xbf[:, :, 514:516], 0.0), writes=[bxbf])
                S.op('act', lambda e_, xi=xi, lo=lo, hi=hi: e_.copy(xbf[:, :, lo:hi], xi[:, :, lo:hi]), reads=[bxi], writes=[bxbf])
                if kl == 'soft':
                    S.op('dve', lambda e_: e_.tensor_scalar(xbf[:, :, 0:2], xbf[:, :, 0:2], C.flagt[:, 0:1], None, ALU.mult),
                         reads=[bxbf, C.bconst], writes=[bxbf])
                if kr == 'soft':
                    S.op('dve', lambda e_: e_.tensor_scalar(xbf[:, :, 514:516], xbf[:, :, 514:516], C.flagt[:, 0:1], None, ALU.mult),
                         reads=[bxbf, C.bconst], writes=[bxbf])
                for c in range(12):
                    ps, bps = next_ps(C)
                    for k in range(5):
                        _mm(S, ps[:, :], dg[:, c, k, :], xbf[:, c, k:k + TT], k == 0, k == 4, [bset, bxbf], [bps])
                    if c < 8:
                        S.op('act', lambda e_, c=c, ps=ps: e_.activation(xcf[:, c, :], ps[:, :], AF.Silu, bias=cvb[:, c:c + 1], scale=1.0),
                             reads=[bps, bset], writes=[bxcf])
                    else:
                        S.op('act', lambda e_, c=c, ps=ps: e_.activation(bcf[:, c - 8, :], ps[:, :], AF.Silu, bias=cvb[:, c:c + 1], scale=1.0),
                             reads=[bps, bset], writes=[bbcf])
                for s4 in range(4):
                    for half in range(2):
                        ps, bps = next_ps(C)
                        for j in range(4):
                            c = half * 4 + j
                            _mm(S, ps[:, j * 128:(j + 1) * 128], xcf[:, c, s4 * 128:(s4 + 1) * 128], C.ident[:], True, True,
                                [bxcf, C.bconst], [bps])
                        evac(C, half, xT[:, s4, half * 512:(half + 1) * 512], ps[:, :], bps, bxT)
                    ps, bps = next_ps(C)
                    for g in range(2):
                        _mm(S, ps[:, g * 128:(g + 1) * 128], bcf[:, g, s4 * 128:(s4 + 1) * 128], C.identb[:], True, True,
                            [bbcf, C.bconst], [bps])
                    S.op('dve', lambda e_, s4=s4, ps=ps: e_.tensor_copy(BT[:, s4, :], ps[:, 0:256]), reads=[bps], writes=[bBT])
                smt, bsm = smts[ti % 2], bsms[ti % 2]
                dt_, bdt_ = dtr[ti % 2], bdtr[ti % 2]
                ti += 1
                dsl = slice(d * 16, d * 16 + 16)
                S.op('sp', lambda e_, dt_=dt_, t0=t0: e_.dma_start(
                    out=dt_[:], in_=C.UT[t0:t0 + TT, 1024:1056].rearrange("(s p) c -> p s c", p=128)),
                    reads=[C.bUT], writes=[bdt_], dma=bdt_)
                S.op('dve', lambda e_, dt_=dt_, dsl=dsl, smt=smt: e_.tensor_tensor(
                    smt[:, 0, :, :], dt_[:, :, dsl], dtb[:, dsl].unsqueeze(1).to_broadcast([128, 4, 16]), ALU.add),
                    reads=[bdt_, bset], writes=[bsm])
                S.op('act', lambda e_, smt=smt: e_.activation(smt[:, 1, :, :], smt[:, 0, :, :], AF.Abs), reads=[bsm], writes=[bsm])
                S.op('act', lambda e_, smt=smt: e_.activation(smt[:, 1, :, :], smt[:, 1, :, :], AF.Exp, scale=-1.0), reads=[bsm], writes=[bsm])
                S.op('act', lambda e_, smt=smt: e_.activation(smt[:, 1, :, :], smt[:, 1, :, :], AF.Ln, bias=C.onec[:, 0:1], scale=1.0),
                     reads=[bsm, C.bconst], writes=[bsm])
                S.op('dve', lambda e_, smt=smt: e_.scalar_tensor_tensor(smt[:, 2, :, :], smt[:, 0, :, :], 0.0, smt[:, 1, :, :], ALU.max, ALU.add),
                     reads=[bsm], writes=[bsm])
                S.op('dve', lambda e_, dsl=dsl, smt=smt: e_.tensor_tensor(
                    smt[:, 3, :, :], smt[:, 2, :, :], Ab[:, dsl].unsqueeze(1).to_broadcast([128, 4, 16]), ALU.mult),
                    reads=[bsm, bset], writes=[bsm])
                psc, bpsc = next_ps(C)
                for s4 in range(4):
                    _mm(S, psc[:, s4 * 32:s4 * 32 + 16], tri[:, d, :], smt[:, 3, s4, :], True, True, [bset, bsm], [bpsc])
                    _mm(S, psc[:, s4 * 32 + 16:s4 * 32 + 32], onesf[:], smt[:, 3, s4, :], True, True, [bset, bsm], [bpsc])
                pv = psc[:, 0:128].rearrange("p (s a h) -> p s a h", s=4, a=2)
                S.op('dve', lambda e_, pv=pv, smt=smt: e_.tensor_copy(smt[:, 4, :, :], pv[:, :, 0, :]), reads=[bpsc], writes=[bsm])
                S.op('dve', lambda e_, pv=pv, smt=smt: e_.tensor_scalar(smt[:, 5, :, :], pv[:, :, 0, :], -1.0, None, ALU.mult), reads=[bpsc], writes=[bsm])
                S.op('dve', lambda e_, pv=pv, smt=smt: e_.tensor_tensor(smt[:, 7, :, :], pv[:, :, 1, :], smt[:, 4, :, :], ALU.subtract),
                     reads=[bpsc, bsm], writes=[bsm])
                S.op('act', lambda e_, pv=pv, smt=smt: e_.activation(smt[:, 6, :, :], pv[:, :, 0, :], AF.Exp), reads=[bpsc], writes=[bsm])
                S.op('act', lambda e_, pv=pv, smt=smt: e_.activation(smt[:, 9, :, :], pv[:, :, 1, :], AF.Exp), reads=[bpsc], writes=[bsm])
                S.op('act', lambda e_, smt=smt: e_.activation(smt[:, 8, :, :], smt[:, 7, :, :], AF.Exp), reads=[bsm], writes=[bsm])
                S.op('dve', lambda e_, smt=smt: e_.tensor_tensor(smt[:, 8, :, :], smt[:, 8, :, :], smt[:, 2, :, :], ALU.mult), reads=[bsm], writes=[bsm])
                chunks = list(range(4)) if d == 0 else [3, 2, 1, 0]
                yo_, byo_ = yo[t % 2], byo[t % 2]
                for s4 in chunks:
                    c0 = t0 + s4 * 128
                    bpos = c0 if d == 0 else c0 + 128
                    bk = bkind(C, bpos)
                    if bk == 'hard':
                        S.op('pool', lambda e_: e_.memset(Sst[:], 0.0), writes=[bS])
                        S.op('pool', lambda e_: e_.memset(Sb[:], 0.0), writes=[bS])
                    elif bk == 'soft':
                        S.op('dve', lambda e_: e_.tensor_scalar(Sst[:], Sst[:], C.flagt[:, 0:1], None, ALU.mult),
                             reads=[bS, C.bconst], writes=[bS])
                        S.op('dve', lambda e_: e_.tensor_scalar(Sb[:], Sb[:], C.flagt[:, 0:1], None, ALU.mult),
                             reads=[bS, C.bconst], writes=[bS])
                    sm = smt[:, :, s4, :]
                    z_, bz_ = zt[ci % 2], bzt[ci % 2]
                    ya, bya = yac[ci % 2], byac[ci % 2]
                    ci += 1
                    S.op('dve', lambda e_, sm=sm, d=d: e_.tensor_tensor(
                        R[:], tri[:, d:d + 1, :].to_broadcast([128, 16, 128]),
                        sm[:, 3, :].unsqueeze(2).to_broadcast([128, 16, 128]), ALU.mult),
                        reads=[bset, bsm], writes=[bR])
                    pcb, bpcb = next_ps(C)
                    for g in range(2):
                        _mm(S, pcb[:, g * 128:(g + 1) * 128], bcf[:, g, s4 * 128:(s4 + 1) * 128],
                            bcf[:, 2 + g, s4 * 128:(s4 + 1) * 128], True, True, [bbcf], [bpcb])
                    S.op('act', lambda e_, sm=sm, pcb=pcb: e_.copy(cb[:].rearrange("p g l -> p (g l)"), pcb[:, 0:256]), reads=[bpcb], writes=[bcb])
                    for q4 in range(4):
                        ps, bps = next_ps(C)
                        _mm(S, ps[:, :], onesf[:], R[:, q4 * 4:(q4 + 1) * 4, :].rearrange("p h l -> p (h l)"), True, False, [bset, bR], [bps])
                        _mm(S, ps[:, :], C.identb[:], nm4[:, d, :, :].rearrange("p j l -> p (j l)"), False, True, [bset, C.bconst], [bps])
                        for j in range(4):
                            h = q4 * 4 + j
                            S.op('act', lambda e_, sm=sm, ps=ps, j=j, h=h: e_.activation(Dm[:, h, :], ps[:, j * 128:(j + 1) * 128], AF.Exp,
                                                                               bias=sm[:, 5, h:h + 1], scale=1.0),
                                 reads=[bps, bsm], writes=[bDm])
                    for g in range(2):
                        S.op('dve', lambda e_, sm=sm, g=g: e_.tensor_tensor(Mt[:, g * 8:(g + 1) * 8, :], Dm[:, g * 8:(g + 1) * 8, :],
                                                                   cb[:, g:g + 1, :].to_broadcast([128, 8, 128]), ALU.mult),
                             reads=[bDm, bcb], writes=[bMt])
                    xv = xT[:, s4, :].rearrange("p (h q) -> p h q", h=16)
                    S.op('dve', lambda e_, sm=sm, xv=xv: e_.tensor_tensor(xdt[:].rearrange("p (h q) -> p h q", h=16), xv,
                                                                 sm[:, 2, :].unsqueeze(2).to_broadcast([128, 16, 64]), ALU.mult),
                         reads=[bxT, bsm], writes=[bxdt])
                    S.op('dve', lambda e_, sm=sm, xv=xv: e_.tensor_tensor(xdt2[:].rearrange("p (h q) -> p h q", h=16), xv,
                                                                 sm[:, 8, :].unsqueeze(2).to_broadcast([128, 16, 64]), ALU.mult),
                         reads=[bxT, bsm], writes=[bxdt])
                    pof = []
                    for g in range(2):
                        ps, bps = next_ps(C)
                        _mm(S, ps[:, :], bcf[:, 2 + g, s4 * 128:(s4 + 1) * 128], Sb[:, g * 512:(g + 1) * 512], True, True, [bbcf, bS], [bps])
                        pof.append((ps, bps))
                    for g in range(2):
                        ps, bps = pof[g]
                        S.op('dve', lambda e_, sm=sm, g=g, ps=ps: e_.tensor_tensor(
                            tmp[:, g * 512:(g + 1) * 512].rearrange("p (h q) -> p h q", h=8),
                            ps[:, :].rearrange("p (h q) -> p h q", h=8),
                            sm[:, 6, g * 8:(g + 1) * 8].unsqueeze(2).to_broadcast([128, 8, 64]), ALU.mult),
                            reads=[bps, bsm], writes=[btmp])
                    pyd = []
                    for g in range(2):
                        ps, bps = next_ps(C)
                        for j in range(8):
                            h = g * 8 + j
                            _mm(S, ps[:, j * 64:(j + 1) * 64], Mt[:, h, :], xdt[:, h * 64:(h + 1) * 64], True, True, [bMt, bxdt], [bps])
                        pyd.append((ps, bps))
                    if d == 0:
                        for g in range(2):
                            ps, bps = pyd[g]
                            S.op('dve', lambda e_, sm=sm, g=g, ps=ps, ya=ya: e_.tensor_tensor(ya[:, g * 512:(g + 1) * 512], ps[:, :],
                                                                                   tmp[:, g * 512:(g + 1) * 512], ALU.add),
                                 reads=[bps, btmp], writes=[bya])
                        S.op('dve', lambda e_, sm=sm, xv=xv: e_.tensor_tensor(tmp[:].rearrange("p (h q) -> p h q", h=16), xv,
                                                                     dsk[:, :].unsqueeze(2).to_broadcast([128, 16, 64]), ALU.mult),
                             reads=[bxT, bset], writes=[btmp])
                        S.op('dve', lambda e_, sm=sm, ya=ya: e_.tensor_tensor(ya[:], ya[:], tmp[:], ALU.add), reads=[bya, btmp], writes=[bya])
                        S.op('pool', lambda e_, sm=sm, ya=ya, c0=c0: e_.dma_start(out=C.YS[c0:c0 + 128, :], in_=ya[:]),
                             reads=[bya], writes=[bYS], dma=bya)
                    else:
                        S.op('sp', lambda e_, sm=sm, ya=ya, c0=c0: e_.dma_start(out=ya[:], in_=C.YS[c0:c0 + 128, :]),
                             reads=[bYS], writes=[bya], dma=bya)
                        S.op('sp', lambda e_, sm=sm, z_=z_, c0=c0: e_.dma_start(out=z_[:], in_=C.UT[c0:c0 + 128, 0:1024]),
                             reads=[C.bUT], writes=[bz_], dma=bz_)
                        S.op('dve', lambda e_, sm=sm, ya=ya: e_.tensor_tensor(ya[:], ya[:], tmp[:], ALU.add), reads=[bya, btmp], writes=[bya])
                        for g in range(2):
                            ps, bps = pyd[g]
                            S.op('dve', lambda e_, sm=sm, g=g, ps=ps, ya=ya: e_.tensor_tensor(ya[:, g * 512:(g + 1) * 512], ps[:, :],
                                                                                   ya[:, g * 512:(g + 1) * 512], ALU.add),
                                 reads=[bps, bya], writes=[bya])
                        S.op('act', lambda e_, sm=sm, z_=z_: e_.activation(z_[:], z_[:], AF.Silu), reads=[bz_], writes=[bz_])
                        S.op('dve', lambda e_, sm=sm, ya=ya, z_=z_: e_.tensor_tensor(ya[:], ya[:], z_[:], ALU.mult), reads=[bya, bz_], writes=[bya])
                        S.op('act', lambda e_, sm=sm, ya=ya: e_.activation(tmp[:], ya[:], AF.Square), reads=[bya], writes=[btmp])
                        S.op('dve', lambda e_, sm=sm: e_.reduce_sum(sm[:, 10, 0:2], tmp[:].rearrange("p (g f) -> p g f", g=2), AX.X),
                             reads=[btmp], writes=[bsm])
                        S.op('act', lambda e_, sm=sm: e_.activation(sm[:, 10, 0:2], sm[:, 10, 0:2], AF.Sqrt, bias=C.epsD[:, 1:2], scale=1.0 / 512),
                             reads=[bsm, C.bconst], writes=[bsm])
                        S.op('dve', lambda e_, sm=sm: e_.reciprocal(sm[:, 10, 0:2], sm[:, 10, 0:2]), reads=[bsm], writes=[bsm])
                        S.op('dve', lambda e_, sm=sm, ya=ya: e_.tensor_tensor(ynb[:].rearrange("p (g f) -> p g f", g=2),
                                                                     ya[:].rearrange("p (g f) -> p g f", g=2),
                                                                     sm[:, 10, 0:2].unsqueeze(2).to_broadcast([128, 2, 512]), ALU.mult),
                             reads=[bya, bsm], writes=[bynb])
                        for half in range(2):
                            ps, bps = next_ps(C)
                            for j in range(4):
                                c = half * 4 + j
                                _mm(S, ps[:, j * 128:(j + 1) * 128], ynb[:, c * 128:(c + 1) * 128], C.identb[:], True, True,
                                    [bynb, C.bconst], [bps])
                            S.op('dve', lambda e_, sm=sm, half=half, ps=ps, yo_=yo_, s4=s4: e_.tensor_tensor(
                                yo_[:, half * 4:(half + 1) * 4, s4 * 128:(s4 + 1) * 128],
                                ps[:, :].rearrange("p (j l) -> p j l", j=4),
                                nrw[:, half * 4:(half + 1) * 4].unsqueeze(2).to_broadcast([128, 4, 128]), ALU.mult),
                                reads=[bps, bset], writes=[byo_])
                    for g in range(2):
                        ps, bps = next_ps(C)
                        _mm(S, ps[:, :], BT[:, s4, g * 128:(g + 1) * 128], xdt2[:, g * 512:(g + 1) * 512], True, True, [bBT, bxdt], [bps])
                        S.op('dve', lambda e_, sm=sm, g=g: e_.tensor_tensor(
                            Sst[:, g * 512:(g + 1) * 512].rearrange("p (h q) -> p h q", h=8),
                            Sst[:, g * 512:(g + 1) * 512].rearrange("p (h q) -> p h q", h=8),
                            sm[:, 9, g * 8:(g + 1) * 8].unsqueeze(2).to_broadcast([128, 8, 64]), ALU.mult),
                            reads=[bS, bsm], writes=[bS])
                        S.op('dve', lambda e_, sm=sm, g=g, ps=ps: e_.tensor_tensor(Sst[:, g * 512:(g + 1) * 512], Sst[:, g * 512:(g + 1) * 512],
                                                                         ps[:, :], ALU.add),
                             reads=[bS, bps], writes=[bS])
                    S.op('act', lambda e_, sm=sm: e_.copy(Sb[:], Sst[:]), reads=[bS], writes=[bS])
                if d == 1:
                    S.op('pool', lambda e_, sm=sm, yo_=yo_, t0=t0: e_.dma_start(out=Yv[:, 0:8, t0:t0 + TT], in_=yo_[:]),
                         reads=[byo_], writes=[C.bY], dma=byo_)
        S.end_phase()


def phase_rwkv(C, e):
    nc, S, N, W = C.nc, C.S, C.N, C.W
    TT = 512
    KD = 0.6065306597126334
    stage = C.cfg.get('rw_stage', 99)
    with ExitStack() as st:
        sb = lambda name, shape, dt=F32: st.enter_context(_sbt(nc, name, list(shape), dt))
        bset = Buf('rw_setup')
        mu = sb('rw_mu', [128, 26])
        load_cols(C, mu[:], bset, W['rwkv_mu'][e], 26)
        hm = sb('rw_hm', [128, 2, 26])
        S.op('dve', lambda e_: e_.tensor_scalar(hm[:, 0, :], mu[:], 0.5, None, ALU.mult), reads=[bset], writes=[bset])
        S.op('dve', lambda e_: e_.tensor_scalar(hm[:, 1, :], mu[:], -1.0, 1.0, ALU.mult, ALU.add), reads=[bset], writes=[bset])
        dgs = sb('rw_dgs', [128, 26, 2, 128], BF16)
        for c in range(26):
            for k in range(2):
                S.op('dve', lambda e_, c=c, k=k: e_.tensor_scalar(dgs[:, c, k, :], C.identb[:], hm[:, k, c:c + 1], None, ALU.mult),
                     reads=[bset, C.bconst], writes=[bset])
        w2b = sb('rw_w2b', [128, 2, 1024], BF16)
        a2p = sb('rw_a2p', [128, 1024], BF16)
        S.op('pool', lambda e_: e_.dma_start(out=a2p[64:128, :], in_=W['rwkv_a2'][e]), writes=[bset], dma=Buf('x'))
        g2b = sb('rw_g2b', [128, 1024], BF16)
        S.op('pool', lambda e_: e_.dma_start(out=g2b[:, :], in_=W['rwkv_g2'][e]), writes=[bset], dma=Buf('x'))
        cols = sb('rw_cols', [128, 7, 8])
        load_cols(C, cols[:, 0, :], bset, W['rwkv_a0'][e], 8)
        load_cols(C, cols[:, 1, :], bset, W['rwkv_k_k'][e], 8)
        load_cols(C, cols[:, 2, :], bset, W['rwkv_k_a'][e], 8)
        load_cols(C, cols[:, 4, :], bset, W['rwkv_r_k'][e].rearrange("a b -> (a b)"), 8)
        load_cols(C, cols[:, 5, :], bset, W['rwkv_ln_w'][e], 8)
        load_cols(C, cols[:, 6, :], bset, W['rwkv_ln_b'][e], 8)
        S.op('dve', lambda e_: e_.tensor_scalar(cols[:, 3, :], cols[:, 2, :], -1.0, None, ALU.mult), reads=[bset], writes=[bset])
        tri = sb('rw_tri', [128, 2, 128])
        S.op('sp', lambda e_: e_.dma_start(out=tri[:], in_=C.c_tri), writes=[bset], dma=Buf('x'))
        tri2 = sb('rw_tri2', [128, 2, 128])
        S.op('sp', lambda e_: e_.dma_start(out=tri2[:], in_=C.c_tri2), writes=[bset], dma=Buf('x'))
        m4 = sb('rw_m4', [128, 2, 4, 128], BF16)
        for d in range(2):
            for j in range(4):
                srcm = tri2 if j % 2 == 0 else tri
                S.op('dve', lambda e_, d=d, j=j, srcm=srcm: e_.tensor_copy(m4[:, d, j, :], srcm[:, d, :]), reads=[bset], writes=[bset])
        bd = sb('rw_bd', [128, 128])
        S.op('sp', lambda e_: e_.dma_start(out=bd[:], in_=C.c_bd), writes=[bset], dma=Buf('x'))

        st0 = ExitStack()
        w0f = st0.enter_context(_sbt(nc, 'rw_w0f', [128, 2, 2, 1024], F32))
        for d in range(2):
            S.op('pool', lambda e_, d=d: e_.dma_start(out=w2b[0:64, d, :], in_=W['rwkv_w2'][e][d]), writes=[bset], dma=Buf('x'))
            S.op('sp', lambda e_, d=d: e_.dma_start(out=w0f[64:65, d, 0, :], in_=W['rwkv_w0'][e][d:d + 1, :]), writes=[bset], dma=Buf('x'))
            S.op('sp', lambda e_, d=d: e_.dma_start(out=w0f[65:66, d, 0, :], in_=W['rwkv_w0'][e][d:d + 1, :]), writes=[bset], dma=Buf('x'))
        tmpb = st0.enter_context(_sbt(nc, 'rw_tmpb', [128, 2, 1024], BF16))
        S.op('dve', lambda e_: e_.tensor_copy(tmpb[64:66, :, :], w0f[64:66, :, 0, :]), reads=[bset], writes=[bset])
        S.op('dve', lambda e_: e_.tensor_copy(w2b[64:66, :, :], tmpb[64:66, :, :]), reads=[bset], writes=[bset])
        S.op('dve', lambda e_: e_.tensor_copy(w0f[64:66, :, 1, :], tmpb[64:66, :, :]), reads=[bset], writes=[bset])
        S.op('dve', lambda e_: e_.tensor_tensor(w0f[64:66, :, 0, :], w0f[64:66, :, 0, :], w0f[64:66, :, 1, :], ALU.subtract),
             reads=[bset], writes=[bset])
        S.op('dve', lambda e_: e_.tensor_copy(tmpb[64:66, :, :], w0f[64:66, :, 0, :]), reads=[bset], writes=[bset])
        S.op('sp', lambda e_: e_.dma_start(out=w2b[65:66, :, :], in_=tmpb[65:66, :, :]), reads=[bset], writes=[bset], dma=Buf('x'))
        S.barrier()
        S.flush()
        st0.close()
        NPB = 6
        pbb = [sb('rw_pb%d' % i, [128, 514], BF16) for i in range(NPB)]
        bpbb = [Buf('rw_pb%d' % i) for i in range(NPB)]
        tcw = sb('rw_tcw', [128, TT], BF16)
        btcw = Buf('rw_tcw')
        S.op('pool', lambda e_: e_.memset(tcw[64:66, :], 1.0), writes=[btcw])
        cab = sb('rw_cab', [128, TT], BF16)
        bcab = Buf('rw_cab')
        scg = sb('rw_scg', [128, TT], BF16)
        bscg = Buf('rw_scg')
        tmpn = ['al', 'kf', 'kkr', 'sq', 'kk32', 't1', 'bon']
        tm = {n: sb('rw_t_' + n, [128, TT]) for n in tmpn}
        btm = {n: Buf('rw_t_' + n) for n in tmpn}
        nkT = sb('rw_nkT', [128, 8, TT], BF16)
        bT = sb('rw_bT', [128, 8, TT], BF16)
        kmT = sb('rw_kmT', [128, 8, TT], BF16)
        rTb = sb('rw_rTb', [128, 8, TT], BF16)
        vTb = sb('rw_vTb', [128, 8, TT], BF16)
        bvT = sb('rw_bv', [128, 8, TT], BF16)
        gT = sb('rw_gT', [128, 8, TT], BF16)
        bprep = Buf('rw_prep')
        Vt = sb('rw_Vt', [128, 1024], BF16)
        bVt = Buf('rw_Vt')
        sig = sb('rw_sig', [128, 1024])
        bsig = Buf('rw_sig')
        eLi = sb('rw_eLi', [128, 8, 128])
        eLn = sb('rw_eLn', [128, 8, 128])
        eLx = sb('rw_eLx', [128, 8, 128])
        beL = Buf('rw_eL')
        AR = sb('rw_AR', [128, 8, 2, 128], BF16)
        BK = sb('rw_BK', [128, 2, 8, 128], BF16)
        bARBK = Buf('rw_ARBK')
        bkTp = sb('rw_bkTp', [128, 2, 2, 8, 128], BF16)
        bbkT = Buf('rw_bkT')
        S.op('pool', lambda e_: e_.memset(bkTp[:], 0.0), writes=[bbkT])
        AB4 = sb('rw_AB4', [128, 16, 512], BF16)
        bAB4 = Buf('rw_AB4')
        Xm = sb('rw_X', [128, 16, 128], BF16)
        bXm = [Buf('rw_X%d' % i) for i in range(4)]
        Pp = [sb('rw_P%d' % i, [128, 2, 16, 128], BF16) for i in range(2)]
        bPp = [[Buf('rw_P%d_%d' % (i, g)) for g in range(4)] for i in range(2)]
        Gs = sb('rw_Gs', [128, 1024], BF16)
        bGs = Buf('rw_Gs')
        Us = sb('rw_Us', [128, 1024], BF16)
        bUs = Buf('rw_Us')
        Sst = sb('rw_S', [128, 8, 64])
        Sbb = sb('rw_Sb', [128, 8, 64], BF16)
        bS = Buf('rw_S')
        ya = sb('rw_ya', [128, 1024])
        bya = Buf('rw_ya')
        yfl = sb('rw_yf', [128, 1024])
        byfl = Buf('rw_yf')
        stt = sb('rw_st', [128, 2, 16])
        bstt = Buf('rw_st')
        ynb = sb('rw_ynb', [128, 1024], BF16)
        bynb = Buf('rw_ynb')
        fin = sb('rw_fin', [128, 8, 128])
        bfin = Buf('rw_fin')
        yo = sb('rw_yo', [128, 8, TT], BF16)
        byo = Buf('rw_yo')
        Yv = C.Y.rearrange("(c p) n -> p c n", p=128)
        UFv = C.UF.rearrange("(c p) n -> p c n", p=128)
        nt = N // TT
        cnt = {'px': 0, 'pb': 0, 'ev': 0}

        UFbv = C.UFb.rearrange("(c p) n -> p c n", p=128)

        def conv_chunk(ch, t0, kl, kr, lo, hi):
            pb, bpb = pbb[cnt['pb'] % NPB], bpbb[cnt['pb'] % NPB]
            cnt['pb'] += 1
            S.op('sp', lambda e_: e_.dma_start(out=pb[:, lo:hi], in_=UFbv[:, ch, t0 - 1 + lo:t0 - 1 + hi]),
                 reads=[C.bUF], writes=[bpb], dma=bpb)
            if lo > 0:
                S.op('pool', lambda e_: e_.memset(pb[:, 0:1], 0.0), writes=[bpb])
            if hi < 514:
                S.op('pool', lambda e_: e_.memset(pb[:, 513:514], 0.0), writes=[bpb])
            if kl == 'soft':
                S.op('pool', lambda e_: e_.tensor_scalar(pb[:, 0:1], pb[:, 0:1], C.flagt[:, 0:1], None, ALU.mult),
                     reads=[bpb, C.bconst], writes=[bpb])
            if kr == 'soft':
                S.op('pool', lambda e_: e_.tensor_scalar(pb[:, 513:514], pb[:, 513:514], C.flagt[:, 0:1], None, ALU.mult),
                     reads=[bpb, C.bconst], writes=[bpb])
            ps, bps = next_ps(C)
            _mm(S, ps[:, :], dgs[:, ch, 0, :], pb[:, 0:TT], True, False, [bset, bpb], [bps])
            _mm(S, ps[:, :], dgs[:, ch, 1, :], pb[:, 1:TT + 1], False, False, [bset, bpb], [bps])
            _mm(S, ps[:, :], dgs[:, ch, 0, :], pb[:, 2:TT + 2], False, True, [bset, bpb], [bps])
            return ps, bps

        for d in range(2):
            if stage < 1:
                break
            order = list(range(nt)) if d == 0 else list(range(nt - 1, -1, -1))
            last = 127 if d == 0 else 0
            for t in order:
                t0 = t * TT
                kl, kr = bkind(C, t0), bkind(C, t0 + TT)
                lo = 1 if kl == 'hard' else 0
                hi = 513 if kr == 'hard' else 514
                ps, bps = conv_chunk(24, t0, kl, kr, lo, hi)
                S.op('act', lambda e_, ps=ps: e_.activation(tcw[0:64, :], ps[0:64, :], AF.Tanh), reads=[bps], writes=[btcw])
                S.op('dve', lambda e_, ps=ps: e_.tensor_copy(cab[64:128, :], ps[64:128, :]), reads=[bps], writes=[bcab])
                if d == 1:
                    ps, bps = conv_chunk(25, t0, kl, kr, lo, hi)
                    S.op('act', lambda e_, ps=ps: e_.activation(scg[:], ps[:, :], AF.Sigmoid), reads=[bps], writes=[bscg])
                    for c in range(8):
                        ps, bps = next_ps(C)
                        _mm(S, ps[:, :], g2b[:, c * 128:(c + 1) * 128], scg[:], True, True, [bset, bscg], [bps])
                        evac(C, c, gT[:, c, :], ps[:, :], bps, bprep)
                for c in range(8):
                    psa, bpsa = next_ps(C)
                    _mm(S, psa[:, :], a2p[64:128, c * 128:(c + 1) * 128], cab[64:128, :], True, True, [bset, bcab], [bpsa])
                    psk, bpsk = conv_chunk(8 + c, t0, kl, kr, lo, hi)
                    psr, bpsr = conv_chunk(c, t0, kl, kr, lo, hi)
                    psv, bpsv = conv_chunk(16 + c, t0, kl, kr, lo, hi)
                    S.op('act', lambda e_, c=c, psa=psa: e_.activation(tm['al'][:], psa[:, :], AF.Sigmoid, bias=cols[:, 0, c:c + 1], scale=1.0),
                         reads=[bpsa, bset], writes=[btm['al']])
                    S.op('dve', lambda e_, c=c, psk=psk: e_.tensor_scalar(tm['kkr'][:], psk[:, :], cols[:, 1, c:c + 1], None, ALU.mult),
                         reads=[bpsk, bset], writes=[btm['kkr']])
                    S.op('act', lambda e_: e_.activation(tm['sq'][:], tm['kkr'][:], AF.Square), reads=[btm['kkr']], writes=[btm['sq']])
                    ps2, bps2 = next_ps(C)
                    _mm(S, ps2[:, :], bd[:], tm['sq'][:], True, True, [bset, btm['sq']], [bps2])
                    S.op('act', lambda e_, ps2=ps2: e_.activation(tm['sq'][:], ps2[:, :], AF.Sqrt), reads=[bps2], writes=[btm['sq']])
                    S.op('dve', lambda e_: e_.tensor_scalar(tm['sq'][:], tm['sq'][:], 1e-12, None, ALU.max), reads=[btm['sq']], writes=[btm['sq']])
                    S.op('dve', lambda e_: e_.reciprocal(tm['sq'][:], tm['sq'][:]), reads=[btm['sq']], writes=[btm['sq']])
                    S.op('dve', lambda e_: e_.tensor_tensor(tm['kk32'][:], tm['kkr'][:], tm['sq'][:], ALU.mult),
                         reads=[btm['kkr'], btm['sq']], writes=[btm['kk32']])
                    S.op('act', lambda e_, c=c: e_.activation(nkT[:, c, :], tm['kk32'][:], AF.Identity, scale=-1.0), reads=[btm['kk32']], writes=[bprep])
                    S.op('dve', lambda e_, c=c: e_.tensor_tensor(bT[:, c, :], tm['kk32'][:], tm['al'][:], ALU.mult),
                         reads=[btm['kk32'], btm['al']], writes=[bprep])
                    S.op('dve', lambda e_: e_.tensor_scalar(tm['t1'][:], tm['al'][:], -1.0, None, ALU.add),
                         reads=[btm['al']], writes=[btm['t1']])
                    S.op('dve', lambda e_, c=c: e_.tensor_scalar(tm['t1'][:], tm['t1'][:], cols[:, 2, c:c + 1], None, ALU.mult),
                         reads=[btm['t1'], bset], writes=[btm['t1']])
                    S.op('dve', lambda e_, psk=psk: e_.scalar_tensor_tensor(tm['t1'][:], tm['t1'][:], 1.0, psk[:, :], ALU.add, ALU.mult),
                         reads=[btm['t1'], bpsk], writes=[btm['t1']])
                    S.op('act', lambda e_, c=c: e_.copy(kmT[:, c, :], tm['t1'][:]), reads=[btm['t1']], writes=[bprep])
                    S.op('act', lambda e_, c=c, psr=psr: e_.copy(rTb[:, c, :], psr[:, :]), reads=[bpsr], writes=[bprep])
                    if d == 1:
                        S.op('dve', lambda e_, c=c, psr=psr: e_.scalar_tensor_tensor(tm['kk32'][:], psr[:, :], cols[:, 4, c:c + 1], tm['t1'][:],
                                                                                  ALU.mult, ALU.mult),
                             reads=[bpsr, bset, btm['t1']], writes=[btm['kk32']])
                        ps3, bps3 = next_ps(C)
                        _mm(S, ps3[:, :], bd[:], tm['kk32'][:], True, True, [bset, btm['kk32']], [bps3])
                        S.op('act', lambda e_, ps3=ps3: e_.copy(tm['bon'][:], ps3[:, :]), reads=[bps3], writes=[btm['bon']])
                    S.op('act', lambda e_, c=c, psv=psv: e_.copy(vTb[:, c, :], psv[:, :]), reads=[bpsv], writes=[bprep])
                    if d == 1:
                        S.op('dve', lambda e_, c=c, psv=psv: e_.tensor_tensor(bvT[:, c, :], psv[:, :], tm['bon'][:], ALU.mult),
                             reads=[bpsv, btm['bon']], writes=[bprep])
                chunks = list(range(4)) if d == 0 else [3, 2, 1, 0]
                if stage < 2:
                    chunks = []
                for s4 in chunks:
                    c0 = t0 + s4 * 128
                    ts = slice(s4 * 128, (s4 + 1) * 128)
                    bpos = c0 if d == 0 else c0 + 128
                    bk = bkind(C, bpos)
                    if bk == 'hard':
                        S.op('pool', lambda e_: e_.memset(Sst[:], 0.0), writes=[bS])
                        S.op('pool', lambda e_: e_.memset(Sbb[:], 0.0), writes=[bS])
                    elif bk == 'soft':
                        S.op('dve', lambda e_: e_.tensor_scalar(Sst[:], Sst[:], C.flagt[:, 0:1], None, ALU.mult),
                             reads=[bS, C.bconst], writes=[bS])
                        S.op('dve', lambda e_: e_.tensor_scalar(Sbb[:], Sbb[:], C.flagt[:, 0:1], None, ALU.mult),
                             reads=[bS, C.bconst], writes=[bS])
                    if d == 1:
                        S.op('sp', lambda e_, c0=c0: e_.dma_start(out=yfl[:], in_=C.YS[c0:c0 + 128, :]),
                             reads=[C.bYS], writes=[byfl], dma=byfl)
                    for half in range(2):
                        ps, bps = next_ps(C)
                        for j in range(4):
                            c = half * 4 + j
                            _mm(S, ps[:, j * 128:(j + 1) * 128], vTb[:, c, ts], C.identb[:], True, True, [bprep, C.bconst], [bps])
                        evac(C, half, Vt[:, half * 512:(half + 1) * 512], ps[:, :], bps, bVt)
                    for half in range(2):
                        ps, bps = next_ps(C)
                        _mm(S, ps[:, :], tcw[0:66, ts], w2b[0:66, d, half * 512:(half + 1) * 512], True, True, [btcw, bset], [bps])
                        S.op('act', lambda e_, half=half, ps=ps: e_.activation(sig[:, half * 512:(half + 1) * 512], ps[:, :], AF.Sigmoid),
                             reads=[bps], writes=[bsig])
                    if stage < 3:
                        continue
                    pli = []
                    for half in range(2):
                        ps, bps = next_ps(C)
                        for j in range(4):
                            c = half * 4 + j
                            _mm(S, ps[:, j * 128:(j + 1) * 128], sig[:, c * 128:(c + 1) * 128], tri[:, d, :], True, True, [bsig, bset], [bps])
                        pli.append((ps, bps))
                    plx = []
                    for half in range(2):
                        ps, bps = next_ps(C)
                        for j in range(4):
                            c = half * 4 + j
                            _mm(S, ps[:, j * 128:(j + 1) * 128], sig[:, c * 128:(c + 1) * 128], tri2[:, d, :], True, True, [bsig, bset], [bps])
                        plx.append((ps, bps))
                    for half in range(2):
                        ps, bps = pli[half]
                        dst = lambda tl: tl[:, half * 4:(half + 1) * 4, :].rearrange("p c l -> p (c l)")
                        S.op('act', lambda e_, ps=ps, o=dst(eLi): e_.activation(o, ps[:, :], AF.Exp, scale=-KD), reads=[bps], writes=[beL])
                        S.op('act', lambda e_, ps=ps, o=dst(eLn): e_.activation(o, ps[:, :], AF.Exp, scale=KD), reads=[bps], writes=[beL])
                        ps, bps = plx[half]
                        S.op('act', lambda e_, ps=ps, o=dst(eLx): e_.activation(o, ps[:, :], AF.Exp, scale=-KD), reads=[bps], writes=[beL])
                    if stage < 4:
                        continue
                    S.op('dve', lambda e_, ts=ts: e_.tensor_tensor(AR[:, :, 0, :], nkT[:, :, ts], eLx[:], ALU.mult), reads=[bprep, beL], writes=[bARBK])
                    S.op('dve', lambda e_, ts=ts: e_.tensor_tensor(AR[:, :, 1, :], rTb[:, :, ts], eLi[:], ALU.mult), reads=[bprep, beL], writes=[bARBK])
                    S.op('dve', lambda e_, ts=ts: e_.tensor_tensor(BK[:, 0, :, :], bT[:, :, ts], eLn[:], ALU.mult), reads=[bprep, beL], writes=[bARBK])
                    S.op('dve', lambda e_, ts=ts: e_.tensor_tensor(BK[:, 1, :, :], kmT[:, :, ts], eLn[:], ALU.mult), reads=[bprep, beL], writes=[bARBK])
                    if stage < 5:
                        continue
                    for q in range(2):
                        for half in range(2):
                            ps, bps = next_ps(C)
                            for j in range(4):
                                c = half * 4 + j
                                _mm(S, ps[:, j * 128:(j + 1) * 128], BK[:, q, c, :], C.identb[:], True, True, [bARBK, C.bconst], [bps])
                            pv = ps[:, :].rearrange("p (c a j) -> p c a j", c=4, a=2)
                            S.op('act', lambda e_, q=q, half=half, pv=pv: e_.copy(bkTp[:, 0, q, half * 4:(half + 1) * 4, 0:64], pv[:, :, 0, :]),
                                 reads=[bps], writes=[bbkT])
                            S.op('dve', lambda e_, q=q, half=half, pv=pv: e_.tensor_copy(bkTp[:, 1, q, half * 4:(half + 1) * 4, 64:128], pv[:, :, 1, :]),
                                 reads=[bps], writes=[bbkT])
                    if stage < 5.2:
                        continue
                    for h in range(16):
                        c, pb = h // 2, (h % 2) * 64
                        sl_ = (h % 2) * 8 + h // 2
                        ps, bps = next_ps(C)
                        arv = AR[pb:pb + 64, c, :, :].rearrange("p a l -> p (a l)")
                        _mm(S, ps[:, 0:256], BK[pb:pb + 64, 0, c, :], arv, True, True, [bARBK], [bps])
                        _mm(S, ps[:, 256:512], BK[pb:pb + 64, 1, c, :], arv, True, True, [bARBK], [bps])
                        mv = m4[:, d, :, :].rearrange("p a l -> p (a l)")
                        if h % 2 == 0:
                            S.op('dve', lambda e_, h=sl_, ps=ps, mv=mv: e_.tensor_tensor(AB4[:, h, :], ps[:, :], mv, ALU.mult),
                                 reads=[bps, bset], writes=[bAB4])
                        else:
                            S.op('act', lambda e_, h=sl_, ps=ps: e_.copy(AB4[:, h, :], ps[:, :]), reads=[bps], writes=[bAB4])
                            S.op('dve', lambda e_, h=sl_, mv=mv: e_.tensor_tensor(AB4[:, h, :], AB4[:, h, :], mv, ALU.mult),
                                 reads=[bAB4, bset], writes=[bAB4])
                    if stage < 5.5:
                        continue
                    P0, bP0 = Pp[0], bPp[0]
                    for hg in range(4):
                        ps, bps = next_ps(C)
                        for j in range(4):
                            slot = hg * 4 + j
                            c, pb = slot % 8, (slot // 8) * 64
                            _mm(S, ps[:, j * 128:(j + 1) * 128], AR[pb:pb + 64, c, 0, :], BK[pb:pb + 64, 0, c, :], True, True, [bARBK], [bps])
                        S.op('dve', lambda e_, hg=hg, ps=ps, d=d: e_.tensor_tensor(
                            P0[:, 1, hg * 4:(hg + 1) * 4, :], ps[:, :].rearrange("p (j l) -> p j l", j=4),
                            tri2[:, 1 - d:2 - d, :].to_broadcast([128, 4, 128]), ALU.mult),
                            reads=[bps, bset], writes=[bP0[hg]])
                        if stage < 5.6:
                            continue
                        S.op('dve', lambda e_, hg=hg: e_.tensor_copy(P0[:, 0, hg * 4:(hg + 1) * 4, :], AB4[:, hg * 4:(hg + 1) * 4, 0:128]),
                             reads=[bAB4], writes=[bP0[hg]])
                        if stage < 5.7:
                            continue
                        S.op('dve', lambda e_, hg=hg: e_.tensor_tensor(Xm[:, hg * 4:(hg + 1) * 4, :], AB4[:, hg * 4:(hg + 1) * 4, 0:128],
                                                                      C.identb[:, :].unsqueeze(1).to_broadcast([128, 4, 128]), ALU.add),
                             reads=[bAB4, C.bconst], writes=[bXm[hg]])
                    if stage < 7:
                        continue
                    dbg_now = C.cfg.get('dbg') and d == 0 and t == 0 and s4 == 0
                    def dump2(idx, tile_ap, n, rd):
                        S.op('pool', lambda e_: e_.dma_start(out=C.DBG[idx, :, 0:n], in_=tile_ap), reads=rd, writes=[Buf('dbg2')], dma=Buf('x'))
                    if dbg_now:
                        dump2(13, Pp[0][:].rearrange("p a b c -> p (a b c)"), 4096, bPp[0] + bXm)
                        dump2(14, Xm[:].rearrange("p a b -> p (a b)"), 2048, bXm)
                    for k in range(1, 7):
                        if dbg_now and k == 2:
                            dump2(15, Pp[1][:].rearrange("p a b c -> p (a b c)"), 4096, bPp[1] + bXm)
                            dump2(16, Xm[:].rearrange("p a b -> p (a b)"), 2048, bXm)
                        Pa, bPa = Pp[(k - 1) % 2], bPp[(k - 1) % 2]
                        Pn, bPn = Pp[k % 2], bPp[k % 2]
                        for hg in range(4):
                            if k < 6:
                                ps, bps = next_ps(C)
                                for j in range(4):
                                    h = hg * 4 + j
                                    _mm(S, ps[:, j * 128:(j + 1) * 128], Pa[:, 1, h, :], Pa[:, 0, h, :], True, True, [bPa[hg]], [bps])
                                S.op('act', lambda e_, hg=hg, ps=ps, Pn=Pn: e_.copy(
                                    Pn[:, 0, hg * 4:(hg + 1) * 4, :].rearrange("p j l -> p (j l)"), ps[:, :]), reads=[bps], writes=[bPn[hg]])
                            ps, bps = next_ps(C)
                            for j in range(4):
                                h = hg * 4 + j
                                _mm(S, ps[:, j * 128:(j + 1) * 128], Pa[:, 0, h, :], Pa[:, 1, h, :], True, True, [bPa[hg]], [bps])
                            if hg % 2 == 1:
                                S.op('act', lambda e_, hg=hg, ps=ps, Pn=Pn: e_.copy(
                                    Pn[:, 1, hg * 4:(hg + 1) * 4, :].rearrange("p j l -> p (j l)"), ps[:, :]), reads=[bps], writes=[bPn[hg]])
                            else:
                                S.op('dve', lambda e_, hg=hg, ps=ps, Pn=Pn: e_.tensor_copy(
                                    Pn[:, 1, hg * 4:(hg + 1) * 4, :].rearrange("p j l -> p (j l)"), ps[:, :]), reads=[bps], writes=[bPn[hg]])
                        for hg in range(4):
                            ps, bps = next_ps(C)
                            for j in range(4):
                                h = hg * 4 + j
                                _mm(S, ps[:, j * 128:(j + 1) * 128], Pn[:, 1, h, :], Xm[:, h, :], True, True, [bPn[hg], bXm[hg]], [bps])
                            S.op('dve', lambda e_, hg=hg, ps=ps: e_.tensor_tensor(
                                Xm[:, hg * 4:(hg + 1) * 4, :].rearrange("p j l -> p (j l)"),
                                Xm[:, hg * 4:(hg + 1) * 4, :].rearrange("p j l -> p (j l)"), ps[:, :], ALU.add),
                                reads=[bps, bXm[hg]], writes=[bXm[hg]])
                    if stage < 8:
                        continue
                    for half in range(2):
                        ps, bps = next_ps(C)
                        for j in range(8):
                            slot, h, c, pb = half * 8 + j, 2 * j + half, j, half * 64
                            _mm(S, ps[:, j * 64:(j + 1) * 64], AR[pb:pb + 64, c, 0, :], Sbb[pb:pb + 64, c, :], True, False, [bARBK, bS], [bps])
                            _mm(S, ps[:, j * 64:(j + 1) * 64], AB4[:, slot, 256:384], Vt[:, h * 64:(h + 1) * 64], False, True, [bAB4, bVt], [bps])
                        evac(C, half, Gs[:, half * 512:(half + 1) * 512], ps[:, :], bps, bGs)
                    for half in range(2):
                        ps, bps = next_ps(C)
                        for j in range(8):
                            slot = half * 8 + j
                            _mm(S, ps[:, j * 64:(j + 1) * 64], Xm[:, slot, :], Gs[:, slot * 64:(slot + 1) * 64], True, True, [bXm[slot // 4], bGs], [bps])
                        evac(C, half, Us[:, half * 512:(half + 1) * 512], ps[:, :], bps, bUs)
                    yav = ya[:].rearrange("p (c a i) -> p c a i", c=8, a=2)
                    yfv = yfl[:].rearrange("p (c a i) -> p c a i", c=8, a=2)
                    for half in range(2):
                        ps, bps = next_ps(C)
                        for j in range(8):
                            slot, h, c, pb = half * 8 + j, 2 * j + half, j, half * 64
                            o = ps[:, j * 64:(j + 1) * 64]
                            _mm(S, o, AR[pb:pb + 64, c, 1, :], Sbb[pb:pb + 64, c, :], True, False, [bARBK, bS], [bps])
                            _mm(S, o, AB4[:, slot, 128:256], Us[:, slot * 64:(slot + 1) * 64], False, False, [bAB4, bUs], [bps])
                            _mm(S, o, AB4[:, slot, 384:512], Vt[:, h * 64:(h + 1) * 64], False, True, [bAB4, bVt], [bps])
                        psv = ps[:, :].rearrange("p (c i) -> p c i", c=8)
                        if d == 0:
                            evac(C, half, yav[:, :, half, :], psv, bps, bya)
                        else:
                            S.op('dve', lambda e_, half=half, psv=psv: e_.tensor_tensor(yav[:, :, half, :], psv, yfv[:, :, half, :], ALU.add),
                                 reads=[bps, byfl], writes=[bya])
                    if d == 0:
                        S.op('pool', lambda e_, c0=c0: e_.dma_start(out=C.YS[c0:c0 + 128, :], in_=ya[:]), reads=[bya], writes=[C.bYS], dma=bya)
                    ps, bps = next_ps(C)
                    for c in range(8):
                        o = ps[:, c * 64:(c + 1) * 64]
                        for par in range(2):
                            h = 2 * c + par
                            slot = par * 8 + c
                            _mm(S, o, bkTp[:, par, 0, c, :], Us[:, slot * 64:(slot + 1) * 64], par == 0, False, [bbkT, bUs], [bps])
                            _mm(S, o, bkTp[:, par, 1, c, :], Vt[:, h * 64:(h + 1) * 64], False, par == 1, [bbkT, bVt], [bps])
                    S.op('dve', lambda e_, ps=ps: e_.tensor_tensor(Sst[:].rearrange("p c i -> p (c i)"), ps[:, :],
                                                                Sst[:].rearrange("p c i -> p (c i)"), ALU.add),
                         reads=[bps, bS], writes=[bS])
                    S.op('dve', lambda e_, last=last: e_.tensor_tensor(Sst[:], Sst[:], eLi[:, :, last:last + 1].to_broadcast([128, 8, 64]), ALU.mult),
                         reads=[bS, beL], writes=[bS])
                    S.op('act', lambda e_: e_.copy(Sbb[:], Sst[:]), reads=[bS], writes=[bS])
                    if stage < 9:
                        continue
                    if C.cfg.get('dbg') and d == 0 and t == 0 and s4 == 0:
                        dbgb = Buf('dbg')
                        def dump(idx, tile_ap, n):
                            S.op('pool', lambda e_: e_.dma_start(out=C.DBG[idx, :, 0:n], in_=tile_ap), reads=[bAB4, bXm[0], bXm[1], bXm[2], bXm[3], bGs, bUs, bya, bARBK, beL, bsig, bVt, bS, bbkT, bprep],
                                 writes=[dbgb], dma=Buf('x'))
                        dump(0, AB4[:].rearrange("p a b -> p (a b)"), 8192)
                        dump(1, Xm[:].rearrange("p a b -> p (a b)"), 2048)
                        dump(2, Gs[:], 1024)
                        dump(3, Us[:], 1024)
                        dump(4, ya[:], 1024)
                        dump(5, AR[:].rearrange("p a b c -> p (a b c)"), 2048)
                        dump(6, BK[:].rearrange("p a b c -> p (a b c)"), 2048)
                        dump(7, eLi[:].rearrange("p a b -> p (a b)"), 1024)
                        dump(8, eLn[:].rearrange("p a b -> p (a b)"), 1024)
                        dump(9, eLx[:].rearrange("p a b -> p (a b)"), 1024)
                        dump(10, sig[:], 1024)
                        dump(11, Vt[:], 1024)
                        dump(12, Sst[:].rearrange("p a b -> p (a b)"), 512)
                        dump(13, nkT[:, :, 0:128].rearrange("p a b -> p (a b)"), 1024) if False else None
                    if d == 1:
                        yv = ya[:].rearrange("p (h i) -> p h i", h=16)
                        S.op('dve', lambda e_, yv=yv: e_.reduce_sum(stt[:, 0, :], yv, AX.X), reads=[bya], writes=[bstt])
                        S.op('dve', lambda e_: e_.tensor_scalar(stt[:, 0, :], stt[:, 0, :], 1.0 / 64, None, ALU.mult), reads=[bstt], writes=[bstt])
                        S.op('dve', lambda e_, yv=yv: e_.tensor_tensor(yv, yv, stt[:, 0, :].unsqueeze(2).to_broadcast([128, 16, 64]), ALU.subtract),
                             reads=[bya, bstt], writes=[bya])
                        S.op('act', lambda e_: e_.activation(sig[:], ya[:], AF.Square), reads=[bya], writes=[bsig])
                        S.op('dve', lambda e_: e_.reduce_sum(stt[:, 1, :], sig[:].rearrange("p (h i) -> p h i", h=16), AX.X),
                             reads=[bsig], writes=[bstt])
                        S.op('act', lambda e_: e_.activation(stt[:, 1, :], stt[:, 1, :], AF.Sqrt, bias=C.epsD[:, 3:4], scale=1.0 / 64),
                             reads=[bstt, C.bconst], writes=[bstt])
                        S.op('dve', lambda e_: e_.reciprocal(stt[:, 1, :], stt[:, 1, :]), reads=[bstt], writes=[bstt])
                        S.op('dve', lambda e_, yv=yv: e_.tensor_tensor(ynb[:].rearrange("p (h i) -> p h i", h=16), yv,
                                                                     stt[:, 1, :].unsqueeze(2).to_broadcast([128, 16, 64]), ALU.mult),
                             reads=[bya, bstt], writes=[bynb])
                        for half in range(2):
                            ps, bps = next_ps(C)
                            for j in range(4):
                                c = half * 4 + j
                                _mm(S, ps[:, j * 128:(j + 1) * 128], ynb[:, c * 128:(c + 1) * 128], C.identb[:], True, True,
                                    [bynb, C.bconst], [bps])
                            for j in range(4):
                                c = half * 4 + j
                                S.op('act', lambda e_, c=c, j=j, ps=ps: e_.activation(fin[:, c, :], ps[:, j * 128:(j + 1) * 128], AF.Identity,
                                                                                   bias=cols[:, 6, c:c + 1], scale=cols[:, 5, c:c + 1]),
                                     reads=[bps, bset], writes=[bfin])
                        S.op('dve', lambda e_, ts=ts: e_.tensor_tensor(fin[:], fin[:], bvT[:, :, ts], ALU.add), reads=[bfin, bprep], writes=[bfin])
                        S.op('dve', lambda e_, ts=ts: e_.tensor_tensor(yo[:, :, ts], fin[:], gT[:, :, ts], ALU.mult), reads=[bfin, bprep], writes=[byo])
                if d == 1:
                    S.op('pool', lambda e_, t0=t0: e_.dma_start(out=Yv[:, 8:16, t0:t0 + TT], in_=yo[:]), reads=[byo], writes=[C.bY], dma=byo)
        S.end_phase()


def bkind(C, pos):
    if pos <= 0 or pos >= C.N:
        return 'hard'
    return C.cfg.get('bounds', {}).get(pos)


def t5_tables():
    oh = np.zeros((33, 3, 255), np.float32)
    for dl in range(3):
        for i in range(255):
            rel = i - 127 + 128 * (dl - 1)
            if abs(rel) > 128:
                oh[32, dl, i] = -30000.0
                continue
            n = abs(rel)
            if n < 8:
                b = n
            else:
                v = np.float32(np.log(np.float32(n) / np.float32(8.0))) / np.float32(math.log(16.0)) * np.float32(8.0)
                b = min(8 + int(np.float32(v)), 15)
            if rel > 0:
                b += 16
            oh[b, dl, i] = 1.0
    return oh.reshape(33, 765)


def phase_mix_odd(C, o):
    nc, S, N, W = C.nc, C.S, C.N, C.W
    TT = 512
    with ExitStack() as st:
        sb = lambda name, shape, dt=F32: st.enter_context(_sbt(nc, name, list(shape), dt))
        bset = Buf('mo_setup')
        rba = sb('mo_rba', [33, 16])
        S.op('dve', lambda e: e.memset(rba[:], 1.0), writes=[bset])
        S.op('sp', lambda e: e.dma_start(out=rba[0:32, :], in_=W['rel_bias']), writes=[bset], dma=Buf('mo_sd2'))
        oh = sb('mo_oh', [33, 765])
        S.op('sp', lambda e: e.dma_start(out=oh[:], in_=C.c_oh), writes=[bset], dma=Buf('mo_sd3'))
        d2 = sb('mo_d2', [16, 765])
        for (a, b) in ((0, 510), (510, 765)):
            ps, bps = next_ps(C)
            _mm(S, ps[0:16, 0:b - a], rba[:, :], oh[:, a:b], True, True, [bset], [bps])
            S.op('dve', lambda e, a=a, b=b, ps=ps: e.tensor_copy(d2[:, a:b], ps[0:16, 0:b - a]), reads=[bps], writes=[bset])
        bD2 = Buf('D2')
        S.op('sp', lambda e: e.dma_start(out=C.D2.ap(), in_=d2[:]), reads=[bset], writes=[bD2], dma=Buf('mo_sd4'))
        hkf = sb('mo_hkf', [128, 16, 3, 128])
        for h in range(16):
            src = bass.AP(C.D2, h * 765, [[1, 128], [255, 3], [1, 128]])
            S.op('sp', lambda e, h=h, src=src: e.dma_start(out=hkf[:, h, :, :], in_=src), reads=[bD2], writes=[bset], dma=Buf('mo_sd5'))
        hk = sb('mo_hk', [128, 16, 3, 128], BF16)
        S.op('dve', lambda e: e.tensor_copy(hk[:], hkf[:]), reads=[bset], writes=[bset])
        jf = sb('mo_jf', [128, 128])
        S.op('sp', lambda e: e.dma_start(out=jf[:], in_=C.c_anti), writes=[bset], dma=Buf('mo_sd6'))
        jb = sb('mo_jb', [128, 128], BF16)
        S.op('dve', lambda e: e.tensor_copy(jb[:], jf[:]), reads=[bset], writes=[bset])
        bd = sb('mo_bd', [128, 128])
        S.op('sp', lambda e: e.dma_start(out=bd[:], in_=C.c_bd), writes=[bset], dma=Buf('mo_sd7'))
        opad = sb('mo_opad', [128, 2, 128], BF16)
        S.op('dve', lambda e: e.memset(opad[:], 0.0), writes=[bset])
        S.op('dve', lambda e: e.memset(opad[:, 0, 0:64], 1.0), writes=[bset])
        S.op('dve', lambda e: e.memset(opad[:, 1, 64:128], 1.0), writes=[bset])
        opadF = sb('mo_opadF', [128, 2, 128], BF16)
        S.op('dve', lambda e: e.tensor_scalar(opadF[:], opad[:], C.flagt[:, 0:1], None, ALU.mult),
             reads=[bset, C.bconst], writes=[bset])
        esk = sb('mo_esk', [128, 8])
        for hh in range(2):
            src = bass.AP(W['att_sink'].tensor, W['att_sink'][o].offset + hh, [[0, 64], [2, 8]])
            S.op('sp', lambda e, hh=hh, src=src: e.dma_start(out=esk[hh * 64:(hh + 1) * 64, :], in_=src, allow_slow_non_contiguous=True),
                 writes=[bset], dma=Buf('mo_sd8'))
        S.op('act', lambda e: e.activation(esk[:], esk[:], AF.Exp), reads=[bset], writes=[bset])
        wq = sb('mo_wq', [128, 2])
        for hh in range(2):
            S.op('sp', lambda e, hh=hh: e.dma_start(out=wq[hh * 64:(hh + 1) * 64, 0:1],
                                                    in_=W['att_q_norm_w'][o].rearrange("(p one) -> p one", one=1)),
                 writes=[bset], dma=Buf('mo_sd9'))
            S.op('sp', lambda e, hh=hh: e.dma_start(out=wq[hh * 64:(hh + 1) * 64, 1:2],
                                                    in_=W['att_k_norm_w'][o].rearrange("(p one) -> p one", one=1)),
                 writes=[bset], dma=Buf('mo_sd10'))
        S.op('dve', lambda e: e.tensor_scalar(wq[:, 0:1], wq[:, 0:1], 0.125, None, ALU.mult), reads=[bset], writes=[bset])

        st1 = ExitStack()
        sb = lambda name, shape, dt=F32, _s=st1: _s.enter_context(_sbt(nc, name, list(shape), dt))
        wnat = sb('mo_wnat', [31, D])
        S.op('sp', lambda e: e.dma_start(out=wnat[:], in_=W['conv_dw_w'][o]), writes=[bset], dma=Buf('mo_sd1'))
        wcol = sb('mo_wcol', [128, 8, 31])
        for c in range(8):
            ps, bps = next_ps(C)
            _mm(S, ps[:, 0:31], wnat[0:31, c * 128:(c + 1) * 128], C.ident[0:31, 0:31], True, True, [bset, C.bconst], [bps])
            S.op('dve', lambda e, c=c, ps=ps: e.tensor_copy(wcol[:, c, :], ps[:, 0:31]), reads=[bps], writes=[bset])
        dg = sb('mo_dg', [128, 8, 31, 128], BF16)
        for c in range(8):
            for k in range(31):
                S.op('dve', lambda e, c=c, k=k: e.tensor_scalar(dg[:, c, k, :], C.identb[:], wcol[:, c, k:k + 1], None, ALU.mult),
                     reads=[bset, C.bconst], writes=[bset])
        cvb = sb('mo_cvb', [128, 8])
        lnw = sb('mo_lnw', [128, 8])
        lnb = sb('mo_lnb', [128, 8])
        load_cols(C, cvb[:], bset, W['conv_dw_b'][o], 8)
        load_cols(C, lnw[:], bset, W['conv_ln_w'][o], 8)
        load_cols(C, lnb[:], bset, W['conv_ln_b'][o], 8)
        onesf = sb('mo_onesf', [128, 128])
        S.op('dve', lambda e: e.memset(onesf[:], 1.0), writes=[bset])
        NB = 2
        vg = [sb('mo_vg%d' % i, [128, 2, 544]) for i in range(NB)]
        bvg = [Buf('mo_vg%d' % i) for i in range(NB)]
        sgm = sb('mo_sgm', [128, 544])
        bsgm = Buf('mo_sgm')
        ub = [sb('mo_u%d' % i, [128, 544], BF16) for i in range(NB)]
        bub = [Buf('mo_u%d' % i) for i in range(NB)]
        cv = sb('mo_cv', [128, 8, TT])
        bcv = Buf('mo_cv')
        sqf = sb('mo_sqf', [128, 8, TT])
        bsqf = Buf('mo_sqf')
        mean = sb('mo_mean', [128, TT])
        rstd = sb('mo_rstd', [128, TT])
        bst = Buf('mo_stat')
        t1 = [sb('mo_t1%d' % i, [128, TT]) for i in range(2)]
        bt1 = [Buf('mo_t1%d' % i) for i in range(2)]
        yc = [sb('mo_yc%d' % i, [128, 8, TT], BF16) for i in range(2)]
        byc = [Buf('mo_yc%d' % i) for i in range(2)]
        Yv = C.Y.rearrange("(c p) n -> p c n", p=128)
        it = 0
        for t in range(N // TT):
            t0 = t * TT
            kl, kr = bkind(C, t0), bkind(C, t0 + TT)
            lo = 15 if kl == 'hard' else 0
            hi = 527 if kr == 'hard' else 542
            for c in range(8):
                v, bv = vg[it % NB], bvg[it % NB]
                u, bu = ub[it % NB], bub[it % NB]
                it += 1
                for j in range(2):
                    src = C.UF[j * 1024 + c * 128: j * 1024 + (c + 1) * 128, t0 - 15 + lo: t0 - 15 + hi]
                    S.op('sp', lambda e, v=v, j=j, src=src, lo=lo, hi=hi: e.dma_start(out=v[:, j, lo:hi], in_=src),
                         reads=[C.bUF], writes=[bv], dma=bv)
                S.op('act', lambda e, v=v, lo=lo, hi=hi: e.activation(sgm[:, lo:hi], v[:, 1, lo:hi], AF.Sigmoid),
                     reads=[bv], writes=[bsgm])
                if lo > 0:
                    S.op('pool', lambda e, u=u: e.memset(u[:, 0:15], 0.0), writes=[bu])
                if hi < 542:
                    S.op('pool', lambda e, u=u: e.memset(u[:, 527:542], 0.0), writes=[bu])
                S.op('dve', lambda e, v=v, u=u, lo=lo, hi=hi: e.tensor_tensor(u[:, lo:hi], v[:, 0, lo:hi], sgm[:, lo:hi], ALU.mult),
                     reads=[bv, bsgm], writes=[bu])
                if kl == 'soft':
                    S.op('dve', lambda e, u=u: e.tensor_scalar(u[:, 0:15], u[:, 0:15], C.flagt[:, 0:1], None, ALU.mult),
                         reads=[bu, C.bconst], writes=[bu])
                if kr == 'soft':
                    S.op('dve', lambda e, u=u: e.tensor_scalar(u[:, 527:542], u[:, 527:542], C.flagt[:, 0:1], None, ALU.mult),
                         reads=[bu, C.bconst], writes=[bu])
                ps, bps = next_ps(C)
                for k in range(31):
                    _mm(S, ps[:, :], dg[:, c, k, :], u[:, k:k + TT], k == 0, k == 30, [bset, bu], [bps])
                S.op('act', lambda e, c=c, ps=ps: e.activation(cv[:, c, :], ps[:, :], AF.Identity, bias=cvb[:, c:c + 1], scale=1.0),
                     reads=[bps, bset], writes=[bcv])
            S.op('act', lambda e: e.activation(sqf[:], cv[:], AF.Square), reads=[bcv], writes=[bsqf])
            ps1, bps1 = next_ps(C)
            for c in range(8):
                _mm(S, ps1[:, :], onesf[:], cv[:, c, :], c == 0, c == 7, [bset, bcv], [bps1])
            ps2, bps2 = next_ps(C)
            for c in range(8):
                _mm(S, ps2[:, :], onesf[:], sqf[:, c, :], c == 0, c == 7, [bset, bsqf], [bps2])
            S.op('dve', lambda e, ps1=ps1: e.tensor_scalar(mean[:], ps1[:, :], 1.0 / D, None, ALU.mult), reads=[bps1], writes=[bst])
            S.op('dve', lambda e: e.tensor_tensor(rstd[:], mean[:], mean[:], ALU.mult), reads=[bst], writes=[bst])
            S.op('dve', lambda e, ps2=ps2: e.scalar_tensor_tensor(rstd[:], ps2[:, :], 1.0 / D, rstd[:], ALU.mult, ALU.subtract),
                 reads=[bps2, bst], writes=[bst])
            S.op('act', lambda e: e.activation(rstd[:], rstd[:], AF.Sqrt, bias=C.epsD[:, 2:3], scale=1.0),
                 reads=[bst, C.bconst], writes=[bst])
            S.op('dve', lambda e: e.reciprocal(rstd[:], rstd[:]), reads=[bst], writes=[bst])
            y, by = yc[t % 2], byc[t % 2]
            for c in range(8):
                tt, btt = t1[c % 2], bt1[c % 2]
                S.op('dve', lambda e, c=c, tt=tt: e.tensor_tensor(tt[:], cv[:, c, :], mean[:], ALU.subtract),
                     reads=[bcv, bst], writes=[btt])
                S.op('dve', lambda e, tt=tt: e.tensor_tensor(tt[:], tt[:], rstd[:], ALU.mult), reads=[btt, bst], writes=[btt])
                S.op('act', lambda e, c=c, tt=tt, y=y: e.activation(y[:, c, :], tt[:], AF.Silu, bias=lnb[:, c:c + 1],
                                                                     scale=lnw[:, c:c + 1]),
                     reads=[btt, bset], writes=[by])
            S.op('pool', lambda e, y=y, t0=t0: e.dma_start(out=Yv[:, 0:8, t0:t0 + TT], in_=y[:]),
                 reads=[by], writes=[C.bY], dma=by)

        S.barrier()
        S.flush()
        st1.close()
        st2 = ExitStack()
        sb = lambda name, shape, dt=F32, _s=st2: _s.enter_context(_sbt(nc, name, list(shape), dt))
        qf = [sb('mo_qf%d' % i, [128, 8, TT]) for i in range(2)]
        bqf = [Buf('mo_qf%d' % i) for i in range(2)]
        kf = [sb('mo_kf%d' % i, [128, 4, 768]) for i in range(2)]
        bkf = [Buf('mo_kf%d' % i) for i in range(2)]
        vf = [sb('mo_vf%d' % i, [128, 6, 256]) for i in range(2)]
        bvf = [Buf('mo_vf%d' % i) for i in range(2)]
        sq2 = sb('mo_sq2', [128, 768])
        bsq2 = Buf('mo_sq2')
        rs2 = sb('mo_rs2', [128, 768])
        brs2 = Buf('mo_rs2')
        qn = [sb('mo_qn%d' % i, [128, 8, TT], BF16) for i in range(2)]
        bqn = [Buf('mo_qn%d' % i) for i in range(2)]
        kn = [sb('mo_kn%d' % i, [128, 4, 768], BF16) for i in range(2)]
        bkn = [Buf('mo_kn%d' % i) for i in range(2)]
        vp = [sb('mo_vp%d' % i, [128, 2, 6, 4, 128], BF16) for i in range(2)]
        bvp = [Buf('mo_vp%d' % i) for i in range(2)]
        for i in range(2):
            S.op('pool', lambda e, i=i: e.memset(vp[i][:], 0.0), writes=[bvp[i]])
        vpF = sb('mo_vpF', [128, 2, 2, 4, 128], BF16)
        bvpF = Buf('mo_vpF')
        pt = [sb('mo_pt%d' % i, [128, 384], BF16) for i in range(3)]
        bpt = [Buf('mo_pt%d' % i) for i in range(3)]
        dn = [sb('mo_dn%d' % i, [128, 128]) for i in range(2)]
        bdn = [Buf('mo_dn%d' % i) for i in range(2)]
        yd = [sb('mo_yd%d' % i, [128, 8, TT], BF16) for i in range(2)]
        byd = [Buf('mo_yd%d' % i) for i in range(2)]
        pi = 0
        di = 0
        for t in range(N // TT):
            t0 = t * TT
            kl, kr = bkind(C, t0), bkind(C, t0 + TT)
            kb_lo = 1 if kl == 'hard' else 0
            kb_hi = 5 if kr == 'hard' else 6
            q_, bq_, k_, bk_, v_, bv_ = qf[t % 2], bqf[t % 2], kf[t % 2], bkf[t % 2], vf[t % 2], bvf[t % 2]
            qn_, bqn_, kn_, bkn_, vp_, bvp_ = qn[t % 2], bqn[t % 2], kn[t % 2], bkn[t % 2], vp[t % 2], bvp[t % 2]
            UFv = C.UF.rearrange("(c p) n -> p c n", p=128)
            for half in range(2):
                S.op('sp', lambda e, q_=q_, half=half, t0=t0: e.dma_start(
                    out=q_[:, half * 4:half * 4 + 4, :], in_=UFv[:, 16 + half * 4:16 + half * 4 + 4, t0:t0 + TT]),
                    reads=[C.bUF], writes=[bq_], dma=bq_)
            c0, c1 = kb_lo * 128, kb_hi * 128
            S.op('sp', lambda e, k_=k_, c0=c0, c1=c1, t0=t0: e.dma_start(
                out=k_[:, :, c0:c1], in_=UFv[:, 24:28, t0 - 128 + c0:t0 - 128 + c1]),
                reads=[C.bUF], writes=[bk_], dma=bk_)
            S.op('sp', lambda e, v_=v_, t0=t0, kb_lo=kb_lo, kb_hi=kb_hi: e.dma_start(
                out=v_[:, kb_lo:kb_hi, :],
                in_=C.UT[t0 - 128 + kb_lo * 128:t0 - 128 + kb_hi * 128, 0:256].rearrange("(b p) c -> p b c", p=128)),
                reads=[C.bUT], writes=[bv_], dma=bv_)
            for c in range(8):
                S.op('act', lambda e, c=c, q_=q_: e.activation(sq2[:, 0:TT], q_[:, c, :], AF.Square), reads=[bq_], writes=[bsq2])
                ps, bps = next_ps(C)
                _mm(S, ps[:, :], bd[:], sq2[:, 0:TT], True, True, [bset, bsq2], [bps])
                S.op('act', lambda e, ps=ps: e.activation(rs2[:, 0:TT], ps[:, :], AF.Sqrt, bias=C.epsD[:, 1:2], scale=1.0 / 64),
                     reads=[bps, C.bconst], writes=[brs2])
                S.op('dve', lambda e: e.reciprocal(rs2[:, 0:TT], rs2[:, 0:TT]), reads=[brs2], writes=[brs2])
                S.op('dve', lambda e, c=c, q_=q_, qn_=qn_: e.scalar_tensor_tensor(qn_[:, c, :], q_[:, c, :], wq[:, 0:1], rs2[:, 0:TT],
                                                                                 ALU.mult, ALU.mult),
                     reads=[bq_, brs2, bset], writes=[bqn_])
            for c in range(4):
                S.op('act', lambda e, c=c, k_=k_, c0=c0, c1=c1: e.activation(sq2[:, c0:c1], k_[:, c, c0:c1], AF.Square),
                     reads=[bk_], writes=[bsq2])
                for (a, b) in ((c0, min(c1, c0 + 512)), (c0 + 512, c1)):
                    if b <= a:
                        continue
                    ps, bps = next_ps(C)
                    _mm(S, ps[:, 0:b - a], bd[:], sq2[:, a:b], True, True, [bset, bsq2], [bps])
                    S.op('act', lambda e, ps=ps, a=a, b=b: e.activation(rs2[:, a:b], ps[:, 0:b - a], AF.Sqrt, bias=C.epsD[:, 1:2],
                                                                        scale=1.0 / 64),
                         reads=[bps, C.bconst], writes=[brs2])
                S.op('dve', lambda e, c0=c0, c1=c1: e.reciprocal(rs2[:, c0:c1], rs2[:, c0:c1]), reads=[brs2], writes=[brs2])
                S.op('dve', lambda e, c=c, k_=k_, kn_=kn_, c0=c0, c1=c1: e.scalar_tensor_tensor(
                    kn_[:, c, c0:c1], k_[:, c, c0:c1], wq[:, 1:2], rs2[:, c0:c1], ALU.mult, ALU.mult),
                    reads=[bk_, brs2, bset], writes=[bkn_])
            vsrc = v_[:, kb_lo:kb_hi, :].rearrange("p b (h d) -> p b h d", h=4)
            S.op('dve', lambda e, vp_=vp_, vsrc=vsrc, kb_lo=kb_lo, kb_hi=kb_hi: e.tensor_copy(vp_[:, 0, kb_lo:kb_hi, :, 0:64], vsrc),
                 reads=[bv_], writes=[bvp_])
            S.op('pool', lambda e, vp_=vp_, vsrc=vsrc, kb_lo=kb_lo, kb_hi=kb_hi: e.tensor_copy(vp_[:, 1, kb_lo:kb_hi, :, 64:128], vsrc),
                 reads=[bv_], writes=[bvp_])
            if kl == 'soft':
                S.op('dve', lambda e, vp_=vp_: e.tensor_scalar(vpF[:, :, 0, :, :], vp_[:, :, 0, :, :], C.flagt[:, 0:1], None, ALU.mult),
                     reads=[bvp_, C.bconst], writes=[bvpF])
            if kr == 'soft':
                S.op('dve', lambda e, vp_=vp_: e.tensor_scalar(vpF[:, :, 1, :, :], vp_[:, :, 5, :, :], C.flagt[:, 0:1], None, ALU.mult),
                     reads=[bvp_, C.bconst], writes=[bvpF])
            y, by = yd[t % 2], byd[t % 2]
            jobs = [(qb, pair, par) for qb in range(4) for pair in range(8) for par in range(2)]

            def part_a(job):
                qb, pair, par = job
                hq = pair * 2 + par
                hkv = hq // 4
                pb = par * 64
                dls = [dl for dl in range(3) if kb_lo <= qb + dl < kb_hi]
                ps, bps = next_ps(C)
                for dl in dls:
                    kb = qb + dl
                    _mm(S, ps[:, dl * 128:(dl + 1) * 128], kn_[pb:pb + 64, hkv, kb * 128:(kb + 1) * 128],
                        qn_[pb:pb + 64, pair, qb * 128:(qb + 1) * 128], True, False, [bkn_, bqn_], [bps])
                    _mm(S, ps[:, dl * 128:(dl + 1) * 128], hk[:, hq, dl, :], jb[:], False, True, [bset], [bps])
                p_, bp_ = pt[cntp[0] % 3], bpt[cntp[0] % 3]
                cntp[0] += 1
                a_, b_ = dls[0] * 128, (dls[-1] + 1) * 128
                S.op('act', lambda e: e.activation(p_[:, a_:b_], ps[:, a_:b_], AF.Exp), reads=[bps], writes=[bp_])
                return dls, p_, bp_

            cntp = [pi]
            nxt_a = part_a(jobs[0])
            cur_pair = None
            for ji, job in enumerate(jobs):
                qb, pair, par = job
                dls, p_, bp_ = nxt_a
                if ji + 1 < len(jobs):
                    nxt_a = part_a(jobs[ji + 1])
                hq = pair * 2 + par
                hkv = hq // 4
                if par == 0:
                    po, bpo = next_ps(C)
                    pd, bpd = next_ps(C)
                    im = 0
                nmm = 2 * len(dls)
                for dl in dls:
                    kb = qb + dl
                    soft = (kb == 0 and kl == 'soft') or (kb == 5 and kr == 'soft')
                    if soft:
                        vl = vpF[:, par, 0 if kb == 0 else 1, hkv, :]
                        ol = opadF[:, par, :]
                        rd = [bvpF, bset]
                    else:
                        vl = vp_[:, par, kb, hkv, :]
                        ol = opad[:, par, :]
                        rd = [bvp_, bset]
                    _mm(S, po[:, 0:128], vl, p_[:, dl * 128:(dl + 1) * 128], im == 0, im == nmm - 1, rd + [bp_], [bpo])
                    _mm(S, pd[:, 0:128], ol, p_[:, dl * 128:(dl + 1) * 128], im == 0, im == nmm - 1, rd + [bp_], [bpd])
                    im += 1
                if par == 1:
                    d_, bd_ = dn[di % 2], bdn[di % 2]
                    di += 1
                    S.op('dve', lambda e, d_=d_, pd=pd, pair=pair: e.tensor_scalar(d_[:], pd[:, 0:128], esk[:, pair:pair + 1], None, ALU.add),
                         reads=[bpd, bset], writes=[bd_])
                    S.op('dve', lambda e, d_=d_: e.reciprocal(d_[:], d_[:]), reads=[bd_], writes=[bd_])
                    S.op('dve', lambda e, d_=d_, po=po, y=y, pair=pair, qb=qb: e.tensor_tensor(
                        y[:, pair, qb * 128:(qb + 1) * 128], po[:, 0:128], d_[:], ALU.mult),
                        reads=[bpo, bd_], writes=[by])
            pi = cntp[0]
            S.op('pool', lambda e, y=y, t0=t0: e.dma_start(out=Yv[:, 8:16, t0:t0 + TT], in_=y[:]),
                 reads=[by], writes=[C.bY], dma=by)
        S.end_phase()
        st2.close()


_WNAMES = ['rel_bias', 'norm_mix_w', 'norm_ffn_w', 'ffn_w_in', 'ffn_w_out', 'ev_w_in', 'ev_w_out', 'ssd_conv_w', 'ssd_conv_b',
           'ssd_dt_bias', 'ssd_a_log', 'ssd_d', 'ssd_norm_w', 'rwkv_mu', 'rwkv_w0', 'rwkv_w2', 'rwkv_a0', 'rwkv_a2', 'rwkv_g2',
           'rwkv_k_k', 'rwkv_k_a', 'rwkv_r_k', 'rwkv_ln_w', 'rwkv_ln_b', 'od_w_in', 'od_w_out', 'conv_dw_w', 'conv_dw_b',
           'conv_ln_w', 'conv_ln_b', 'att_q_norm_w', 'att_k_norm_w', 'att_sink']
_PROG = {}


def kernel(x_prompt, x_sample, **weights):
    SEG = 4096
    NCORE = 8
    x_prompt = np.asarray(x_prompt, dtype=np.float32)
    x_sample = np.asarray(x_sample, dtype=np.float32)
    assert x_prompt.shape == (16, SEG, D) and x_sample.shape == (4, 2 * SEG, D)
    if 'full' not in _PROG:
        cfg = dict(N=3 * SEG, bounds={SEG: 'soft', 2 * SEG: 'hard'}, layers=[0, 1, 2, 3], debug=False, mixers=True)
        _PROG['full'] = build_program(cfg)
    nc, C = _PROG['full']
    wmap = {k: np.ascontiguousarray(np.asarray(weights[k], dtype=np.float32)) for k in _WNAMES}
    in_maps = []
    layout = []
    for c in range(NCORE):
        if c < 4:
            xin = np.concatenate([x_sample[c], x_prompt[c]], axis=0)
            fl = 1.0
            layout.append([('s', c), ('p', c)])
        else:
            ids = [4 + 3 * (c - 4) + j for j in range(3)]
            xin = np.concatenate([x_prompt[i] for i in ids], axis=0)
            fl = 0.0
            layout.append([('p', i) for i in ids])
        m = {'xin': np.ascontiguousarray(xin), 'flag': np.full((128, 1), fl, np.float32)}
        m.update(wmap)
        m.update(C.const_inputs)
        in_maps.append(m)
    res = run_bass_kernel_spmd(nc, in_maps, core_ids=list(range(NCORE)))
    y_prompt = np.empty((16, SEG, D), np.float32)
    y_sample = np.empty((4, 2 * SEG, D), np.float32)
    for c in range(NCORE):
        y = np.asarray(res.results[c]['yout'])
        pos = 0
        for kind, i in layout[c]:
            if kind == 's':
                y_sample[i] = y[pos:pos + 2 * SEG]
                pos += 2 * SEG
            else:
                y_prompt[i] = y[pos:pos + SEG]
                pos += SEG
    return (y_prompt, y_sample)
```

```python
import math
from contextlib import ExitStack
import numpy as np
import concourse.bass as bass
import concourse.mybir as mybir
from concourse.bass_utils import run_bass_kernel_spmd

F32 = mybir.dt.float32
BF16 = mybir.dt.bfloat16
AF = mybir.ActivationFunctionType
ALU = mybir.AluOpType
AX = mybir.AxisListType

D = 1024
DFF = 2816
EV_IN = 5920
OD_IN = 3584
ENGS = ('pe', 'act', 'dve', 'pool', 'sp')


class Buf:
    __slots__ = ('name', 'writers', 'readers', 'dsem', 'excl')

    def __init__(self, name, excl=False):
        self.name = name
        self.excl = excl
        self.writers = {}
        self.readers = {}
        self.dsem = None


class Sched:
    def __init__(self, nc, stack):
        self.nc = nc
        self.stack = stack
        self.q = {e: [] for e in ENGS}
        self.seq = {}
        self.known = {e: {} for e in ENGS}
        self.semh = {}
        self.ninst = 0
        self.free_keys = []
        self.phase_keys = []
        for e in ('pe', 'act', 'dve', 'pool'):
            self.semh[e] = stack.enter_context(nc.semaphore('s_' + e))
            self.seq[e] = 0

    def _dsem(self, buf):
        if buf.dsem is None:
            if self.free_keys:
                k = self.free_keys.pop()
            else:
                k = 'd%d' % len(self.semh)
                self.semh[k] = self.stack.enter_context(self.nc.semaphore(k))
                self.seq[k] = 0
            buf.dsem = k
            self.phase_keys.append(k)
        return buf.dsem

    def end_phase(self):
        self.barrier()
        self.flush()
        self.free_keys.extend(self.phase_keys)
        self.phase_keys = []

    def op(self, eng, fn, reads=(), writes=(), dma=None):
        if dma is not None:
            key = self._dsem(dma)
            inc = 16
        else:
            key = eng
            inc = 1
        deps = {}
        for b in reads:
            for k, v in b.writers.items():
                if deps.get(k, 0) < v:
                    deps[k] = v
            if b.excl:
                for k, v in b.readers.items():
                    if k != key and deps.get(k, 0) < v:
                        deps[k] = v
        for b in writes:
            for k, v in b.writers.items():
                if k == key:
                    continue
                if deps.get(k, 0) < v:
                    deps[k] = v
            for k, v in b.readers.items():
                if k == key and dma is None:
                    continue
                if deps.get(k, 0) < v:
                    deps[k] = v
        kn = self.known[eng]
        waits = []
        for k, v in deps.items():
            if kn.get(k, 0) < v:
                kn[k] = v
                waits.append((k, v))
        self.seq[key] += inc
        tok = self.seq[key]
        self.q[eng].append((waits, fn, key, inc))
        self.ninst += 1 + len(waits)
        for b in reads:
            if b.readers.get(key, 0) < tok:
                b.readers[key] = tok
        for b in writes:
            b.writers[key] = tok
            b.readers = {}

    def barrier(self):
        for e in ENGS:
            kn = self.known[e]
            waits = []
            for k, v in self.seq.items():
                if kn.get(k, 0) < v:
                    kn[k] = v
                    waits.append((k, v))
            if waits:
                self.q[e].append((waits, None, None, 0))

    def flush(self):
        nc = self.nc
        semh = self.semh
        q = self.q

        def replay(name, eng):
            for waits, fn, key, inc in q[name]:
                for k, v in waits:
                    eng.wait_ge(semh[k], v)
                if fn is not None:
                    fn(eng).then_inc(semh[key], inc)

        with nc.Block() as block:
            @block.tensor
            def _(e):
                replay('pe', e)

            @block.scalar
            def _(e):
                replay('act', e)

            @block.vector
            def _(e):
                replay('dve', e)

            @block.gpsimd
            def _(e):
                replay('pool', e)

            @block.sync
            def _(e):
                replay('sp', e)
        self.q = {e: [] for e in ENGS}


class Ctx:
    pass


_UID = [0]


def _sbt(nc, name, shape, dt):
    _UID[0] += 1
    return nc.sbuf_tensor('%s_%d' % (name, _UID[0]), shape, dt)


def _mm(S, out, lhsT, rhs, start, stop, reads, writes):
    S.op('pe', lambda e: e.matmul(out, lhsT, rhs, start=start, stop=stop), reads=reads, writes=writes)


def build_program(cfg):
    N = cfg['N']
    debug = cfg.get('debug', False)
    layers = cfg.get('layers', [0, 1, 2, 3])
    nc = bass.Bass("TRN2", target_bir_lowering=False)
    C = Ctx()
    C.nc = nc
    C.N = N
    C.cfg = cfg
    TT = 512
    assert N % TT == 0
    NT = N // TT

    def din(name, shape, dt=F32):
        return nc.dram_tensor(name, list(shape), dt, kind="ExternalInput").ap()

    def dscr(name, shape, dt=F32):
        return nc.dram_tensor(name, list(shape), dt, kind="ExternalOutput" if debug else "Internal").ap()

    xin = din('xin', [N, D])
    flag = din('flag', [128, 1])
    W = {}
    wshapes = dict(
        rel_bias=(32, 16), norm_mix_w=(4, D), norm_ffn_w=(4, D), ffn_w_in=(4, D, 2 * DFF), ffn_w_out=(4, DFF, D),
        ev_w_in=(2, D, EV_IN), ev_w_out=(2, 2048, D), ssd_conv_w=(2, 5, 1536), ssd_conv_b=(2, 1536),
        ssd_dt_bias=(2, 2, 16), ssd_a_log=(2, 2, 16), ssd_d=(2, 16), ssd_norm_w=(2, D), rwkv_mu=(2, 3328),
        rwkv_w0=(2, 2, D), rwkv_w2=(2, 2, 64, D), rwkv_a0=(2, D), rwkv_a2=(2, 64, D), rwkv_g2=(2, 128, D),
        rwkv_k_k=(2, D), rwkv_k_a=(2, D), rwkv_r_k=(2, 16, 64), rwkv_ln_w=(2, D), rwkv_ln_b=(2, D),
        od_w_in=(2, D, OD_IN), od_w_out=(2, 2048, D), conv_dw_w=(2, 31, D), conv_dw_b=(2, D), conv_ln_w=(2, D),
        conv_ln_b=(2, D), att_q_norm_w=(2, 64), att_k_norm_w=(2, 64), att_sink=(2, 16))
    for k, shp in wshapes.items():
        W[k] = din(k, shp)
    yout = nc.dram_tensor('yout', [N, D], F32, kind="ExternalOutput").ap()
    XA = dscr('XA', [D, N])
    XB = dscr('XB', [D, N])
    UF = dscr('UF', [38 * 128, N])
    UT = dscr('UT', [N, 1056])
    C.UFb = dscr('UFb', [26 * 128, N], BF16)
    Y = dscr('Y', [2048, N], BF16)
    H = dscr('H', [DFF, N], BF16)
    C.xin, C.flag, C.W, C.yout = xin, flag, W, yout
    C.XA, C.XB, C.UF, C.UT, C.Y, C.H = XA, XB, UF, UT, Y, H
    C.bXA, C.bXB, C.bUF, C.bUT, C.bY, C.bH = Buf('XA'), Buf('XB'), Buf('UF'), Buf('UT'), Buf('Y'), Buf('H')
    C.bIN = Buf('in')
    C.bOUT = Buf('yout')

    with ExitStack() as top:
        S = Sched(nc, top)
        C.S = S
        C.ps = []
        for i in range(8):
            C.ps.append((top.enter_context(nc.psum_tensor('ps%d' % i, [128, 512], F32)), Buf('ps%d' % i, excl=True)))
        C.psi = 0
        ident = top.enter_context(_sbt(nc, 'ident', [128, 128], F32))
        identb = top.enter_context(_sbt(nc, 'identb', [128, 128], BF16))
        onesb = top.enter_context(_sbt(nc, 'onesb', [128, 128], BF16))
        C.ident, C.identb, C.onesb = ident, identb, onesb
        C.bconst = Buf('const')
        identd = din('c_ident', [128, 128])
        C.const_inputs = {'c_ident': np.eye(128, dtype=np.float32)}
        S.op('sp', lambda e: e.dma_start(out=ident[:], in_=identd), writes=[C.bconst], dma=C.bconst)
        S.op('dve', lambda e: e.tensor_copy(identb[:], ident[:]), reads=[C.bconst], writes=[C.bconst])
        S.op('dve', lambda e: e.memset(onesb[:], 1.0), writes=[C.bconst])
        C.epsD = top.enter_context(_sbt(nc, 'epsD', [128, 4], F32))
        S.op('dve', lambda e: e.memset(C.epsD[:, 0:1], float(D * 1e-6)), writes=[C.bconst])
        S.op('dve', lambda e: e.memset(C.epsD[:, 1:2], 1e-6), writes=[C.bconst])
        S.op('dve', lambda e: e.memset(C.epsD[:, 2:3], 1e-5), writes=[C.bconst])
        S.op('dve', lambda e: e.memset(C.epsD[:, 3:4], 64e-5), writes=[C.bconst])
        C.flagt = top.enter_context(_sbt(nc, 'flagt', [128, 1], F32))
        S.op('sp', lambda e: e.dma_start(out=C.flagt[:], in_=flag), writes=[C.bconst], dma=C.bconst)
        C.c_oh = din('c_oh', [33, 765])
        C.c_anti = din('c_anti', [128, 128])
        C.c_bd = din('c_bd', [128, 128])
        C.const_inputs['c_oh'] = t5_tables()
        C.const_inputs['c_anti'] = np.ascontiguousarray(np.eye(128, dtype=np.float32)[::-1])
        C.const_inputs['c_bd'] = np.kron(np.eye(2, dtype=np.float32), np.ones((64, 64), np.float32))
        C.D2 = nc.dram_tensor('D2', [16, 765], F32, kind="Internal")
        C.c_tri = din('c_tri', [128, 2, 128])
        C.c_nm = din('c_nm', [128, 2, 128])
        ii = np.arange(128)
        tri = np.zeros((128, 2, 128), np.float32)
        tri[:, 0, :] = (ii[:, None] <= ii[None, :])
        tri[:, 1, :] = (ii[:, None] >= ii[None, :])
        nm = np.zeros((128, 2, 128), np.float32)
        nm[:, 0, :] = np.where(ii[:, None] > ii[None, :], -30000.0, 0.0)
        nm[:, 1, :] = np.where(ii[:, None] < ii[None, :], -30000.0, 0.0)
        C.const_inputs['c_tri'] = tri
        C.c_tri2 = din('c_tri2', [128, 2, 128])
        tri2 = np.zeros((128, 2, 128), np.float32)
        tri2[:, 0, :] = (ii[:, None] < ii[None, :])
        tri2[:, 1, :] = (ii[:, None] > ii[None, :])
        C.const_inputs['c_tri2'] = tri2
        C.const_inputs['c_nm'] = nm
        C.onec = top.enter_context(_sbt(nc, 'onec', [128, 1], F32))
        S.op('dve', lambda e: e.memset(C.onec[:], 1.0), writes=[C.bconst])
        C.YS = dscr('YS', [N, D])
        if cfg.get('dbg'):
            C.DBG = nc.dram_tensor('DBG', [17, 128, 8192], F32, kind='ExternalOutput').ap()
        C.bYS = Buf('YS')
        S.barrier()
        S.flush()
        S.phase_keys = []

        phase_in(C)
        cur, nxt = (XA, C.bXA), (XB, C.bXB)
        for li, layer in enumerate(layers):
            last = (li == len(layers) - 1)
            if layer % 2 == 0:
                e = layer // 2
                phase_inproj_even(C, cur, layer, e)
                if cfg.get('mixers', True):
                    phase_mix_even(C, e)
            else:
                o = layer // 2
                phase_inproj_odd(C, cur, layer, o)
                if cfg.get('mixers', True):
                    phase_mix_odd(C, o)
            wout = W['ev_w_out'][layer // 2] if layer % 2 == 0 else W['od_w_out'][layer // 2]
            phase_outproj(C, cur, nxt, wout)
            phase_ffn_in(C, nxt, layer)
            phase_ffn_out(C, nxt, cur, layer, last)
        if not layers:
            phase_out_only(C, cur)
        S.barrier()
        S.flush()
    return nc, C


def next_ps(C):
    t, b = C.ps[C.psi]
    C.psi = (C.psi + 1) % 8
    return t, b


def phase_in(C):
    nc, S, N = C.nc, C.S, C.N
    with ExitStack() as st:
        xt = [st.enter_context(_sbt(nc, 'pi_x%d' % i, [128, D], F32)) for i in range(2)]
        bx = [Buf('pi_x%d' % i) for i in range(2)]
        ot = [st.enter_context(_sbt(nc, 'pi_o%d' % i, [128, 8, 512], F32)) for i in range(2)]
        bo = [Buf('pi_o%d' % i) for i in range(2)]
        nsub = N // 128
        for t512 in range(N // 512):
            o, bob = ot[t512 % 2], bo[t512 % 2]
            for s4 in range(4):
                sub = t512 * 4 + s4
                x, bxb = xt[sub % 2], bx[sub % 2]
                S.op('sp', lambda e, x=x, sub=sub: e.dma_start(out=x[:], in_=C.xin[sub * 128:(sub + 1) * 128, :]),
                     reads=[C.bIN], writes=[bxb], dma=bxb)
                for half in range(2):
                    ps, bps = next_ps(C)
                    for j in range(4):
                        c = half * 4 + j
                        _mm(S, ps[:, j * 128:(j + 1) * 128], x[:, c * 128:(c + 1) * 128], C.ident[:], True, True,
                            [bxb, C.bconst], [bps])
                    eng = 'act' if half == 0 else 'dve'
                    dst = o[:, half * 4:half * 4 + 4, s4 * 128:(s4 + 1) * 128]
                    src = ps[:, :].rearrange("p (j t) -> p j t", j=4)
                    if eng == 'act':
                        S.op('act', lambda e, dst=dst, src=src: e.copy(dst, src), reads=[bps], writes=[bob])
                    else:
                        S.op('dve', lambda e, dst=dst, src=src: e.tensor_copy(dst, src), reads=[bps], writes=[bob])
            dst = C.XA.rearrange("(c p) n -> p c n", p=128)[:, :, t512 * 512:(t512 + 1) * 512]
            S.op('pool', lambda e, dst=dst, o=o: e.dma_start(out=dst, in_=o[:]), reads=[bob], writes=[C.bXA], dma=bob)
        S.end_phase()


def load_weight_bf16(C, wt, bw, wdram, K, col0, ncols, dcol0=0):
    S = C.S
    kc = K // 128
    src = wdram.rearrange("(c p) m -> p c m", p=128)
    step = 1024
    for c in range(kc):
        for m0 in range(0, ncols, step):
            m1 = min(ncols, m0 + step)
            S.op('pool', lambda e, c=c, m0=m0, m1=m1: e.dma_start(
                out=wt[:, c, dcol0 + m0:dcol0 + m1], in_=src[:, c, col0 + m0:col0 + m1]),
                writes=[bw], dma=bw)


def load_cols(C, dst_tile, bdst, vec_dram, nchunk):
    S = C.S
    src = vec_dram.rearrange("(c p) -> p c", p=128)
    C.nc
    S.op('sp', lambda e: e.dma_start(out=dst_tile, in_=src, allow_slow_non_contiguous=True), writes=[bdst], dma=Buf('lc'))


def rmsnorm_tile(C, xT, bx, hT, bh, w32, bw, sq, bsq, rstd, brs):
    S = C.S
    S.op('act', lambda e: e.activation(sq[:], xT[:], AF.Square), reads=[bx], writes=[bsq])
    ps, bps = next_ps(C)
    for c in range(8):
        _mm(S, ps[:, :], C.onesb[:], sq[:, c, :], c == 0, c == 7, [bsq, C.bconst], [bps])
    S.op('act', lambda e: e.activation(rstd[:], ps[:, :], AF.Sqrt, bias=C.epsD[:, 0:1], scale=1.0),
         reads=[bps, C.bconst], writes=[brs])
    S.op('dve', lambda e: e.reciprocal(rstd[:], rstd[:]), reads=[brs], writes=[brs])
    for c in range(8):
        S.op('dve', lambda e, c=c: e.scalar_tensor_tensor(hT[:, c, :], xT[:, c, :], w32[:, c:c + 1], rstd[:],
                                                         ALU.mult, ALU.mult),
             reads=[bx, brs, bw], writes=[bh])


def evac(C, idx, dst, src, bsrc, bdst, extra_reads=()):
    S = C.S
    if idx % 2 == 0:
        S.op('act', lambda e: e.copy(dst, src), reads=[bsrc] + list(extra_reads), writes=[bdst])
    else:
        S.op('dve', lambda e: e.tensor_copy(dst, src), reads=[bsrc] + list(extra_reads), writes=[bdst])


def phase_inproj(C, cur, norm_w, wdram, M, wcols_extra, fm_chunks, tm_groups, name):
    nc, S, N = C.nc, C.S, C.N
    X, bX = cur
    Mtot = M + sum(n for _, n, _ in wcols_extra)
    with ExitStack() as st:
        wt = st.enter_context(_sbt(nc, name + '_w', [128, 8, Mtot], BF16))
        bw = Buf(name + '_w')
        load_weight_bf16(C, wt, bw, wdram, D, 0, M)
        for (s0, n, d0) in wcols_extra:
            load_weight_bf16(C, wt, bw, wdram, D, s0, n, d0)
        nw = st.enter_context(_sbt(nc, name + '_nw', [128, 8], F32))
        bnw = Buf(name + '_nw')
        load_cols(C, nw[:], bnw, norm_w, 8)
        S.op('dve', lambda e: e.tensor_scalar(nw[:], nw[:], 32.0, None, ALU.mult), reads=[bnw], writes=[bnw])
        xT = [st.enter_context(_sbt(nc, name + '_x%d' % i, [128, 8, 512], F32)) for i in range(2)]
        bx = [Buf(name + '_x%d' % i) for i in range(2)]
        hT = [st.enter_context(_sbt(nc, name + '_h%d' % i, [128, 8, 512], BF16)) for i in range(2)]
        bh = [Buf(name + '_h%d' % i) for i in range(2)]
        sq = st.enter_context(_sbt(nc, name + '_sq', [128, 8, 512], BF16))
        bsq = Buf('sq')
        rstd = st.enter_context(_sbt(nc, name + '_rs', [128, 512], F32))
        brs = Buf('rs')
        NSTG = 4
        stg = [st.enter_context(_sbt(nc, name + '_st%d' % i, [128, 2, 512], F32)) for i in range(NSTG)]
        bst = [Buf(name + '_st%d' % i) for i in range(NSTG)]
        stgb = [st.enter_context(_sbt(nc, name + '_sb%d' % i, [128, 2, 512], BF16)) for i in range(NSTG)]
        bstb = [Buf(name + '_sb%d' % i) for i in range(NSTG)]
        Xv = X.rearrange("(c p) n -> p c n", p=128)
        si = 0
        ei = 0
        def load_norm(t):
            x, bxx, h, bhh = xT[t % 2], bx[t % 2], hT[t % 2], bh[t % 2]
            for half in range(2):
                S.op('sp', lambda e, x=x, t=t, half=half: e.dma_start(
                    out=x[:, half * 4:half * 4 + 4, :], in_=Xv[:, half * 4:half * 4 + 4, t * 512:(t + 1) * 512]),
                    reads=[bX], writes=[bxx], dma=bxx)
            rmsnorm_tile(C, x, bxx, h, bhh, nw, bnw, sq, bsq, rstd, brs)

        load_norm(0)
        for t in range(N // 512):
            x, bxx, h, bhh = xT[t % 2], bx[t % 2], hT[t % 2], bh[t % 2]
            if t + 1 < N // 512:
                load_norm(t + 1)
            for i in range(0, len(fm_chunks), 2):
                grp = fm_chunks[i:i + 2]
                isb = len(grp[0]) > 2
                if isb:
                    sg, bsg = stgb[si % NSTG], bstb[si % NSTG]
                else:
                    sg, bsg = stg[si % NSTG], bst[si % NSTG]
                si += 1
                for j, gg in enumerate(grp):
                    wc0, ur0 = gg[0], gg[1]
                    ps, bps = next_ps(C)
                    for kc in range(8):
                        _mm(S, ps[:, :], wt[:, kc, wc0:wc0 + 128], h[:, kc, :], kc == 0, kc == 7, [bw, bhh], [bps])
                    evac(C, ei, sg[:, j, :], ps[:, :], bps, bsg)
                    ei += 1
                contiguous = len(grp) == 2 and grp[1][1] == grp[0][1] + 128
                if isb:
                    assert contiguous
                    ur0 = grp[0][1]
                    dst = C.UFb[ur0:ur0 + 256, t * 512:(t + 1) * 512].rearrange("(c p) n -> p c n", p=128)
                    S.op('pool', lambda e, dst=dst, sg=sg: e.dma_start(out=dst, in_=sg[:]),
                         reads=[bsg], writes=[C.bUF], dma=bsg)
                elif contiguous:
                    ur0 = grp[0][1]
                    dst = C.UF[ur0:ur0 + 256, t * 512:(t + 1) * 512].rearrange("(c p) n -> p c n", p=128)
                    S.op('pool', lambda e, dst=dst, sg=sg: e.dma_start(out=dst, in_=sg[:]),
                         reads=[bsg], writes=[C.bUF], dma=bsg)
                else:
                    for j, gg in enumerate(grp):
                        wc0, ur0 = gg[0], gg[1]
                        dst = C.UF[ur0:ur0 + 128, t * 512:(t + 1) * 512]
                        S.op('pool', lambda e, dst=dst, sg=sg, j=j: e.dma_start(out=dst, in_=sg[:, j, :]),
                             reads=[bsg], writes=[C.bUF], dma=bsg)
            for s4 in range(4):
                for (wc0, ncol, uc0) in tm_groups:
                    sg, bsg = stg[si % NSTG], bst[si % NSTG]
                    si += 1
                    ps, bps = next_ps(C)
                    for kc in range(8):
                        _mm(S, ps[:, 0:ncol], h[:, kc, s4 * 128:(s4 + 1) * 128], wt[:, kc, wc0:wc0 + ncol],
                            kc == 0, kc == 7, [bw, bhh], [bps])
                    sgv = sg[:].rearrange("p a b -> p (a b)")[:, 0:ncol]
                    evac(C, ei, sgv, ps[:, 0:ncol], bps, bsg)
                    ei += 1
                    dst = C.UT[t * 512 + s4 * 128: t * 512 + (s4 + 1) * 128, uc0:uc0 + ncol]
                    S.op('pool', lambda e, dst=dst, sgv=sgv: e.dma_start(out=dst, in_=sgv),
                         reads=[bsg], writes=[C.bUT], dma=bsg)
        S.end_phase()


def phase_inproj_even(C, cur, layer, e):
    W = C.W
    fm = [(1024 + i * 128, i * 128) for i in range(12)] + [(2592 + i * 128, i * 128, 'b') for i in range(26)]
    tm = [(0, 512, 0), (512, 512, 512), (2560, 32, 1024)]
    phase_inproj(C, cur, W['norm_mix_w'][layer], W['ev_w_in'][e], EV_IN, [], fm, tm, 'ie%d' % layer)


def phase_inproj_odd(C, cur, layer, o):
    W = C.W
    extra = []
    for hk in range(4):
        extra.append((3072 + hk * 64, 64, OD_IN + hk * 128))
        extra.append((3072 + hk * 64, 64, OD_IN + hk * 128 + 64))
    fm = [(i * 128, i * 128) for i in range(24)] + [(OD_IN + i * 128, 3072 + i * 128) for i in range(4)]
    tm = [(3328, 256, 0)]
    phase_inproj(C, cur, W['norm_mix_w'][layer], W['od_w_in'][o], OD_IN, extra, fm, tm, 'io%d' % layer)


def phase_outproj(C, cur, nxt, wdram):
    nc, S, N = C.nc, C.S, C.N
    X, bX = cur
    X2, bX2 = nxt
    with ExitStack() as st:
        wt = st.enter_context(_sbt(nc, 'op_w', [128, 16, D], BF16))
        bw = Buf('op_w')
        load_weight_bf16(C, wt, bw, wdram, 2048, 0, D)
        xT = [st.enter_context(_sbt(nc, 'op_x%d' % i, [128, 8, 512], F32)) for i in range(2)]
        bx = [Buf('op_x%d' % i) for i in range(2)]
        yT = [st.enter_context(_sbt(nc, 'op_y%d' % i, [128, 16, 512], BF16)) for i in range(2)]
        by = [Buf('op_y%d' % i) for i in range(2)]
        Xv = X.rearrange("(c p) n -> p c n", p=128)
        X2v = X2.rearrange("(c p) n -> p c n", p=128)
        Yv = C.Y.rearrange("(c p) n -> p c n", p=128)
        for t in range(N // 512):
            x, bxx, y, byy = xT[t % 2], bx[t % 2], yT[t % 2], by[t % 2]
            sl = slice(t * 512, (t + 1) * 512)
            for half in range(2):
                S.op('sp', lambda e, x=x, sl=sl, half=half: e.dma_start(
                    out=x[:, half * 4:half * 4 + 4, :], in_=Xv[:, half * 4:half * 4 + 4, sl]),
                    reads=[bX], writes=[bxx], dma=bxx)
                S.op('sp', lambda e, y=y, sl=sl, half=half: e.dma_start(
                    out=y[:, half * 8:half * 8 + 8, :], in_=Yv[:, half * 8:half * 8 + 8, sl]),
                    reads=[C.bY], writes=[byy], dma=byy)
            for oc in range(8):
                ps, bps = next_ps(C)
                for kc in range(16):
                    _mm(S, ps[:, :], wt[:, kc, oc * 128:(oc + 1) * 128], y[:, kc, :], kc == 0, kc == 15, [bw, byy], [bps])
                S.op('dve', lambda e, x=x, ps=ps, oc=oc: e.tensor_tensor(x[:, oc, :], x[:, oc, :], ps[:, :], ALU.add),
                     reads=[bps, bxx], writes=[bxx])
            for half in range(2):
                S.op('pool', lambda e, x=x, sl=sl, half=half: e.dma_start(
                    out=X2v[:, half * 4:half * 4 + 4, sl], in_=x[:, half * 4:half * 4 + 4, :]),
                    reads=[bxx], writes=[bX2], dma=bxx)
        S.end_phase()


def phase_ffn_in(C, cur, layer):
    nc, S, N = C.nc, C.S, C.N
    X, bX = cur
    W = C.W
    name = 'fi'
    with ExitStack() as st:
        wt = st.enter_context(_sbt(nc, 'fi_w', [128, 8, 2 * DFF], BF16))
        bw = Buf('fi_w')
        load_weight_bf16(C, wt, bw, W['ffn_w_in'][layer], D, 0, 2 * DFF)
        nw = st.enter_context(_sbt(nc, 'fi_nw', [128, 8], F32))
        bnw = Buf('fi_nw')
        load_cols(C, nw[:], bnw, W['norm_ffn_w'][layer], 8)
        S.op('dve', lambda e: e.tensor_scalar(nw[:], nw[:], 32.0, None, ALU.mult), reads=[bnw], writes=[bnw])
        xT = [st.enter_context(_sbt(nc, 'fi_x%d' % i, [128, 8, 512], F32)) for i in range(2)]
        bx = [Buf('fi_x%d' % i) for i in range(2)]
        hT = [st.enter_context(_sbt(nc, 'fi_h%d' % i, [128, 8, 512], BF16)) for i in range(2)]
        bh = [Buf('fi_h%d' % i) for i in range(2)]
        sq = st.enter_context(_sbt(nc, 'fi_sq', [128, 8, 512], BF16))
        bsq = Buf('sq')
        rstd = st.enter_context(_sbt(nc, 'fi_rs', [128, 512], F32))
        brs = Buf('rs')
        hid = [st.enter_context(_sbt(nc, 'fi_hid%d' % i, [128, 22, 512], BF16)) for i in range(2)]
        bhid = [Buf('fi_hid%d' % i) for i in range(2)]
        sg = [st.enter_context(_sbt(nc, 'fi_sg%d' % i, [128, 512], F32)) for i in range(2)]
        bsg = [Buf('fi_sg%d' % i) for i in range(2)]
        Xv = X.rearrange("(c p) n -> p c n", p=128)
        Hv = C.H.rearrange("(c p) n -> p c n", p=128)
        def load_norm(t):
            x, bxx, h, bhh = xT[t % 2], bx[t % 2], hT[t % 2], bh[t % 2]
            sl = slice(t * 512, (t + 1) * 512)
            for half in range(2):
                S.op('sp', lambda e, x=x, sl=sl, half=half: e.dma_start(
                    out=x[:, half * 4:half * 4 + 4, :], in_=Xv[:, half * 4:half * 4 + 4, sl]),
                    reads=[bX], writes=[bxx], dma=bxx)
            rmsnorm_tile(C, x, bxx, h, bhh, nw, bnw, sq, bsq, rstd, brs)

        load_norm(0)
        for t in range(N // 512):
            x, bxx, h, bhh = xT[t % 2], bx[t % 2], hT[t % 2], bh[t % 2]
            hd, bhd = hid[t % 2], bhid[t % 2]
            sl = slice(t * 512, (t + 1) * 512)
            if t + 1 < N // 512:
                load_norm(t + 1)
            for fc in range(22):
                psg, bpsg = next_ps(C)
                for kc in range(8):
                    _mm(S, psg[:, :], wt[:, kc, fc * 128:(fc + 1) * 128], h[:, kc, :], kc == 0, kc == 7, [bw, bhh], [bpsg])
                psu, bpsu = next_ps(C)
                for kc in range(8):
                    _mm(S, psu[:, :], wt[:, kc, DFF + fc * 128:DFF + (fc + 1) * 128], h[:, kc, :], kc == 0, kc == 7,
                        [bw, bhh], [bpsu])
                s_, bs_ = sg[fc % 2], bsg[fc % 2]
                S.op('act', lambda e, s_=s_, psg=psg: e.activation(s_[:], psg[:, :], AF.Silu), reads=[bpsg], writes=[bs_])
                S.op('dve', lambda e, s_=s_, psu=psu, hd=hd, fc=fc: e.tensor_tensor(hd[:, fc, :], s_[:], psu[:, :], ALU.mult),
                     reads=[bs_, bpsu], writes=[bhd])
            for (c0, c1) in ((0, 8), (8, 16), (16, 22)):
                S.op('pool', lambda e, hd=hd, sl=sl, c0=c0, c1=c1: e.dma_start(out=Hv[:, c0:c1, sl], in_=hd[:, c0:c1, :]),
                     reads=[bhd], writes=[C.bH], dma=bhd)
        S.end_phase()


def phase_ffn_out(C, cur, nxt, layer, last):
    nc, S, N = C.nc, C.S, C.N
    X, bX = cur
    X2, bX2 = nxt
    W = C.W
    with ExitStack() as st:
        wt = st.enter_context(_sbt(nc, 'fo_w', [128, 22, D], BF16))
        bw = Buf('fo_w')
        load_weight_bf16(C, wt, bw, W['ffn_w_out'][layer], DFF, 0, D)
        xT = [st.enter_context(_sbt(nc, 'fo_x%d' % i, [128, 8, 512], F32)) for i in range(2)]
        bx = [Buf('fo_x%d' % i) for i in range(2)]
        hd = [st.enter_context(_sbt(nc, 'fo_h%d' % i, [128, 22, 512], BF16)) for i in range(2)]
        bhd = [Buf('fo_h%d' % i) for i in range(2)]
        ot = [st.enter_context(_sbt(nc, 'fo_o%d' % i, [128, D], F32)) for i in range(2)]
        bo = [Buf('fo_o%d' % i) for i in range(2)]
        Xv = X.rearrange("(c p) n -> p c n", p=128)
        X2v = X2.rearrange("(c p) n -> p c n", p=128)
        Hv = C.H.rearrange("(c p) n -> p c n", p=128)
        oi = 0
        for t in range(N // 512):
            x, bxx, h, bhh = xT[t % 2], bx[t % 2], hd[t % 2], bhd[t % 2]
            sl = slice(t * 512, (t + 1) * 512)
            for half in range(2):
                S.op('sp', lambda e, x=x, sl=sl, half=half: e.dma_start(
                    out=x[:, half * 4:half * 4 + 4, :], in_=Xv[:, half * 4:half * 4 + 4, sl]),
                    reads=[bX], writes=[bxx], dma=bxx)
            for (c0, c1) in ((0, 8), (8, 16), (16, 22)):
                S.op('sp', lambda e, h=h, sl=sl, c0=c0, c1=c1: e.dma_start(out=h[:, c0:c1, :], in_=Hv[:, c0:c1, sl]),
                     reads=[C.bH], writes=[bhh], dma=bhh)
            for oc in range(8):
                ps, bps = next_ps(C)
                for kc in range(22):
                    _mm(S, ps[:, :], wt[:, kc, oc * 128:(oc + 1) * 128], h[:, kc, :], kc == 0, kc == 21, [bw, bhh], [bps])
                S.op('dve', lambda e, x=x, ps=ps, oc=oc: e.tensor_tensor(x[:, oc, :], x[:, oc, :], ps[:, :], ALU.add),
                     reads=[bps, bxx], writes=[bxx])
            if not last:
                for half in range(2):
                    S.op('pool', lambda e, x=x, sl=sl, half=half: e.dma_start(
                        out=X2v[:, half * 4:half * 4 + 4, sl], in_=x[:, half * 4:half * 4 + 4, :]),
                        reads=[bxx], writes=[bX2], dma=bxx)
            else:
                for s4 in range(4):
                    o, bob = ot[oi % 2], bo[oi % 2]
                    oi += 1
                    for half in range(2):
                        ps, bps = next_ps(C)
                        for j in range(4):
                            c = half * 4 + j
                            _mm(S, ps[:, j * 128:(j + 1) * 128], x[:, c, s4 * 128:(s4 + 1) * 128], C.ident[:], True, True,
                                [bxx, C.bconst], [bps])
                        evac(C, half, o[:, half * 512:(half + 1) * 512], ps[:, :], bps, bob)
                    r0 = t * 512 + s4 * 128
                    S.op('pool', lambda e, o=o, r0=r0: e.dma_start(out=C.yout[r0:r0 + 128, :], in_=o[:]),
                         reads=[bob], writes=[C.bOUT], dma=bob)
        S.end_phase()


def phase_out_only(C, cur):
    raise NotImplementedError


def phase_mix_even(C, e):
    phase_ssd(C, e)
    if C.cfg.get('rwkv', True):
        phase_rwkv(C, e)


def bcast_row(C, dst, vec_ap_1d, n, bdst):
    src = bass.AP(vec_ap_1d.tensor, vec_ap_1d.offset, [[0, 128], [1, n]])
    C.S.op('sp', lambda e: e.dma_start(out=dst, in_=src), writes=[bdst], dma=Buf('bc'))


def phase_ssd(C, e):
    nc, S, N, W = C.nc, C.S, C.N, C.W
    TT = 512
    with ExitStack() as st:
        sb = lambda name, shape, dt=F32: st.enter_context(_sbt(nc, name, list(shape), dt))
        bset = Buf('ss_setup')
        wnat = sb('ss_wnat', [5, 1536])
        S.op('sp', lambda e_: e_.dma_start(out=wnat[:], in_=W['ssd_conv_w'][e]), writes=[bset], dma=Buf('x'))
        wcol = sb('ss_wcol', [128, 12, 5])
        for c in range(12):
            ps, bps = next_ps(C)
            _mm(S, ps[:, 0:5], wnat[0:5, c * 128:(c + 1) * 128], C.ident[0:5, 0:5], True, True, [bset, C.bconst], [bps])
            S.op('dve', lambda e_, c=c, ps=ps: e_.tensor_copy(wcol[:, c, :], ps[:, 0:5]), reads=[bps], writes=[bset])
        dg = sb('ss_dg', [128, 12, 5, 128], BF16)
        for c in range(12):
            for k in range(5):
                S.op('dve', lambda e_, c=c, k=k: e_.tensor_scalar(dg[:, c, k, :], C.identb[:], wcol[:, c, k:k + 1], None, ALU.mult),
                     reads=[bset, C.bconst], writes=[bset])
        cvb = sb('ss_cvb', [128, 12])
        load_cols(C, cvb[:], bset, W['ssd_conv_b'][e], 12)
        nrw = sb('ss_nrw', [128, 8])
        load_cols(C, nrw[:], bset, W['ssd_norm_w'][e], 8)
        dtb = sb('ss_dtb', [128, 32])
        bcast_row(C, dtb[:], W['ssd_dt_bias'][e].rearrange("a b -> (a b)"), 32, bset)
        Ab = sb('ss_Ab', [128, 32])
        bcast_row(C, Ab[:], W['ssd_a_log'][e].rearrange("a b -> (a b)"), 32, bset)
        S.op('act', lambda e_: e_.activation(Ab[:], Ab[:], AF.Exp), reads=[bset], writes=[bset])
        S.op('dve', lambda e_: e_.tensor_scalar(Ab[:], Ab[:], -1.0, None, ALU.mult), reads=[bset], writes=[bset])
        dsk = sb('ss_dsk', [128, 16])
        bcast_row(C, dsk[:], W['ssd_d'][e], 16, bset)
        onesf = sb('ss_onesf', [128, 128])
        S.op('dve', lambda e_: e_.memset(onesf[:], 1.0), writes=[bset])
        tri = sb('ss_tri', [128, 2, 128])
        S.op('sp', lambda e_: e_.dma_start(out=tri[:], in_=C.c_tri), writes=[bset], dma=Buf('x'))
        nmf = sb('ss_nmf', [128, 2, 128])
        S.op('sp', lambda e_: e_.dma_start(out=nmf[:], in_=C.c_nm), writes=[bset], dma=Buf('x'))
        nm4 = sb('ss_nm4', [128, 2, 4, 128], BF16)
        for d in range(2):
            for j in range(4):
                S.op('dve', lambda e_, d=d, j=j: e_.tensor_copy(nm4[:, d, j, :], nmf[:, d, :]), reads=[bset], writes=[bset])

        xin = [sb('ss_xin%d' % i, [128, 12, 516]) for i in range(2)]
        bxin = [Buf('ss_xin%d' % i) for i in range(2)]
        xbf = sb('ss_xbf', [128, 12, 516], BF16)
        bxbf = Buf('ss_xbf')
        xcf = sb('ss_xcf', [128, 8, TT])
        bxcf = Buf('ss_xcf')
        bcf = sb('ss_bcf', [128, 4, TT], BF16)
        bbcf = Buf('ss_bcf')
        xT = sb('ss_xT', [128, 4, 1024])
        bxT = Buf('ss_xT')
        BT = sb('ss_BT', [128, 4, 256], BF16)
        bBT = Buf('ss_BT')
        Sst = sb('ss_S', [128, 1024])
        Sb = sb('ss_Sb', [128, 1024], BF16)
        bS = Buf('ss_S')
        smts = [sb('ss_sm%d' % i, [128, 12, 4, 16]) for i in range(2)]
        bsms = [Buf('ss_sm%d' % i) for i in range(2)]
        R = sb('ss_R', [128, 16, 128])
        bR = Buf('ss_R')
        Dm = sb('ss_Dm', [128, 16, 128], BF16)
        bDm = Buf('ss_Dm')
        cb = sb('ss_cb', [128, 2, 128], BF16)
        bcb = Buf('ss_cb')
        Mt = sb('ss_Mt', [128, 16, 128], BF16)
        bMt = Buf('ss_Mt')
        xdt = sb('ss_xdt', [128, 1024], BF16)
        xdt2 = sb('ss_xdt2', [128, 1024], BF16)
        bxdt = Buf('ss_xdt')
        tmp = sb('ss_tmp', [128, 1024])
        btmp = Buf('ss_tmp')
        yac = [sb('ss_yac%d' % i, [128, 1024]) for i in range(2)]
        byac = [Buf('ss_yac%d' % i) for i in range(2)]
        zt = [sb('ss_z%d' % i, [128, 1024]) for i in range(2)]
        bzt = [Buf('ss_z%d' % i) for i in range(2)]
        dtr = [sb('ss_dtr%d' % i, [128, 4, 32]) for i in range(2)]
        bdtr = [Buf('ss_dtr%d' % i) for i in range(2)]
        ynb = sb('ss_ynb', [128, 1024], BF16)
        bynb = Buf('ss_ynb')
        yo = [sb('ss_yo%d' % i, [128, 8, TT], BF16) for i in range(2)]
        byo = [Buf('ss_yo%d' % i) for i in range(2)]
        Yv = C.Y.rearrange("(c p) n -> p c n", p=128)
        UFv = C.UF.rearrange("(c p) n -> p c n", p=128)
        bYS = C.bYS
        nt = N // TT
        it = 0
        ci = 0
        ti = 0
        for d in range(2):
            order = list(range(nt)) if d == 0 else list(range(nt - 1, -1, -1))
            for t in order:
                t0 = t * TT
                kl, kr = bkind(C, t0), bkind(C, t0 + TT)
                lo = 2 if kl == 'hard' else 0
                hi = 514 if kr == 'hard' else 516
                xi, bxi = xin[it % 2], bxin[it % 2]
                it += 1
                for (c0_, c1_) in ((0, 6), (6, 12)):
                    S.op('sp', lambda e_, xi=xi, c0_=c0_, c1_=c1_, t0=t0, lo=lo, hi=hi: e_.dma_start(
                        out=xi[:, c0_:c1_, lo:hi], in_=UFv[:, c0_:c1_, t0 - 2 + lo:t0 - 2 + hi]),
                        reads=[C.bUF], writes=[bxi], dma=bxi)
                if lo > 0:
                    S.op('pool', lambda e_: e_.memset(xbf[:, :, 0:2], 0.0), writes=[bxbf])
                if hi < 516:
                    S.op('pool', lambda e_: e_.memset(xbf[:, :, 514:516], 0.0), writes=[bxbf])
                S.op('act', lambda e_, xi=xi, lo=lo, hi=hi: e_.copy(xbf[:, :, lo:hi], xi[:, :, lo:hi]), reads=[bxi], writes=[bxbf])
                if kl == 'soft':
                    S.op('dve', lambda e_: e_.tensor_scalar(xbf[:, :, 0:2], xbf[:, :, 0:2], C.flagt[:, 0:1], None, ALU.mult),
                         reads=[bxbf, C.bconst], writes=[bxbf])
                if kr == 'soft':
                    S.op('dve', lambda e_: e_.tensor_scalar(xbf[:, :, 514:516], xbf[:, :, 514:516], C.flagt[:, 0:1], None, ALU.mult),
                         reads=[bxbf, C.bconst], writes=[bxbf])
                for c in range(12):
                    ps, bps = next_ps(C)
                    for k in range(5):
                        _mm(S, ps[:, :], dg[:, c, k, :], xbf[:, c, k:k + TT], k == 0, k == 4, [bset, bxbf], [bps])
                    if c < 8:
                        S.op('act', lambda e_, c=c, ps=ps: e_.activation(xcf[:, c, :], ps[:, :], AF.Silu, bias=cvb[:, c:c + 1], scale=1.0),
                             reads=[bps, bset], writes=[bxcf])
                    else:
                        S.op('act', lambda e_, c=c, ps=ps: e_.activation(bcf[:, c - 8, :], ps[:, :], AF.Silu, bias=cvb[:, c:c + 1], scale=1.0),
                             reads=[bps, bset], writes=[bbcf])
                for s4 in range(4):
                    for half in range(2):
                        ps, bps = next_ps(C)
                        for j in range(4):
                            c = half * 4 + j
                            _mm(S, ps[:, j * 128:(j + 1) * 128], xcf[:, c, s4 * 128:(s4 + 1) * 128], C.ident[:], True, True,
                                [bxcf, C.bconst], [bps])
                        evac(C, half, xT[:, s4, half * 512:(half + 1) * 512], ps[:, :], bps, bxT)
                    ps, bps = next_ps(C)
                    for g in range(2):
                        _mm(S, ps[:, g * 128:(g + 1) * 128], bcf[:, g, s4 * 128:(s4 + 1) * 128], C.identb[:], True, True,
                            [bbcf, C.bconst], [bps])
                    S.op('dve', lambda e_, s4=s4, ps=ps: e_.tensor_copy(BT[:, s4, :], ps[:, 0:256]), reads=[bps], writes=[bBT])
                smt, bsm = smts[ti % 2], bsms[ti % 2]
                dt_, bdt_ = dtr[ti % 2], bdtr[ti % 2]
                ti += 1
                dsl = slice(d * 16, d * 16 + 16)
                S.op('sp', lambda e_, dt_=dt_, t0=t0: e_.dma_start(
                    out=dt_[:], in_=C.UT[t0:t0 + TT, 1024:1056].rearrange("(s p) c -> p s c", p=128)),
                    reads=[C.bUT], writes=[bdt_], dma=bdt_)
                S.op('dve', lambda e_, dt_=dt_, dsl=dsl, smt=smt: e_.tensor_tensor(
                    smt[:, 0, :, :], dt_[:, :, dsl], dtb[:, dsl].unsqueeze(1).to_broadcast([128, 4, 16]), ALU.add),
                    reads=[bdt_, bset], writes=[bsm])
                S.op('act', lambda e_, smt=smt: e_.activation(smt[:, 1, :, :], smt[:, 0, :, :], AF.Abs), reads=[bsm], writes=[bsm])
                S.op('act', lambda e_, smt=smt: e_.activation(smt[:, 1, :, :], smt[:, 1, :, :], AF.Exp, scale=-1.0), reads=[bsm], writes=[bsm])
                S.op('act', lambda e_, smt=smt: e_.activation(smt[:, 1, :, :], smt[:, 1, :, :], AF.Ln, bias=C.onec[:, 0:1], scale=1.0),
                     reads=[bsm, C.bconst], writes=[bsm])
                S.op('dve', lambda e_, smt=smt: e_.scalar_tensor_tensor(smt[:, 2, :, :], smt[:, 0, :, :], 0.0, smt[:, 1, :, :], ALU.max, ALU.add),
                     reads=[bsm], writes=[bsm])
                S.op('dve', lambda e_, dsl=dsl, smt=smt: e_.tensor_tensor(
                    smt[:, 3, :, :], smt[:, 2, :, :], Ab[:, dsl].unsqueeze(1).to_broadcast([128, 4, 16]), ALU.mult),
                    reads=[bsm, bset], writes=[bsm])
                psc, bpsc = next_ps(C)
                for s4 in range(4):
                    _mm(S, psc[:, s4 * 32:s4 * 32 + 16], tri[:, d, :], smt[:, 3, s4, :], True, True, [bset, bsm], [bpsc])
                    _mm(S, psc[:, s4 * 32 + 16:s4 * 32 + 32], onesf[:], smt[:, 3, s4, :], True, True, [bset, bsm], [bpsc])
                pv = psc[:, 0:128].rearrange("p (s a h) -> p s a h", s=4, a=2)
                S.op('dve', lambda e_, pv=pv, smt=smt: e_.tensor_copy(smt[:, 4, :, :], pv[:, :, 0, :]), reads=[bpsc], writes=[bsm])
                S.op('dve', lambda e_, pv=pv, smt=smt: e_.tensor_scalar(smt[:, 5, :, :], pv[:, :, 0, :], -1.0, None, ALU.mult), reads=[bpsc], writes=[bsm])
                S.op('dve', lambda e_, pv=pv, smt=smt: e_.tensor_tensor(smt[:, 7, :, :], pv[:, :, 1, :], smt[:, 4, :, :], ALU.subtract),
                     reads=[bpsc, bsm], writes=[bsm])
                S.op('act', lambda e_, pv=pv, smt=smt: e_.activation(smt[:, 6, :, :], pv[:, :, 0, :], AF.Exp), reads=[bpsc], writes=[bsm])
                S.op('act', lambda e_, pv=pv, smt=smt: e_.activation(smt[:, 9, :, :], pv[:, :, 1, :], AF.Exp), reads=[bpsc], writes=[bsm])
                S.op('act', lambda e_, smt=smt: e_.activation(smt[:, 8, :, :], smt[:, 7, :, :], AF.Exp), reads=[bsm], writes=[bsm])
                S.op('dve', lambda e_, smt=smt: e_.tensor_tensor(smt[:, 8, :, :], smt[:, 8, :, :], smt[:, 2, :, :], ALU.mult), reads=[bsm], writes=[bsm])
                chunks = list(range(4)) if d == 0 else [3, 2, 1, 0]
                yo_, byo_ = yo[t % 2], byo[t % 2]
                for s4 in chunks:
                    c0 = t0 + s4 * 128
                    bpos = c0 if d == 0 else c0 + 128
                    bk = bkind(C, bpos)
                    if bk == 'hard':
                        S.op('pool', lambda e_: e_.memset(Sst[:], 0.0), writes=[bS])
                        S.op('pool', lambda e_: e_.memset(Sb[:], 0.0), writes=[bS])
                    elif bk == 'soft':
                        S.op('dve', lambda e_: e_.tensor_scalar(Sst[:], Sst[:], C.flagt[:, 0:1], None, ALU.mult),
                             reads=[bS, C.bconst], writes=[bS])
                        S.op('dve', lambda e_: e_.tensor_scalar(Sb[:], Sb[:], C.flagt[:, 0:1], None, ALU.mult),
                             reads=[bS, C.bconst], writes=[bS])
                    sm = smt[:, :, s4, :]
                    z_, bz_ = zt[ci % 2], bzt[ci % 2]
                    ya, bya = yac[ci % 2], byac[ci % 2]
                    ci += 1
                    S.op('dve', lambda e_, sm=sm, d=d: e_.tensor_tensor(
                        R[:], tri[:, d:d + 1, :].to_broadcast([128, 16, 128]),
                        sm[:, 3, :].unsqueeze(2).to_broadcast([128, 16, 128]), ALU.mult),
                        reads=[bset, bsm], writes=[bR])
                    pcb, bpcb = next_ps(C)
                    for g in range(2):
                        _mm(S, pcb[:, g * 128:(g + 1) * 128], bcf[:, g, s4 * 128:(s4 + 1) * 128],
                            bcf[:, 2 + g, s4 * 128:(s4 + 1) * 128], True, True, [bbcf], [bpcb])
                    S.op('act', lambda e_, sm=sm, pcb=pcb: e_.copy(cb[:].rearrange("p g l -> p (g l)"), pcb[:, 0:256]), reads=[bpcb], writes=[bcb])
                    for q4 in range(4):
                        ps, bps = next_ps(C)
                        _mm(S, ps[:, :], onesf[:], R[:, q4 * 4:(q4 + 1) * 4, :].rearrange("p h l -> p (h l)"), True, False, [bset, bR], [bps])
                        _mm(S, ps[:, :], C.identb[:], nm4[:, d, :, :].rearrange("p j l -> p (j l)"), False, True, [bset, C.bconst], [bps])
                        for j in range(4):
                            h = q4 * 4 + j
                            S.op('act', lambda e_, sm=sm, ps=ps, j=j, h=h: e_.activation(Dm[:, h, :], ps[:, j * 128:(j + 1) * 128], AF.Exp,
                                                                               bias=sm[:, 5, h:h + 1], scale=1.0),
                                 reads=[bps, bsm], writes=[bDm])
                    for g in range(2):
                        S.op('dve', lambda e_, sm=sm, g=g: e_.tensor_tensor(Mt[:, g * 8:(g + 1) * 8, :], Dm[:, g * 8:(g + 1) * 8, :],
                                                                   cb[:, g:g + 1, :].to_broadcast([128, 8, 128]), ALU.mult),
                             reads=[bDm, bcb], writes=[bMt])
                    xv = xT[:, s4, :].rearrange("p (h q) -> p h q", h=16)
                    S.op('dve', lambda e_, sm=sm, xv=xv: e_.tensor_tensor(xdt[:].rearrange("p (h q) -> p h q", h=16), xv,
                                                                 sm[:, 2, :].unsqueeze(2).to_broadcast([128, 16, 64]), ALU.mult),
                         reads=[bxT, bsm], writes=[bxdt])
                    S.op('dve', lambda e_, sm=sm, xv=xv: e_.tensor_tensor(xdt2[:].rearrange("p (h q) -> p h q", h=16), xv,
                                                                 sm[:, 8, :].unsqueeze(2).to_broadcast([128, 16, 64]), ALU.mult),
                         reads=[bxT, bsm], writes=[bxdt])
                    pof = []
                    for g in range(2):
                        ps, bps = next_ps(C)
                        _mm(S, ps[:, :], bcf[:, 2 + g, s4 * 128:(s4 + 1) * 128], Sb[:, g * 512:(g + 1) * 512], True, True, [bbcf, bS], [bps])
                        pof.append((ps, bps))
                    for g in range(2):
                        ps, bps = pof[g]
                        S.op('dve', lambda e_, sm=sm, g=g, ps=ps: e_.tensor_tensor(
                            tmp[:, g * 512:(g + 1) * 512].rearrange("p (h q) -> p h q", h=8),
                            ps[:, :].rearrange("p (h q) -> p h q", h=8),
                            sm[:, 6, g * 8:(g + 1) * 8].unsqueeze(2).to_broadcast([128, 8, 64]), ALU.mult),
                            reads=[bps, bsm], writes=[btmp])
                    pyd = []
                    for g in range(2):
                        ps, bps = next_ps(C)
                        for j in range(8):
                            h = g * 8 + j
                            _mm(S, ps[:, j * 64:(j + 1) * 64], Mt[:, h, :], xdt[:, h * 64:(h + 1) * 64], True, True, [bMt, bxdt], [bps])
                        pyd.append((ps, bps))
                    if d == 0:
                        for g in range(2):
                            ps, bps = pyd[g]
                            S.op('dve', lambda e_, sm=sm, g=g, ps=ps, ya=ya: e_.tensor_tensor(ya[:, g * 512:(g + 1) * 512], ps[:, :],
                                                                                   tmp[:, g * 512:(g + 1) * 512], ALU.add),
                                 reads=[bps, btmp], writes=[bya])
                        S.op('dve', lambda e_, sm=sm, xv=xv: e_.tensor_tensor(tmp[:].rearrange("p (h q) -> p h q", h=16), xv,
                                                                     dsk[:, :].unsqueeze(2).to_broadcast([128, 16, 64]), ALU.mult),
                             reads=[bxT, bset], writes=[btmp])
                        S.op('dve', lambda e_, sm=sm, ya=ya: e_.tensor_tensor(ya[:], ya[:], tmp[:], ALU.add), reads=[bya, btmp], writes=[bya])
                        S.op('pool', lambda e_, sm=sm, ya=ya, c0=c0: e_.dma_start(out=C.YS[c0:c0 + 128, :], in_=ya[:]),
                             reads=[bya], writes=[bYS], dma=bya)
                    else:
                        S.op('sp', lambda e_, sm=sm, ya=ya, c0=c0: e_.dma_start(out=ya[:], in_=C.YS[c0:c0 + 128, :]),
                             reads=[bYS], writes=[bya], dma=bya)
                        S.op('sp', lambda e_, sm=sm, z_=z_, c0=c0: e_.dma_start(out=z_[:], in_=C.UT[c0:c0 + 128, 0:1024]),
                             reads=[C.bUT], writes=[bz_], dma=bz_)
                        S.op('dve', lambda e_, sm=sm, ya=ya: e_.tensor_tensor(ya[:], ya[:], tmp[:], ALU.add), reads=[bya, btmp], writes=[bya])
                        for g in range(2):
                            ps, bps = pyd[g]
                            S.op('dve', lambda e_, sm=sm, g=g, ps=ps, ya=ya: e_.tensor_tensor(ya[:, g * 512:(g + 1) * 512], ps[:, :],
                                                                                   ya[:, g * 512:(g + 1) * 512], ALU.add),
                                 reads=[bps, bya], writes=[bya])
                        S.op('act', lambda e_, sm=sm, z_=z_: e_.activation(z_[:], z_[:], AF.Silu), reads=[bz_], writes=[bz_])
                        S.op('dve', lambda e_, sm=sm, ya=ya, z_=z_: e_.tensor_tensor(ya[:], ya[:], z_[:], ALU.mult), reads=[bya, bz_], writes=[bya])
                        S.op('act', lambda e_, sm=sm, ya=ya: e_.activation(tmp[:], ya[:], AF.Square), reads=[bya], writes=[btmp])
                        S.op('dve', lambda e_, sm=sm: e_.reduce_sum(sm[:, 10, 0:2], tmp[:].rearrange("p (g f) -> p g f", g=2), AX.X),
                             reads=[btmp], writes=[bsm])
                        S.op('act', lambda e_, sm=sm: e_.activation(sm[:, 10, 0:2], sm[:, 10, 0:2], AF.Sqrt, bias=C.epsD[:, 1:2], scale=1.0 / 512),
                             reads=[bsm, C.bconst], writes=[bsm])
                        S.op('dve', lambda e_, sm=sm: e_.reciprocal(sm[:, 10, 0:2], sm[:, 10, 0:2]), reads=[bsm], writes=[bsm])
                        S.op('dve', lambda e_, sm=sm, ya=ya: e_.tensor_tensor(ynb[:].rearrange("p (g f) -> p g f", g=2),
                                                                     ya[:].rearrange("p (g f) -> p g f", g=2),
                                                                     sm[:, 10, 0:2].unsqueeze(2).to_broadcast([128, 2, 512]), ALU.mult),
                             reads=[bya, bsm], writes=[bynb])
                        for half in range(2):
                            ps, bps = next_ps(C)
                            for j in range(4):
                                c = half * 4 + j
                                _mm(S, ps[:, j * 128:(j + 1) * 128], ynb[:, c * 128:(c + 1) * 128], C.identb[:], True, True,
                                    [bynb, C.bconst], [bps])
                            S.op('dve', lambda e_, sm=sm, half=half, ps=ps, yo_=yo_, s4=s4: e_.tensor_tensor(
                                yo_[:, half * 4:(half + 1) * 4, s4 * 128:(s4 + 1) * 128],
                                ps[:, :].rearrange("p (j l) -> p j l", j=4),
                                nrw[:, half * 4:(half + 1) * 4].unsqueeze(2).to_broadcast([128, 4, 128]), ALU.mult),
                                reads=[bps, bset], writes=[byo_])
                    for g in range(2):
                        ps, bps = next_ps(C)
                        _mm(S, ps[:, :], BT[:, s4, g * 128:(g + 1) * 128], xdt2[:, g * 512:(g + 1) * 512], True, True, [bBT, bxdt], [bps])
                        S.op('dve', lambda e_, sm=sm, g=g: e_.tensor_tensor(
                            Sst[:, g * 512:(g + 1) * 512].rearrange("p (h q) -> p h q", h=8),
                            Sst[:, g * 512:(g + 1) * 512].rearrange("p (h q) -> p h q", h=8),
                            sm[:, 9, g * 8:(g + 1) * 8].unsqueeze(2).to_broadcast([128, 8, 64]), ALU.mult),
                            reads=[bS, bsm], writes=[bS])
                        S.op('dve', lambda e_, sm=sm, g=g, ps=ps: e_.tensor_tensor(Sst[:, g * 512:(g + 1) * 512], Sst[:, g * 512:(g + 1) * 512],
                                                                         ps[:, :], ALU.add),
                             reads=[bS, bps], writes=[bS])
                    S.op('act', lambda e_, sm=sm: e_.copy(Sb[:], Sst[:]), reads=[bS], writes=[bS])
                if d == 1:
                    S.op('pool', lambda e_, sm=sm, yo_=yo_, t0=t0: e_.dma_start(out=Yv[:, 0:8, t0:t0 + TT], in_=yo_[:]),
                         reads=[byo_], writes=[C.bY], dma=byo_)
        S.end_phase()


def phase_rwkv(C, e):
    nc, S, N, W = C.nc, C.S, C.N, C.W
    TT = 512
    KD = 0.6065306597126334
    stage = C.cfg.get('rw_stage', 99)
    with ExitStack() as st:
        sb = lambda name, shape, dt=F32: st.enter_context(_sbt(nc, name, list(shape), dt))
        bset = Buf('rw_setup')
        mu = sb('rw_mu', [128, 26])
        load_cols(C, mu[:], bset, W['rwkv_mu'][e], 26)
        hm = sb('rw_hm', [128, 2, 26])
        S.op('dve', lambda e_: e_.tensor_scalar(hm[:, 0, :], mu[:], 0.5, None, ALU.mult), reads=[bset], writes=[bset])
        S.op('dve', lambda e_: e_.tensor_scalar(hm[:, 1, :], mu[:], -1.0, 1.0, ALU.mult, ALU.add), reads=[bset], writes=[bset])
        dgs = sb('rw_dgs', [128, 26, 2, 128], BF16)
        for c in range(26):
            for k in range(2):
                S.op('dve', lambda e_, c=c, k=k: e_.tensor_scalar(dgs[:, c, k, :], C.identb[:], hm[:, k, c:c + 1], None, ALU.mult),
                     reads=[bset, C.bconst], writes=[bset])
        w2b = sb('rw_w2b', [128, 2, 1024], BF16)
        a2p = sb('rw_a2p', [128, 1024], BF16)
        S.op('pool', lambda e_: e_.dma_start(out=a2p[64:128, :], in_=W['rwkv_a2'][e]), writes=[bset], dma=Buf('x'))
        g2b = sb('rw_g2b', [128, 1024], BF16)
        S.op('pool', lambda e_: e_.dma_start(out=g2b[:, :], in_=W['rwkv_g2'][e]), writes=[bset], dma=Buf('x'))
        cols = sb('rw_cols', [128, 7, 8])
        load_cols(C, cols[:, 0, :], bset, W['rwkv_a0'][e], 8)
        load_cols(C, cols[:, 1, :], bset, W['rwkv_k_k'][e], 8)
        load_cols(C, cols[:, 2, :], bset, W['rwkv_k_a'][e], 8)
        load_cols(C, cols[:, 4, :], bset, W['rwkv_r_k'][e].rearrange("a b -> (a b)"), 8)
        load_cols(C, cols[:, 5, :], bset, W['rwkv_ln_w'][e], 8)
        load_cols(C, cols[:, 6, :], bset, W['rwkv_ln_b'][e], 8)
        S.op('dve', lambda e_: e_.tensor_scalar(cols[:, 3, :], cols[:, 2, :], -1.0, None, ALU.mult), reads=[bset], writes=[bset])
        tri = sb('rw_tri', [128, 2, 128])
        S.op('sp', lambda e_: e_.dma_start(out=tri[:], in_=C.c_tri), writes=[bset], dma=Buf('x'))
        tri2 = sb('rw_tri2', [128, 2, 128])
        S.op('sp', lambda e_: e_.dma_start(out=tri2[:], in_=C.c_tri2), writes=[bset], dma=Buf('x'))
        m4 = sb('rw_m4', [128, 2, 4, 128], BF16)
        for d in range(2):
            for j in range(4):
                srcm = tri2 if j % 2 == 0 else tri
                S.op('dve', lambda e_, d=d, j=j, srcm=srcm: e_.tensor_copy(m4[:, d, j, :], srcm[:, d, :]), reads=[bset], writes=[bset])
        bd = sb('rw_bd', [128, 128])
        S.op('sp', lambda e_: e_.dma_start(out=bd[:], in_=C.c_bd), writes=[bset], dma=Buf('x'))

        st0 = ExitStack()
        w0f = st0.enter_context(_sbt(nc, 'rw_w0f', [128, 2, 2, 1024], F32))
        for d in range(2):
            S.op('pool', lambda e_, d=d: e_.dma_start(out=w2b[0:64, d, :], in_=W['rwkv_w2'][e][d]), writes=[bset], dma=Buf('x'))
            S.op('sp', lambda e_, d=d: e_.dma_start(out=w0f[64:65, d, 0, :], in_=W['rwkv_w0'][e][d:d + 1, :]), writes=[bset], dma=Buf('x'))
            S.op('sp', lambda e_, d=d: e_.dma_start(out=w0f[65:66, d, 0, :], in_=W['rwkv_w0'][e][d:d + 1, :]), writes=[bset], dma=Buf('x'))
        tmpb = st0.enter_context(_sbt(nc, 'rw_tmpb', [128, 2, 1024], BF16))
        S.op('dve', lambda e_: e_.tensor_copy(tmpb[64:66, :, :], w0f[64:66, :, 0, :]), reads=[bset], writes=[bset])
        S.op('dve', lambda e_: e_.tensor_copy(w2b[64:66, :, :], tmpb[64:66, :, :]), reads=[bset], writes=[bset])
        S.op('dve', lambda e_: e_.tensor_copy(w0f[64:66, :, 1, :], tmpb[64:66, :, :]), reads=[bset], writes=[bset])
        S.op('dve', lambda e_: e_.tensor_tensor(w0f[64:66, :, 0, :], w0f[64:66, :, 0, :], w0f[64:66, :, 1, :], ALU.subtract),
             reads=[bset], writes=[bset])
        S.op('dve', lambda e_: e_.tensor_copy(tmpb[64:66, :, :], w0f[64:66, :, 0, :]), reads=[bset], writes=[bset])
        S.op('sp', lambda e_: e_.dma_start(out=w2b[65:66, :, :], in_=tmpb[65:66, :, :]), reads=[bset], writes=[bset], dma=Buf('x'))
        S.barrier()
        S.flush()
        st0.close()
        NPB = 6
        pbb = [sb('rw_pb%d' % i, [128, 514], BF16) for i in range(NPB)]
        bpbb = [Buf('rw_pb%d' % i) for i in range(NPB)]
        tcw = sb('rw_tcw', [128, TT], BF16)
        btcw = Buf('rw_tcw')
        S.op('pool', lambda e_: e_.memset(tcw[64:66, :], 1.0), writes=[btcw])
        cab = sb('rw_cab', [128, TT], BF16)
        bcab = Buf('rw_cab')
        scg = sb('rw_scg', [128, TT], BF16)
        bscg = Buf('rw_scg')
        tmpn = ['al', 'kf', 'kkr', 'sq', 'kk32', 't1', 'bon']
        tm = {n: sb('rw_t_' + n, [128, TT]) for n in tmpn}
        btm = {n: Buf('rw_t_' + n) for n in tmpn}
        nkT = sb('rw_nkT', [128, 8, TT], BF16)
        bT = sb('rw_bT', [128, 8, TT], BF16)
        kmT = sb('rw_kmT', [128, 8, TT], BF16)
        rTb = sb('rw_rTb', [128, 8, TT], BF16)
        vTb = sb('rw_vTb', [128, 8, TT], BF16)
        bvT = sb('rw_bv', [128, 8, TT], BF16)
        gT = sb('rw_gT', [128, 8, TT], BF16)
        bprep = Buf('rw_prep')
        Vt = sb('rw_Vt', [128, 1024], BF16)
        bVt = Buf('rw_Vt')
        sig = sb('rw_sig', [128, 1024])
        bsig = Buf('rw_sig')
        eLi = sb('rw_eLi', [128, 8, 128])
        eLn = sb('rw_eLn', [128, 8, 128])
        eLx = sb('rw_eLx', [128, 8, 128])
        beL = Buf('rw_eL')
        AR = sb('rw_AR', [128, 8, 2, 128], BF16)
        BK = sb('rw_BK', [128, 2, 8, 128], BF16)
        bARBK = Buf('rw_ARBK')
        bkTp = sb('rw_bkTp', [128, 2, 2, 8, 128], BF16)
        bbkT = Buf('rw_bkT')
        S.op('pool', lambda e_: e_.memset(bkTp[:], 0.0), writes=[bbkT])
        AB4 = sb('rw_AB4', [128, 16, 512], BF16)
        bAB4 = Buf('rw_AB4')
        Xm = sb('rw_X', [128, 16, 128], BF16)
        bXm = [Buf('rw_X%d' % i) for i in range(4)]
        Pp = [sb('rw_P%d' % i, [128, 2, 16, 128], BF16) for i in range(2)]
        bPp = [[Buf('rw_P%d_%d' % (i, g)) for g in range(4)] for i in range(2)]
        Gs = sb('rw_Gs', [128, 1024], BF16)
        bGs = Buf('rw_Gs')
        Us = sb('rw_Us', [128, 1024], BF16)
        bUs = Buf('rw_Us')
        Sst = sb('rw_S', [128, 8, 64])
        Sbb = sb('rw_Sb', [128, 8, 64], BF16)
        bS = Buf('rw_S')
        ya = sb('rw_ya', [128, 1024])
        bya = Buf('rw_ya')
        yfl = sb('rw_yf', [128, 1024])
        byfl = Buf('rw_yf')
        stt = sb('rw_st', [128, 2, 16])
        bstt = Buf('rw_st')
        ynb = sb('rw_ynb', [128, 1024], BF16)
        bynb = Buf('rw_ynb')
        fin = sb('rw_fin', [128, 8, 128])
        bfin = Buf('rw_fin')
        yo = sb('rw_yo', [128, 8, TT], BF16)
        byo = Buf('rw_yo')
        Yv = C.Y.rearrange("(c p) n -> p c n", p=128)
        UFv = C.UF.rearrange("(c p) n -> p c n", p=128)
        nt = N // TT
        cnt = {'px': 0, 'pb': 0, 'ev': 0}

        UFbv = C.UFb.rearrange("(c p) n -> p c n", p=128)

        def conv_chunk(ch, t0, kl, kr, lo, hi):
            pb, bpb = pbb[cnt['pb'] % NPB], bpbb[cnt['pb'] % NPB]
            cnt['pb'] += 1
            S.op('sp', lambda e_: e_.dma_start(out=pb[:, lo:hi], in_=UFbv[:, ch, t0 - 1 + lo:t0 - 1 + hi]),
                 reads=[C.bUF], writes=[bpb], dma=bpb)
            if lo > 0:
                S.op('pool', lambda e_: e_.memset(pb[:, 0:1], 0.0), writes=[bpb])
            if hi < 514:
                S.op('pool', lambda e_: e_.memset(pb[:, 513:514], 0.0), writes=[bpb])
            if kl == 'soft':
                S.op('pool', lambda e_: e_.tensor_scalar(pb[:, 0:1], pb[:, 0:1], C.flagt[:, 0:1], None, ALU.mult),
                     reads=[bpb, C.bconst], writes=[bpb])
            if kr == 'soft':
                S.op('pool', lambda e_: e_.tensor_scalar(pb[:, 513:514], pb[:, 513:514], C.flagt[:, 0:1], None, ALU.mult),
                     reads=[bpb, C.bconst], writes=[bpb])
            ps, bps = next_ps(C)
            _mm(S, ps[:, :], dgs[:, ch, 0, :], pb[:, 0:TT], True, False, [bset, bpb], [bps])
            _mm(S, ps[:, :], dgs[:, ch, 1, :], pb[:, 1:TT + 1], False, False, [bset, bpb], [bps])
            _mm(S, ps[:, :], dgs[:, ch, 0, :], pb[:, 2:TT + 2], False, True, [bset, bpb], [bps])
            return ps, bps

        for d in range(2):
            if stage < 1:
                break
            order = list(range(nt)) if d == 0 else list(range(nt - 1, -1, -1))
            last = 127 if d == 0 else 0
            for t in order:
                t0 = t * TT
                kl, kr = bkind(C, t0), bkind(C, t0 + TT)
                lo = 1 if kl == 'hard' else 0
                hi = 513 if kr == 'hard' else 514
                ps, bps = conv_chunk(24, t0, kl, kr, lo, hi)
                S.op('act', lambda e_, ps=ps: e_.activation(tcw[0:64, :], ps[0:64, :], AF.Tanh), reads=[bps], writes=[btcw])
                S.op('dve', lambda e_, ps=ps: e_.tensor_copy(cab[64:128, :], ps[64:128, :]), reads=[bps], writes=[bcab])
                if d == 1:
                    ps, bps = conv_chunk(25, t0, kl, kr, lo, hi)
                    S.op('act', lambda e_, ps=ps: e_.activation(scg[:], ps[:, :], AF.Sigmoid), reads=[bps], writes=[bscg])
                    for c in range(8):
                        ps, bps = next_ps(C)
                        _mm(S, ps[:, :], g2b[:, c * 128:(c + 1) * 128], scg[:], True, True, [bset, bscg], [bps])
                        evac(C, c, gT[:, c, :], ps[:, :], bps, bprep)
                for c in range(8):
                    psa, bpsa = next_ps(C)
                    _mm(S, psa[:, :], a2p[64:128, c * 128:(c + 1) * 128], cab[64:128, :], True, True, [bset, bcab], [bpsa])
                    psk, bpsk = conv_chunk(8 + c, t0, kl, kr, lo, hi)
                    psr, bpsr = conv_chunk(c, t0, kl, kr, lo, hi)
                    psv, bpsv = conv_chunk(16 + c, t0, kl, kr, lo, hi)
                    S.op('act', lambda e_, c=c, psa=psa: e_.activation(tm['al'][:], psa[:, :], AF.Sigmoid, bias=cols[:, 0, c:c + 1], scale=1.0),
                         reads=[bpsa, bset], writes=[btm['al']])
                    S.op('dve', lambda e_, c=c, psk=psk: e_.tensor_scalar(tm['kkr'][:], psk[:, :], cols[:, 1, c:c + 1], None, ALU.mult),
                         reads=[bpsk, bset], writes=[btm['kkr']])
                    S.op('act', lambda e_: e_.activation(tm['sq'][:], tm['kkr'][:], AF.Square), reads=[btm['kkr']], writes=[btm['sq']])
                    ps2, bps2 = next_ps(C)
                    _mm(S, ps2[:, :], bd[:], tm['sq'][:], True, True, [bset, btm['sq']], [bps2])
                    S.op('act', lambda e_, ps2=ps2: e_.activation(tm['sq'][:], ps2[:, :], AF.Sqrt), reads=[bps2], writes=[btm['sq']])
                    S.op('dve', lambda e_: e_.tensor_scalar(tm['sq'][:], tm['sq'][:], 1e-12, None, ALU.max), reads=[btm['sq']], writes=[btm['sq']])
                    S.op('dve', lambda e_: e_.reciprocal(tm['sq'][:], tm['sq'][:]), reads=[btm['sq']], writes=[btm['sq']])
                    S.op('dve', lambda e_: e_.tensor_tensor(tm['kk32'][:], tm['kkr'][:], tm['sq'][:], ALU.mult),
                         reads=[btm['kkr'], btm['sq']], writes=[btm['kk32']])
                    S.op('act', lambda e_, c=c: e_.activation(nkT[:, c, :], tm['kk32'][:], AF.Identity, scale=-1.0), reads=[btm['kk32']], writes=[bprep])
                    S.op('dve', lambda e_, c=c: e_.tensor_tensor(bT[:, c, :], tm['kk32'][:], tm['al'][:], ALU.mult),
                         reads=[btm['kk32'], btm['al']], writes=[bprep])
                    S.op('dve', lambda e_: e_.tensor_scalar(tm['t1'][:], tm['al'][:], -1.0, None, ALU.add),
                         reads=[btm['al']], writes=[btm['t1']])
                    S.op('dve', lambda e_, c=c: e_.tensor_scalar(tm['t1'][:], tm['t1'][:], cols[:, 2, c:c + 1], None, ALU.mult),
                         reads=[btm['t1'], bset], writes=[btm['t1']])
                    S.op('dve', lambda e_, psk=psk: e_.scalar_tensor_tensor(tm['t1'][:], tm['t1'][:], 1.0, psk[:, :], ALU.add, ALU.mult),
                         reads=[btm['t1'], bpsk], writes=[btm['t1']])
                    S.op('act', lambda e_, c=c: e_.copy(kmT[:, c, :], tm['t1'][:]), reads=[btm['t1']], writes=[bprep])
                    S.op('act', lambda e_, c=c, psr=psr: e_.copy(rTb[:, c, :], psr[:, :]), reads=[bpsr], writes=[bprep])
                    if d == 1:
                        S.op('dve', lambda e_, c=c, psr=psr: e_.scalar_tensor_tensor(tm['kk32'][:], psr[:, :], cols[:, 4, c:c + 1], tm['t1'][:],
                                                                                  ALU.mult, ALU.mult),
                             reads=[bpsr, bset, btm['t1']], writes=[btm['kk32']])
                        ps3, bps3 = next_ps(C)
                        _mm(S, ps3[:, :], bd[:], tm['kk32'][:], True, True, [bset, btm['kk32']], [bps3])
                        S.op('act', lambda e_, ps3=ps3: e_.copy(tm['bon'][:], ps3[:, :]), reads=[bps3], writes=[btm['bon']])
                    S.op('act', lambda e_, c=c, psv=psv: e_.copy(vTb[:, c, :], psv[:, :]), reads=[bpsv], writes=[bprep])
                    if d == 1:
                        S.op('dve', lambda e_, c=c, psv=psv: e_.tensor_tensor(bvT[:, c, :], psv[:, :], tm['bon'][:], ALU.mult),
                             reads=[bpsv, btm['bon']], writes=[bprep])
                chunks = list(range(4)) if d == 0 else [3, 2, 1, 0]
                if stage < 2:
                    chunks = []
                for s4 in chunks:
                    c0 = t0 + s4 * 128
                    ts = slice(s4 * 128, (s4 + 1) * 128)
                    bpos = c0 if d == 0 else c0 + 128
                    bk = bkind(C, bpos)
                    if bk == 'hard':
                        S.op('pool', lambda e_: e_.memset(Sst[:], 0.0), writes=[bS])
                        S.op('pool', lambda e_: e_.memset(Sbb[:], 0.0), writes=[bS])
                    elif bk == 'soft':
                        S.op('dve', lambda e_: e_.tensor_scalar(Sst[:], Sst[:], C.flagt[:, 0:1], None, ALU.mult),
                             reads=[bS, C.bconst], writes=[bS])
                        S.op('dve', lambda e_: e_.tensor_scalar(Sbb[:], Sbb[:], C.flagt[:, 0:1], None, ALU.mult),
                             reads=[bS, C.bconst], writes=[bS])
                    if d == 1:
                        S.op('sp', lambda e_, c0=c0: e_.dma_start(out=yfl[:], in_=C.YS[c0:c0 + 128, :]),
                             reads=[C.bYS], writes=[byfl], dma=byfl)
                    for half in range(2):
                        ps, bps = next_ps(C)
                        for j in range(4):
                            c = half * 4 + j
                            _mm(S, ps[:, j * 128:(j + 1) * 128], vTb[:, c, ts], C.identb[:], True, True, [bprep, C.bconst], [bps])
                        evac(C, half, Vt[:, half * 512:(half + 1) * 512], ps[:, :], bps, bVt)
                    for half in range(2):
                        ps, bps = next_ps(C)
                        _mm(S, ps[:, :], tcw[0:66, ts], w2b[0:66, d, half * 512:(half + 1) * 512], True, True, [btcw, bset], [bps])
                        S.op('act', lambda e_, half=half, ps=ps: e_.activation(sig[:, half * 512:(half + 1) * 512], ps[:, :], AF.Sigmoid),
                             reads=[bps], writes=[bsig])
                    if stage < 3:
                        continue
                    pli = []
                    for half in range(2):
                        ps, bps = next_ps(C)
                        for j in range(4):
                            c = half * 4 + j
                            _mm(S, ps[:, j * 128:(j + 1) * 128], sig[:, c * 128:(c + 1) * 128], tri[:, d, :], True, True, [bsig, bset], [bps])
                        pli.append((ps, bps))
                    plx = []
                    for half in range(2):
                        ps, bps = next_ps(C)
                        for j in range(4):
                            c = half * 4 + j
                            _mm(S, ps[:, j * 128:(j + 1) * 128], sig[:, c * 128:(c + 1) * 128], tri2[:, d, :], True, True, [bsig, bset], [bps])
                        plx.append((ps, bps))
                    for half in range(2):
                        ps, bps = pli[half]
                        dst = lambda tl: tl[:, half * 4:(half + 1) * 4, :].rearrange("p c l -> p (c l)")
                        S.op('act', lambda e_, ps=ps, o=dst(eLi): e_.activation(o, ps[:, :], AF.Exp, scale=-KD), reads=[bps], writes=[beL])
                        S.op('act', lambda e_, ps=ps, o=dst(eLn): e_.activation(o, ps[:, :], AF.Exp, scale=KD), reads=[bps], writes=[beL])
                        ps, bps = plx[half]
                        S.op('act', lambda e_, ps=ps, o=dst(eLx): e_.activation(o, ps[:, :], AF.Exp, scale=-KD), reads=[bps], writes=[beL])
                    if stage < 4:
                        continue
                    S.op('dve', lambda e_, ts=ts: e_.tensor_tensor(AR[:, :, 0, :], nkT[:, :, ts], eLx[:], ALU.mult), reads=[bprep, beL], writes=[bARBK])
                    S.op('dve', lambda e_, ts=ts: e_.tensor_tensor(AR[:, :, 1, :], rTb[:, :, ts], eLi[:], ALU.mult), reads=[bprep, beL], writes=[bARBK])
                    S.op('dve', lambda e_, ts=ts: e_.tensor_tensor(BK[:, 0, :, :], bT[:, :, ts], eLn[:], ALU.mult), reads=[bprep, beL], writes=[bARBK])
                    S.op('dve', lambda e_, ts=ts: e_.tensor_tensor(BK[:, 1, :, :], kmT[:, :, ts], eLn[:], ALU.mult), reads=[bprep, beL], writes=[bARBK])
                    if stage < 5:
                        continue
                    for q in range(2):
                        for half in range(2):
                            ps, bps = next_ps(C)
                            for j in range(4):
                                c = half * 4 + j
                                _mm(S, ps[:, j * 128:(j + 1) * 128], BK[:, q, c, :], C.identb[:], True, True, [bARBK, C.bconst], [bps])
                            pv = ps[:, :].rearrange("p (c a j) -> p c a j", c=4, a=2)
                            S.op('act', lambda e_, q=q, half=half, pv=pv: e_.copy(bkTp[:, 0, q, half * 4:(half + 1) * 4, 0:64], pv[:, :, 0, :]),
                                 reads=[bps], writes=[bbkT])
                            S.op('dve', lambda e_, q=q, half=half, pv=pv: e_.tensor_copy(bkTp[:, 1, q, half * 4:(half + 1) * 4, 64:128], pv[:, :, 1, :]),
                                 reads=[bps], writes=[bbkT])
                    if stage < 5.2:
                        continue
                    for h in range(16):
                        c, pb = h // 2, (h % 2) * 64
                        sl_ = (h % 2) * 8 + h // 2
                        ps, bps = next_ps(C)
                        arv = AR[pb:pb + 64, c, :, :].rearrange("p a l -> p (a l)")
                        _mm(S, ps[:, 0:256], BK[pb:pb + 64, 0, c, :], arv, True, True, [bARBK], [bps])
                        _mm(S, ps[:, 256:512], BK[pb:pb + 64, 1, c, :], arv, True, True, [bARBK], [bps])
                        mv = m4[:, d, :, :].rearrange("p a l -> p (a l)")
                        if h % 2 == 0:
                            S.op('dve', lambda e_, h=sl_, ps=ps, mv=mv: e_.tensor_tensor(AB4[:, h, :], ps[:, :], mv, ALU.mult),
                                 reads=[bps, bset], writes=[bAB4])
                        else:
                            S.op('act', lambda e_, h=sl_, ps=ps: e_.copy(AB4[:, h, :], ps[:, :]), reads=[bps], writes=[bAB4])
                            S.op('dve', lambda e_, h=sl_, mv=mv: e_.tensor_tensor(AB4[:, h, :], AB4[:, h, :], mv, ALU.mult),
                                 reads=[bAB4, bset], writes=[bAB4])
                    if stage < 5.5:
                        continue
                    P0, bP0 = Pp[0], bPp[0]
                    for hg in range(4):
                        ps, bps = next_ps(C)
                        for j in range(4):
                            slot = hg * 4 + j
                            c, pb = slot % 8, (slot // 8) * 64
                            _mm(S, ps[:, j * 128:(j + 1) * 128], AR[pb:pb + 64, c, 0, :], BK[pb:pb + 64, 0, c, :], True, True, [bARBK], [bps])
                        S.op('dve', lambda e_, hg=hg, ps=ps, d=d: e_.tensor_tensor(
                            P0[:, 1, hg * 4:(hg + 1) * 4, :], ps[:, :].rearrange("p (j l) -> p j l", j=4),
                            tri2[:, 1 - d:2 - d, :].to_broadcast([128, 4, 128]), ALU.mult),
                            reads=[bps, bset], writes=[bP0[hg]])
                        if stage < 5.6:
                            continue
                        S.op('dve', lambda e_, hg=hg: e_.tensor_copy(P0[:, 0, hg * 4:(hg + 1) * 4, :], AB4[:, hg * 4:(hg + 1) * 4, 0:128]),
                             reads=[bAB4], writes=[bP0[hg]])
                        if stage < 5.7:
                            continue
                        S.op('dve', lambda e_, hg=hg: e_.tensor_tensor(Xm[:, hg * 4:(hg + 1) * 4, :], AB4[:, hg * 4:(hg + 1) * 4, 0:128],
                                                                      C.identb[:, :].unsqueeze(1).to_broadcast([128, 4, 128]), ALU.add),
                             reads=[bAB4, C.bconst], writes=[bXm[hg]])
                    if stage < 7:
                        continue
                    dbg_now = C.cfg.get('dbg') and d == 0 and t == 0 and s4 == 0
                    def dump2(idx, tile_ap, n, rd):
                        S.op('pool', lambda e_: e_.dma_start(out=C.DBG[idx, :, 0:n], in_=tile_ap), reads=rd, writes=[Buf('dbg2')], dma=Buf('x'))
                    if dbg_now:
                        dump2(13, Pp[0][:].rearrange("p a b c -> p (a b c)"), 4096, bPp[0] + bXm)
                        dump2(14, Xm[:].rearrange("p a b -> p (a b)"), 2048, bXm)
                    for k in range(1, 7):
                        if dbg_now and k == 2:
                            dump2(15, Pp[1][:].rearrange("p a b c -> p (a b c)"), 4096, bPp[1] + bXm)
                            dump2(16, Xm[:].rearrange("p a b -> p (a b)"), 2048, bXm)
                        Pa, bPa = Pp[(k - 1) % 2], bPp[(k - 1) % 2]
                        Pn, bPn = Pp[k % 2], bPp[k % 2]
                        for hg in range(4):
                            if k < 6:
                                ps, bps = next_ps(C)
                                for j in range(4):
                                    h = hg * 4 + j
                                    _mm(S, ps[:, j * 128:(j + 1) * 128], Pa[:, 1, h, :], Pa[:, 0, h, :], True, True, [bPa[hg]], [bps])
                                S.op('act', lambda e_, hg=hg, ps=ps, Pn=Pn: e_.copy(
                                    Pn[:, 0, hg * 4:(hg + 1) * 4, :].rearrange("p j l -> p (j l)"), ps[:, :]), reads=[bps], writes=[bPn[hg]])
                            ps, bps = next_ps(C)
                            for j in range(4):
                                h = hg * 4 + j
                                _mm(S, ps[:, j * 128:(j + 1) * 128], Pa[:, 0, h, :], Pa[:, 1, h, :], True, True, [bPa[hg]], [bps])
                            if hg % 2 == 1:
                                S.op('act', lambda e_, hg=hg, ps=ps, Pn=Pn: e_.copy(
                                    Pn[:, 1, hg * 4:(hg + 1) * 4, :].rearrange("p j l -> p (j l)"), ps[:, :]), reads=[bps], writes=[bPn[hg]])
                            else:
                                S.op('dve', lambda e_, hg=hg, ps=ps, Pn=Pn: e_.tensor_copy(
                                    Pn[:, 1, hg * 4:(hg + 1) * 4, :].rearrange("p j l -> p (j l)"), ps[:, :]), reads=[bps], writes=[bPn[hg]])
                        for hg in range(4):
                            ps, bps = next_ps(C)
                            for j in range(4):
                                h = hg * 4 + j
                                _mm(S, ps[:, j * 128:(j + 1) * 128], Pn[:, 1, h, :], Xm[:, h, :], True, True, [bPn[hg], bXm[hg]], [bps])
                            S.op('dve', lambda e_, hg=hg, ps=ps: e_.tensor_tensor(
                                Xm[:, hg * 4:(hg + 1) * 4, :].rearrange("p j l -> p (j l)"),
                                Xm[:, hg * 4:(hg + 1) * 4, :].rearrange("p j l -> p (j l)"), ps[:, :], ALU.add),
                                reads=[bps, bXm[hg]], writes=[bXm[hg]])
                    if stage < 8:
                        continue
                    for half in range(2):
                        ps, bps = next_ps(C)
                        for j in range(8):
                            slot, h, c, pb = half * 8 + j, 2 * j + half, j, half * 64
                            _mm(S, ps[:, j * 64:(j + 1) * 64], AR[pb:pb + 64, c, 0, :], Sbb[pb:pb + 64, c, :], True, False, [bARBK, bS], [bps])
                            _mm(S, ps[:, j * 64:(j + 1) * 64], AB4[:, slot, 256:384], Vt[:, h * 64:(h + 1) * 64], False, True, [bAB4, bVt], [bps])
                        evac(C, half, Gs[:, half * 512:(half + 1) * 512], ps[:, :], bps, bGs)
                    for half in range(2):
                        ps, bps = next_ps(C)
                        for j in range(8):
                            slot = half * 8 + j
                            _mm(S, ps[:, j * 64:(j + 1) * 64], Xm[:, slot, :], Gs[:, slot * 64:(slot + 1) * 64], True, True, [bXm[slot // 4], bGs], [bps])
                        evac(C, half, Us[:, half * 512:(half + 1) * 512], ps[:, :], bps, bUs)
                    yav = ya[:].rearrange("p (c a i) -> p c a i", c=8, a=2)
                    yfv = yfl[:].rearrange("p (c a i) -> p c a i", c=8, a=2)
                    for half in range(2):
                        ps, bps = next_ps(C)
                        for j in range(8):
                            slot, h, c, pb = half * 8 + j, 2 * j + half, j, half * 64
                            o = ps[:, j * 64:(j + 1) * 64]
                            _mm(S, o, AR[pb:pb + 64, c, 1, :], Sbb[pb:pb + 64, c, :], True, False, [bARBK, bS], [bps])
                            _mm(S, o, AB4[:, slot, 128:256], Us[:, slot * 64:(slot + 1) * 64], False, False, [bAB4, bUs], [bps])
                            _mm(S, o, AB4[:, slot, 384:512], Vt[:, h * 64:(h + 1) * 64], False, True, [bAB4, bVt], [bps])
                        psv = ps[:, :].rearrange("p (c i) -> p c i", c=8)
                        if d == 0:
                            evac(C, half, yav[:, :, half, :], psv, bps, bya)
                        else:
                            S.op('dve', lambda e_, half=half, psv=psv: e_.tensor_tensor(yav[:, :, half, :], psv, yfv[:, :, half, :], ALU.add),
                                 reads=[bps, byfl], writes=[bya])
                    if d == 0:
                        S.op('pool', lambda e_, c0=c0: e_.dma_start(out=C.YS[c0:c0 + 128, :], in_=ya[:]), reads=[bya], writes=[C.bYS], dma=bya)
                    ps, bps = next_ps(C)
                    for c in range(8):
                        o = ps[:, c * 64:(c + 1) * 64]
                        for par in range(2):
                            h = 2 * c + par
                            slot = par * 8 + c
                            _mm(S, o, bkTp[:, par, 0, c, :], Us[:, slot * 64:(slot + 1) * 64], par == 0, False, [bbkT, bUs], [bps])
                            _mm(S, o, bkTp[:, par, 1, c, :], Vt[:, h * 64:(h + 1) * 64], False, par == 1, [bbkT, bVt], [bps])
                    S.op('dve', lambda e_, ps=ps: e_.tensor_tensor(Sst[:].rearrange("p c i -> p (c i)"), ps[:, :],
                                                                Sst[:].rearrange("p c i -> p (c i)"), ALU.add),
                         reads=[bps, bS], writes=[bS])
                    S.op('dve', lambda e_, last=last: e_.tensor_tensor(Sst[:], Sst[:], eLi[:, :, last:last + 1].to_broadcast([128, 8, 64]), ALU.mult),
                         reads=[bS, beL], writes=[bS])
                    S.op('act', lambda e_: e_.copy(Sbb[:], Sst[:]), reads=[bS], writes=[bS])
                    if stage < 9:
                        continue
                    if C.cfg.get('dbg') and d == 0 and t == 0 and s4 == 0:
                        dbgb = Buf('dbg')
                        def dump(idx, tile_ap, n):
                            S.op('pool', lambda e_: e_.dma_start(out=C.DBG[idx, :, 0:n], in_=tile_ap), reads=[bAB4, bXm[0], bXm[1], bXm[2], bXm[3], bGs, bUs, bya, bARBK, beL, bsig, bVt, bS, bbkT, bprep],
                                 writes=[dbgb], dma=Buf('x'))
                        dump(0, AB4[:].rearrange("p a b -> p (a b)"), 8192)
                        dump(1, Xm[:].rearrange("p a b -> p (a b)"), 2048)
                        dump(2, Gs[:], 1024)
                        dump(3, Us[:], 1024)
                        dump(4, ya[:], 1024)
                        dump(5, AR[:].rearrange("p a b c -> p (a b c)"), 2048)
                        dump(6, BK[:].rearrange("p a b c -> p (a b c)"), 2048)
                        dump(7, eLi[:].rearrange("p a b -> p (a b)"), 1024)
                        dump(8, eLn[:].rearrange("p a b -> p (a b)"), 1024)
                        dump(9, eLx[:].rearrange("p a b -> p (a b)"), 1024)
                        dump(10, sig[:], 1024)
                        dump(11, Vt[:], 1024)
                        dump(12, Sst[:].rearrange("p a b -> p (a b)"), 512)
                        dump(13, nkT[:, :, 0:128].rearrange("p a b -> p (a b)"), 1024) if False else None
                    if d == 1:
                        yv = ya[:].rearrange("p (h i) -> p h i", h=16)
                        S.op('dve', lambda e_, yv=yv: e_.reduce_sum(stt[:, 0, :], yv, AX.X), reads=[bya], writes=[bstt])
                        S.op('dve', lambda e_: e_.tensor_scalar(stt[:, 0, :], stt[:, 0, :], 1.0 / 64, None, ALU.mult), reads=[bstt], writes=[bstt])
                        S.op('dve', lambda e_, yv=yv: e_.tensor_tensor(yv, yv, stt[:, 0, :].unsqueeze(2).to_broadcast([128, 16, 64]), ALU.subtract),
                             reads=[bya, bstt], writes=[bya])
                        S.op('act', lambda e_: e_.activation(sig[:], ya[:], AF.Square), reads=[bya], writes=[bsig])
                        S.op('dve', lambda e_: e_.reduce_sum(stt[:, 1, :], sig[:].rearrange("p (h i) -> p h i", h=16), AX.X),
                             reads=[bsig], writes=[bstt])
                        S.op('act', lambda e_: e_.activation(stt[:, 1, :], stt[:, 1, :], AF.Sqrt, bias=C.epsD[:, 3:4], scale=1.0 / 64),
                             reads=[bstt, C.bconst], writes=[bstt])
                        S.op('dve', lambda e_: e_.reciprocal(stt[:, 1, :], stt[:, 1, :]), reads=[bstt], writes=[bstt])
                        S.op('dve', lambda e_, yv=yv: e_.tensor_tensor(ynb[:].rearrange("p (h i) -> p h i", h=16), yv,
                                                                     stt[:, 1, :].unsqueeze(2).to_broadcast([128, 16, 64]), ALU.mult),
                             reads=[bya, bstt], writes=[bynb])
                        for half in range(2):
                            ps, bps = next_ps(C)
                            for j in range(4):
                                c = half * 4 + j
                                _mm(S, ps[:, j * 128:(j + 1) * 128], ynb[:, c * 128:(c + 1) * 128], C.identb[:], True, True,
                                    [bynb, C.bconst], [bps])
                            for j in range(4):
                                c = half * 4 + j
                                S.op('act', lambda e_, c=c, j=j, ps=ps: e_.activation(fin[:, c, :], ps[:, j * 128:(j + 1) * 128], AF.Identity,
                                                                                   bias=cols[:, 6, c:c + 1], scale=cols[:, 5, c:c + 1]),
                                     reads=[bps, bset], writes=[bfin])
                        S.op('dve', lambda e_, ts=ts: e_.tensor_tensor(fin[:], fin[:], bvT[:, :, ts], ALU.add), reads=[bfin, bprep], writes=[bfin])
                        S.op('dve', lambda e_, ts=ts: e_.tensor_tensor(yo[:, :, ts], fin[:], gT[:, :, ts], ALU.mult), reads=[bfin, bprep], writes=[byo])
                if d == 1:
                    S.op('pool', lambda e_, t0=t0: e_.dma_start(out=Yv[:, 8:16, t0:t0 + TT], in_=yo[:]), reads=[byo], writes=[C.bY], dma=byo)
        S.end_phase()


def bkind(C, pos):
    if pos <= 0 or pos >= C.N:
        return 'hard'
    return C.cfg.get('bounds', {}).get(pos)


def t5_tables():
    oh = np.zeros((33, 3, 255), np.float32)
    for dl in range(3):
        for i in range(255):
            rel = i - 127 + 128 * (dl - 1)
            if abs(rel) > 128:
                oh[32, dl, i] = -30000.0
                continue
            n = abs(rel)
            if n < 8:
                b = n
            else:
                v = np.float32(np.log(np.float32(n) / np.float32(8.0))) / np.float32(math.log(16.0)) * np.float32(8.0)
                b = min(8 + int(np.float32(v)), 15)
            if rel > 0:
                b += 16
            oh[b, dl, i] = 1.0
    return oh.reshape(33, 765)


def phase_mix_odd(C, o):
    nc, S, N, W = C.nc, C.S, C.N, C.W
    TT = 512
    with ExitStack() as st:
        sb = lambda name, shape, dt=F32: st.enter_context(_sbt(nc, name, list(shape), dt))
        bset = Buf('mo_setup')
        rba = sb('mo_rba', [33, 16])
        S.op('dve', lambda e: e.memset(rba[:], 1.0), writes=[bset])
        S.op('sp', lambda e: e.dma_start(out=rba[0:32, :], in_=W['rel_bias']), writes=[bset], dma=Buf('mo_sd2'))
        oh = sb('mo_oh', [33, 765])
        S.op('sp', lambda e: e.dma_start(out=oh[:], in_=C.c_oh), writes=[bset], dma=Buf('mo_sd3'))
        d2 = sb('mo_d2', [16, 765])
        for (a, b) in ((0, 510), (510, 765)):
            ps, bps = next_ps(C)
            _mm(S, ps[0:16, 0:b - a], rba[:, :], oh[:, a:b], True, True, [bset], [bps])
            S.op('dve', lambda e, a=a, b=b, ps=ps: e.tensor_copy(d2[:, a:b], ps[0:16, 0:b - a]), reads=[bps], writes=[bset])
        bD2 = Buf('D2')
        S.op('sp', lambda e: e.dma_start(out=C.D2.ap(), in_=d2[:]), reads=[bset], writes=[bD2], dma=Buf('mo_sd4'))
        hkf = sb('mo_hkf', [128, 16, 3, 128])
        for h in range(16):
            src = bass.AP(C.D2, h * 765, [[1, 128], [255, 3], [1, 128]])
            S.op('sp', lambda e, h=h, src=src: e.dma_start(out=hkf[:, h, :, :], in_=src), reads=[bD2], writes=[bset], dma=Buf('mo_sd5'))
        hk = sb('mo_hk', [128, 16, 3, 128], BF16)
        S.op('dve', lambda e: e.tensor_copy(hk[:], hkf[:]), reads=[bset], writes=[bset])
        jf = sb('mo_jf', [128, 128])
        S.op('sp', lambda e: e.dma_start(out=jf[:], in_=C.c_anti), writes=[bset], dma=Buf('mo_sd6'))
        jb = sb('mo_jb', [128, 128], BF16)
        S.op('dve', lambda e: e.tensor_copy(jb[:], jf[:]), reads=[bset], writes=[bset])
        bd = sb('mo_bd', [128, 128])
        S.op('sp', lambda e: e.dma_start(out=bd[:], in_=C.c_bd), writes=[bset], dma=Buf('mo_sd7'))
        opad = sb('mo_opad', [128, 2, 128], BF16)
        S.op('dve', lambda e: e.memset(opad[:], 0.0), writes=[bset])
        S.op('dve', lambda e: e.memset(opad[:, 0, 0:64], 1.0), writes=[bset])
        S.op('dve', lambda e: e.memset(opad[:, 1, 64:128], 1.0), writes=[bset])
        opadF = sb('mo_opadF', [128, 2, 128], BF16)
        S.op('dve', lambda e: e.tensor_scalar(opadF[:], opad[:], C.flagt[:, 0:1], None, ALU.mult),
             reads=[bset, C.bconst], writes=[bset])
        esk = sb('mo_esk', [128, 8])
        for hh in range(2):
            src = bass.AP(W['att_sink'].tensor, W['att_sink'][o].offset + hh, [[0, 64], [2, 8]])
            S.op('sp', lambda e, hh=hh, src=src: e.dma_start(out=esk[hh * 64:(hh + 1) * 64, :], in_=src, allow_slow_non_contiguous=True),
                 writes=[bset], dma=Buf('mo_sd8'))
        S.op('act', lambda e: e.activation(esk[:], esk[:], AF.Exp), reads=[bset], writes=[bset])
        wq = sb('mo_wq', [128, 2])
        for hh in range(2):
            S.op('sp', lambda e, hh=hh: e.dma_start(out=wq[hh * 64:(hh + 1) * 64, 0:1],
                                                    in_=W['att_q_norm_w'][o].rearrange("(p one) -> p one", one=1)),
                 writes=[bset], dma=Buf('mo_sd9'))
            S.op('sp', lambda e, hh=hh: e.dma_start(out=wq[hh * 64:(hh + 1) * 64, 1:2],
                                                    in_=W['att_k_norm_w'][o].rearrange("(p one) -> p one", one=1)),
                 writes=[bset], dma=Buf('mo_sd10'))
        S.op('dve', lambda e: e.tensor_scalar(wq[:, 0:1], wq[:, 0:1], 0.125, None, ALU.mult), reads=[bset], writes=[bset])

        st1 = ExitStack()
        sb = lambda name, shape, dt=F32, _s=st1: _s.enter_context(_sbt(nc, name, list(shape), dt))
        wnat = sb('mo_wnat', [31, D])
        S.op('sp', lambda e: e.dma_start(out=wnat[:], in_=W['conv_dw_w'][o]), writes=[bset], dma=Buf('mo_sd1'))
        wcol = sb('mo_wcol', [128, 8, 31])
        for c in range(8):
            ps, bps = next_ps(C)
            _mm(S, ps[:, 0:31], wnat[0:31, c * 128:(c + 1) * 128], C.ident[0:31, 0:31], True, True, [bset, C.bconst], [bps])
            S.op('dve', lambda e, c=c, ps=ps: e.tensor_copy(wcol[:, c, :], ps[:, 0:31]), reads=[bps], writes=[bset])
        dg = sb('mo_dg', [128, 8, 31, 128], BF16)
        for c in range(8):
            for k in range(31):
                S.op('dve', lambda e, c=c, k=k: e.tensor_scalar(dg[:, c, k, :], C.identb[:], wcol[:, c, k:k + 1], None, ALU.mult),
                     reads=[bset, C.bconst], writes=[bset])
        cvb = sb('mo_cvb', [128, 8])
        lnw = sb('mo_lnw', [128, 8])
        lnb = sb('mo_lnb', [128, 8])
        load_cols(C, cvb[:], bset, W['conv_dw_b'][o], 8)
        load_cols(C, lnw[:], bset, W['conv_ln_w'][o], 8)
        load_cols(C, lnb[:], bset, W['conv_ln_b'][o], 8)
        onesf = sb('mo_onesf', [128, 128])
        S.op('dve', lambda e: e.memset(onesf[:], 1.0), writes=[bset])
        NB = 2
        vg = [sb('mo_vg%d' % i, [128, 2, 544]) for i in range(NB)]
        bvg = [Buf('mo_vg%d' % i) for i in range(NB)]
        sgm = sb('mo_sgm', [128, 544])
        bsgm = Buf('mo_sgm')
        ub = [sb('mo_u%d' % i, [128, 544], BF16) for i in range(NB)]
        bub = [Buf('mo_u%d' % i) for i in range(NB)]
        cv = sb('mo_cv', [128, 8, TT])
        bcv = Buf('mo_cv')
        sqf = sb('mo_sqf', [128, 8, TT])
        bsqf = Buf('mo_sqf')
        mean = sb('mo_mean', [128, TT])
        rstd = sb('mo_rstd', [128, TT])
        bst = Buf('mo_stat')
        t1 = [sb('mo_t1%d' % i, [128, TT]) for i in range(2)]
        bt1 = [Buf('mo_t1%d' % i) for i in range(2)]
        yc = [sb('mo_yc%d' % i, [128, 8, TT], BF16) for i in range(2)]
        byc = [Buf('mo_yc%d' % i) for i in range(2)]
        Yv = C.Y.rearrange("(c p) n -> p c n", p=128)
        it = 0
        for t in range(N // TT):
            t0 = t * TT
            kl, kr = bkind(C, t0), bkind(C, t0 + TT)
            lo = 15 if kl == 'hard' else 0
            hi = 527 if kr == 'hard' else 542
            for c in range(8):
                v, bv = vg[it % NB], bvg[it % NB]
                u, bu = ub[it % NB], bub[it % NB]
                it += 1
                for j in range(2):
                    src = C.UF[j * 1024 + c * 128: j * 1024 + (c + 1) * 128, t0 - 15 + lo: t0 - 15 + hi]
                    S.op('sp', lambda e, v=v, j=j, src=src, lo=lo, hi=hi: e.dma_start(out=v[:, j, lo:hi], in_=src),
                         reads=[C.bUF], writes=[bv], dma=bv)
                S.op('act', lambda e, v=v, lo=lo, hi=hi: e.activation(sgm[:, lo:hi], v[:, 1, lo:hi], AF.Sigmoid),
                     reads=[bv], writes=[bsgm])
                if lo > 0:
                    S.op('pool', lambda e, u=u: e.memset(u[:, 0:15], 0.0), writes=[bu])
                if hi < 542:
                    S.op('pool', lambda e, u=u: e.memset(u[:, 527:542], 0.0), writes=[bu])
                S.op('dve', lambda e, v=v, u=u, lo=lo, hi=hi: e.tensor_tensor(u[:, lo:hi], v[:, 0, lo:hi], sgm[:, lo:hi], ALU.mult),
                     reads=[bv, bsgm], writes=[bu])
                if kl == 'soft':
                    S.op('dve', lambda e, u=u: e.tensor_scalar(u[:, 0:15], u[:, 0:15], C.flagt[:, 0:1], None, ALU.mult),
                         reads=[bu, C.bconst], writes=[bu])
                if kr == 'soft':
                    S.op('dve', lambda e, u=u: e.tensor_scalar(u[:, 527:542], u[:, 527:542], C.flagt[:, 0:1], None, ALU.mult),
                         reads=[bu, C.bconst], writes=[bu])
                ps, bps = next_ps(C)
                for k in range(31):
                    _mm(S, ps[:, :], dg[:, c, k, :], u[:, k:k + TT], k == 0, k == 30, [bset, bu], [bps])
                S.op('act', lambda e, c=c, ps=ps: e.activation(cv[:, c, :], ps[:, :], AF.Identity, bias=cvb[:, c:c + 1], scale=1.0),
                     reads=[bps, bset], writes=[bcv])
            S.op('act', lambda e: e.activation(sqf[:], cv[:], AF.Square), reads=[bcv], writes=[bsqf])
            ps1, bps1 = next_ps(C)
            for c in range(8):
                _mm(S, ps1[:, :], onesf[:], cv[:, c, :], c == 0, c == 7, [bset, bcv], [bps1])
            ps2, bps2 = next_ps(C)
            for c in range(8):
                _mm(S, ps2[:, :], onesf[:], sqf[:, c, :], c == 0, c == 7, [bset, bsqf], [bps2])
            S.op('dve', lambda e, ps1=ps1: e.tensor_scalar(mean[:], ps1[:, :], 1.0 / D, None, ALU.mult), reads=[bps1], writes=[bst])
            S.op('dve', lambda e: e.tensor_tensor(rstd[:], mean[:], mean[:], ALU.mult), reads=[bst], writes=[bst])
            S.op('dve', lambda e, ps2=ps2: e.scalar_tensor_tensor(rstd[:], ps2[:, :], 1.0 / D, rstd[:], ALU.mult, ALU.subtract),
                 reads=[bps2, bst], writes=[bst])
            S.op('act', lambda e: e.activation(rstd[:], rstd[:], AF.Sqrt, bias=C.epsD[:, 2:3], scale=1.0),
                 reads=[bst, C.bconst], writes=[bst])
            S.op('dve', lambda e: e.reciprocal(rstd[:], rstd[:]), reads=[bst], writes=[bst])
            y, by = yc[t % 2], byc[t % 2]
            for c in range(8):
                tt, btt = t1[c % 2], bt1[c % 2]
                S.op('dve', lambda e, c=c, tt=tt: e.tensor_tensor(tt[:], cv[:, c, :], mean[:], ALU.subtract),
                     reads=[bcv, bst], writes=[btt])
                S.op('dve', lambda e, tt=tt: e.tensor_tensor(tt[:], tt[:], rstd[:], ALU.mult), reads=[btt, bst], writes=[btt])
                S.op('act', lambda e, c=c, tt=tt, y=y: e.activation(y[:, c, :], tt[:], AF.Silu, bias=lnb[:, c:c + 1],
                                                                     scale=lnw[:, c:c + 1]),
                     reads=[btt, bset], writes=[by])
            S.op('pool', lambda e, y=y, t0=t0: e.dma_start(out=Yv[:, 0:8, t0:t0 + TT], in_=y[:]),
                 reads=[by], writes=[C.bY], dma=by)

        S.barrier()
        S.flush()
        st1.close()
        st2 = ExitStack()
        sb = lambda name, shape, dt=F32, _s=st2: _s.enter_context(_sbt(nc, name, list(shape), dt))
        qf = [sb('mo_qf%d' % i, [128, 8, TT]) for i in range(2)]
        bqf = [Buf('mo_qf%d' % i) for i in range(2)]
        kf = [sb('mo_kf%d' % i, [128, 4, 768]) for i in range(2)]
        bkf = [Buf('mo_kf%d' % i) for i in range(2)]
        vf = [sb('mo_vf%d' % i, [128, 6, 256]) for i in range(2)]
        bvf = [Buf('mo_vf%d' % i) for i in range(2)]
        sq2 = sb('mo_sq2', [128, 768])
        bsq2 = Buf('mo_sq2')
        rs2 = sb('mo_rs2', [128, 768])
        brs2 = Buf('mo_rs2')
        qn = [sb('mo_qn%d' % i, [128, 8, TT], BF16) for i in range(2)]
        bqn = [Buf('mo_qn%d' % i) for i in range(2)]
        kn = [sb('mo_kn%d' % i, [128, 4, 768], BF16) for i in range(2)]
        bkn = [Buf('mo_kn%d' % i) for i in range(2)]
        vp = [sb('mo_vp%d' % i, [128, 2, 6, 4, 128], BF16) for i in range(2)]
        bvp = [Buf('mo_vp%d' % i) for i in range(2)]
        for i in range(2):
            S.op('pool', lambda e, i=i: e.memset(vp[i][:], 0.0), writes=[bvp[i]])
        vpF = sb('mo_vpF', [128, 2, 2, 4, 128], BF16)
        bvpF = Buf('mo_vpF')
        pt = [sb('mo_pt%d' % i, [128, 384], BF16) for i in range(3)]
        bpt = [Buf('mo_pt%d' % i) for i in range(3)]
        dn = [sb('mo_dn%d' % i, [128, 128]) for i in range(2)]
        bdn = [Buf('mo_dn%d' % i) for i in range(2)]
        yd = [sb('mo_yd%d' % i, [128, 8, TT], BF16) for i in range(2)]
        byd = [Buf('mo_yd%d' % i) for i in range(2)]
        pi = 0
        di = 0
        for t in range(N // TT):
            t0 = t * TT
            kl, kr = bkind(C, t0), bkind(C, t0 + TT)
            kb_lo = 1 if kl == 'hard' else 0
            kb_hi = 5 if kr == 'hard' else 6
            q_, bq_, k_, bk_, v_, bv_ = qf[t % 2], bqf[t % 2], kf[t % 2], bkf[t % 2], vf[t % 2], bvf[t % 2]
            qn_, bqn_, kn_, bkn_, vp_, bvp_ = qn[t % 2], bqn[t % 2], kn[t % 2], bkn[t % 2], vp[t % 2], bvp[t % 2]
            UFv = C.UF.rearrange("(c p) n -> p c n", p=128)
            for half in range(2):
                S.op('sp', lambda e, q_=q_, half=half, t0=t0: e.dma_start(
                    out=q_[:, half * 4:half * 4 + 4, :], in_=UFv[:, 16 + half * 4:16 + half * 4 + 4, t0:t0 + TT]),
                    reads=[C.bUF], writes=[bq_], dma=bq_)
            c0, c1 = kb_lo * 128, kb_hi * 128
            S.op('sp', lambda e, k_=k_, c0=c0, c1=c1, t0=t0: e.dma_start(
                out=k_[:, :, c0:c1], in_=UFv[:, 24:28, t0 - 128 + c0:t0 - 128 + c1]),
                reads=[C.bUF], writes=[bk_], dma=bk_)
            S.op('sp', lambda e, v_=v_, t0=t0, kb_lo=kb_lo, kb_hi=kb_hi: e.dma_start(
                out=v_[:, kb_lo:kb_hi, :],
                in_=C.UT[t0 - 128 + kb_lo * 128:t0 - 128 + kb_hi * 128, 0:256].rearrange("(b p) c -> p b c", p=128)),
                reads=[C.bUT], writes=[bv_], dma=bv_)
            for c in range(8):
                S.op('act', lambda e, c=c, q_=q_: e.activation(sq2[:, 0:TT], q_[:, c, :], AF.Square), reads=[bq_], writes=[bsq2])
                ps, bps = next_ps(C)
                _mm(S, ps[:, :], bd[:], sq2[:, 0:TT], True, True, [bset, bsq2], [bps])
                S.op('act', lambda e, ps=ps: e.activation(rs2[:, 0:TT], ps[:, :], AF.Sqrt, bias=C.epsD[:, 1:2], scale=1.0 / 64),
                     reads=[bps, C.bconst], writes=[brs2])
                S.op('dve', lambda e: e.reciprocal(rs2[:, 0:TT], rs2[:, 0:TT]), reads=[brs2], writes=[brs2])
                S.op('dve', lambda e, c=c, q_=q_, qn_=qn_: e.scalar_tensor_tensor(qn_[:, c, :], q_[:, c, :], wq[:, 0:1], rs2[:, 0:TT],
                                                                                 ALU.mult, ALU.mult),
                     reads=[bq_, brs2, bset], writes=[bqn_])
            for c in range(4):
                S.op('act', lambda e, c=c, k_=k_, c0=c0, c1=c1: e.activation(sq2[:, c0:c1], k_[:, c, c0:c1], AF.Square),
                     reads=[bk_], writes=[bsq2])
                for (a, b) in ((c0, min(c1, c0 + 512)), (c0 + 512, c1)):
                    if b <= a:
                        continue
                    ps, bps = next_ps(C)
                    _mm(S, ps[:, 0:b - a], bd[:], sq2[:, a:b], True, True, [bset, bsq2], [bps])
                    S.op('act', lambda e, ps=ps, a=a, b=b: e.activation(rs2[:, a:b], ps[:, 0:b - a], AF.Sqrt, bias=C.epsD[:, 1:2],
                                                                        scale=1.0 / 64),
                         reads=[bps, C.bconst], writes=[brs2])
                S.op('dve', lambda e, c0=c0, c1=c1: e.reciprocal(rs2[:, c0:c1], rs2[:, c0:c1]), reads=[brs2], writes=[brs2])
                S.op('dve', lambda e, c=c, k_=k_, kn_=kn_, c0=c0, c1=c1: e.scalar_tensor_tensor(
                    kn_[:, c, c0:c1], k_[:, c, c0:c1], wq[:, 1:2], rs2[:, c0:c1], ALU.mult, ALU.mult),
                    reads=[bk_, brs2, bset], writes=[bkn_])
            vsrc = v_[:, kb_lo:kb_hi, :].rearrange("p b (h d) -> p b h d", h=4)
            S.op('dve', lambda e, vp_=vp_, vsrc=vsrc, kb_lo=kb_lo, kb_hi=kb_hi: e.tensor_copy(vp_[:, 0, kb_lo:kb_hi, :, 0:64], vsrc),
                 reads=[bv_], writes=[bvp_])
            S.op('pool', lambda e, vp_=vp_, vsrc=vsrc, kb_lo=kb_lo, kb_hi=kb_hi: e.tensor_copy(vp_[:, 1, kb_lo:kb_hi, :, 64:128], vsrc),
                 reads=[bv_], writes=[bvp_])
            if kl == 'soft':
                S.op('dve', lambda e, vp_=vp_: e.tensor_scalar(vpF[:, :, 0, :, :], vp_[:, :, 0, :, :], C.flagt[:, 0:1], None, ALU.mult),
                     reads=[bvp_, C.bconst], writes=[bvpF])
            if kr == 'soft':
                S.op('dve', lambda e, vp_=vp_: e.tensor_scalar(vpF[:, :, 1, :, :], vp_[:, :, 5, :, :], C.flagt[:, 0:1], None, ALU.mult),
                     reads=[bvp_, C.bconst], writes=[bvpF])
            y, by = yd[t % 2], byd[t % 2]
            jobs = [(qb, pair, par) for qb in range(4) for pair in range(8) for par in range(2)]

            def part_a(job):
                qb, pair, par = job
                hq = pair * 2 + par
                hkv = hq // 4
                pb = par * 64
                dls = [dl for dl in range(3) if kb_lo <= qb + dl < kb_hi]
                ps, bps = next_ps(C)
                for dl in dls:
                    kb = qb + dl
                    _mm(S, ps[:, dl * 128:(dl + 1) * 128], kn_[pb:pb + 64, hkv, kb * 128:(kb + 1) * 128],
                        qn_[pb:pb + 64, pair, qb * 128:(qb + 1) * 128], True, False, [bkn_, bqn_], [bps])
                    _mm(S, ps[:, dl * 128:(dl + 1) * 128], hk[:, hq, dl, :], jb[:], False, True, [bset], [bps])
                p_, bp_ = pt[cntp[0] % 3], bpt[cntp[0] % 3]
                cntp[0] += 1
                a_, b_ = dls[0] * 128, (dls[-1] + 1) * 128
                S.op('act', lambda e: e.activation(p_[:, a_:b_], ps[:, a_:b_], AF.Exp), reads=[bps], writes=[bp_])
                return dls, p_, bp_

            cntp = [pi]
            nxt_a = part_a(jobs[0])
            cur_pair = None
            for ji, job in enumerate(jobs):
                qb, pair, par = job
                dls, p_, bp_ = nxt_a
                if ji + 1 < len(jobs):
                    nxt_a = part_a(jobs[ji + 1])
                hq = pair * 2 + par
                hkv = hq // 4
                if par == 0:
                    po, bpo = next_ps(C)
                    pd, bpd = next_ps(C)
                    im = 0
                nmm = 2 * len(dls)
                for dl in dls:
                    kb = qb + dl
                    soft = (kb == 0 and kl == 'soft') or (kb == 5 and kr == 'soft')
                    if soft:
                        vl = vpF[:, par, 0 if kb == 0 else 1, hkv, :]
                        ol = opadF[:, par, :]
                        rd = [bvpF, bset]
                    else:
                        vl = vp_[:, par, kb, hkv, :]
                        ol = opad[:, par, :]
                        rd = [bvp_, bset]
                    _mm(S, po[:, 0:128], vl, p_[:, dl * 128:(dl + 1) * 128], im == 0, im == nmm - 1, rd + [bp_], [bpo])
                    _mm(S, pd[:, 0:128], ol, p_[:, dl * 128:(dl + 1) * 128], im == 0, im == nmm - 1, rd + [bp_], [bpd])
                    im += 1
                if par == 1:
                    d_, bd_ = dn[di % 2], bdn[di % 2]
                    di += 1
                    S.op('dve', lambda e, d_=d_, pd=pd, pair=pair: e.tensor_scalar(d_[:], pd[:, 0:128], esk[:, pair:pair + 1], None, ALU.add),
                         reads=[bpd, bset], writes=[bd_])
                    S.op('dve', lambda e, d_=d_: e.reciprocal(d_[:], d_[:]), reads=[bd_], writes=[bd_])
                    S.op('dve', lambda e, d_=d_, po=po, y=y, pair=pair, qb=qb: e.tensor_tensor(
                        y[:, pair, qb * 128:(qb + 1) * 128], po[:, 0:128], d_[:], ALU.mult),
                        reads=[bpo, bd_], writes=[by])
            pi = cntp[0]
            S.op('pool', lambda e, y=y, t0=t0: e.dma_start(out=Yv[:, 8:16, t0:t0 + TT], in_=y[:]),
                 reads=[by], writes=[C.bY], dma=by)
        S.end_phase()
        st2.close()


_WNAMES = ['rel_bias', 'norm_mix_w', 'norm_ffn_w', 'ffn_w_in', 'ffn_w_out', 'ev_w_in', 'ev_w_out', 'ssd_conv_w', 'ssd_conv_b',
           'ssd_dt_bias', 'ssd_a_log', 'ssd_d', 'ssd_norm_w', 'rwkv_mu', 'rwkv_w0', 'rwkv_w2', 'rwkv_a0', 'rwkv_a2', 'rwkv_g2',
           'rwkv_k_k', 'rwkv_k_a', 'rwkv_r_k', 'rwkv_ln_w', 'rwkv_ln_b', 'od_w_in', 'od_w_out', 'conv_dw_w', 'conv_dw_b',
           'conv_ln_w', 'conv_ln_b', 'att_q_norm_w', 'att_k_norm_w', 'att_sink']
_PROG = {}


def kernel(x_prompt, x_sample, **weights):
    SEG = 4096
    NCORE = 8
    x_prompt = np.asarray(x_prompt, dtype=np.float32)
    x_sample = np.asarray(x_sample, dtype=np.float32)
    assert x_prompt.shape == (16, SEG, D) and x_sample.shape == (4, 2 * SEG, D)
    if 'full' not in _PROG:
        cfg = dict(N=3 * SEG, bounds={SEG: 'soft', 2 * SEG: 'hard'}, layers=[0, 1, 2, 3], debug=False, mixers=True)
        _PROG['full'] = build_program(cfg)
    nc, C = _PROG['full']
    wmap = {k: np.ascontiguousarray(np.asarray(weights[k], dtype=np.float32)) for k in _WNAMES}
    in_maps = []
    layout = []
    for c in range(NCORE):
        if c < 4:
            xin = np.concatenate([x_sample[c], x_prompt[c]], axis=0)
            fl = 1.0
            layout.append([('s', c), ('p', c)])
        else:
            ids = [4 + 3 * (c - 4) + j for j in range(3)]
            xin = np.concatenate([x_prompt[i] for i in ids], axis=0)
            fl = 0.0
            layout.append([('p', i) for i in ids])
        m = {'xin': np.ascontiguousarray(xin), 'flag': np.full((128, 1), fl, np.float32)}
        m.update(wmap)
        m.update(C.const_inputs)
        in_maps.append(m)
    res = run_bass_kernel_spmd(nc, in_maps, core_ids=list(range(NCORE)))
    y_prompt = np.empty((16, SEG, D), np.float32)
    y_sample = np.empty((4, 2 * SEG, D), np.float32)
    for c in range(NCORE):
        y = np.asarray(res.results[c]['yout'])
        pos = 0
        for kind, i in layout[c]:
            if kind == 's':
                y_sample[i] = y[pos:pos + 2 * SEG]
                pos += 2 * SEG
            else:
                y_prompt[i] = y[pos:pos + SEG]
                pos += SEG
    return (y_prompt, y_sample)
```
